# Optimizing a Trainium2 kernel written in Bass

```python
import jax, jax.numpy as jnp
from jax import lax
import numpy as np

D_MODEL = 1024
BATCH = 8
SEQ = 2048
DEPTH = 4
DEC_BATCH = 128
DEC_SEQ = 4
PAST_LEN = 2048
PAGE_SIZE = 128

N_EVEN = (DEPTH + 1) // 2
N_ODD = DEPTH // 2
HEAD_DIM = 64
N_ATTN_HEADS = (D_MODEL // 2) // HEAD_DIM
ATTN_WIDTH = N_ATTN_HEADS * HEAD_DIM
POOL_WIDTH = D_MODEL // 2
POOL_WINDOWS = (2, 4, 8, 16)
N_POOL_GROUPS = len(POOL_WINDOWS)
POOL_GROUP = POOL_WIDTH // N_POOL_GROUPS
POOL_HIST = max(POOL_WINDOWS) - 1
MOBA_BLOCK = 256
MOBA_TOPK = 3
PROMPT_Q_BLOCK = 16
SAMPLE_Q_BLOCK = 1
ROPE_THETA = 10000.0
SGU_WIDTH = D_MODEL
SGU_CHUNK = 128
N_SGU_GROUPS = 8
SGU_GROUP = SGU_WIDTH // N_SGU_GROUPS
D_FF = 4 * D_MODEL
NORM_EPS = 1e-6
LN_EPS = 1e-5

kernel_name = 'moba_pool_gmlp_hybrid_step'


def rms_norm(x, g):
    xf = x.astype(jnp.float32)
    y = xf * lax.rsqrt(jnp.mean(xf * xf, axis=-1, keepdims=True) + NORM_EPS)
    return (y * g.astype(jnp.float32)).astype(x.dtype)


def layer_norm(x, g, b):
    xf = x.astype(jnp.float32)
    mu = jnp.mean(xf, axis=-1, keepdims=True)
    var = jnp.mean(jnp.square(xf - mu), axis=-1, keepdims=True)
    y = (xf - mu) * lax.rsqrt(var + LN_EPS)
    return (y * g.astype(jnp.float32) + b.astype(jnp.float32)).astype(x.dtype)


def rotary(x, pos0):
    s, dh = x.shape[1], x.shape[-1]
    pos = jnp.arange(s, dtype=jnp.float32) + pos0
    inv = ROPE_THETA ** (-jnp.arange(0, dh, 2, dtype=jnp.float32) / dh)
    ang = pos[:, None] * inv[None, :]
    cos = jnp.cos(ang)[None, :, None, :]
    sin = jnp.sin(ang)[None, :, None, :]
    xf = x.astype(jnp.float32)
    x1, x2 = xf[..., : dh // 2], xf[..., dh // 2:]
    return jnp.concatenate([x1 * cos - x2 * sin, x2 * cos + x1 * sin], axis=-1).astype(x.dtype)


def moba_attention(q, k, v, pos0, q_block):
    b, sq, h, dh = q.shape
    t = k.shape[1]
    nb = -(-t // MOBA_BLOCK)
    pad = nb * MOBA_BLOCK - t
    k = jnp.pad(k, ((0, 0), (0, pad), (0, 0), (0, 0)))
    v = jnp.pad(v, ((0, 0), (0, pad), (0, 0), (0, 0)))
    kb = k.reshape(b, nb, MOBA_BLOCK, h, dh).transpose(0, 3, 1, 2, 4)
    vb = v.reshape(b, nb, MOBA_BLOCK, h, dh).transpose(0, 3, 1, 2, 4)
    k_mean = jnp.mean(kb.astype(jnp.float32), axis=3)
    n_top = min(MOBA_TOPK, nb)
    blk = jnp.arange(nb, dtype=jnp.int32)
    offs = jnp.arange(MOBA_BLOCK, dtype=jnp.int32)
    gather = jax.vmap(jax.vmap(lambda blocks, ix: blocks[ix]))
    scale = dh ** -0.5

    def attend_block(c):
        qc = lax.dynamic_slice_in_dim(q, c * q_block, q_block, axis=1)
        qpos = pos0 + c * q_block + jnp.arange(q_block, dtype=jnp.int32)
        own = qpos // MOBA_BLOCK
        gate = jnp.einsum('bqhd,bhnd->bhqn', qc.astype(jnp.float32), k_mean)
        gate = jnp.where(blk[None, None, None, :] < own[None, None, :, None], gate, -jnp.inf)
        _, top = lax.top_k(gate, n_top)
        own_idx = jnp.broadcast_to(own[None, None, :, None], (b, h, q_block, 1))
        idx = jnp.concatenate([top.astype(jnp.int32), own_idx], axis=-1)
        slot_ok = jnp.concatenate([jnp.arange(n_top)[None, :] < own[:, None],
                                   jnp.ones((q_block, 1), dtype=bool)], axis=-1)
        kg = gather(kb, idx)
        vg = gather(vb, idx)
        key_pos = idx[..., None] * MOBA_BLOCK + offs
        mask = slot_ok[None, None, :, :, None] & (key_pos <= qpos[None, None, :, None, None])
        s = jnp.einsum('bqhd,bhqrkd->bhqrk', qc, kg, preferred_element_type=jnp.float32) * scale
        s = jnp.where(mask, s, -jnp.inf)
        p = jax.nn.softmax(s.reshape(b, h, q_block, -1), axis=-1).reshape(s.shape)
        return jnp.einsum('bhqrk,bhqrkd->bqhd', p.astype(vg.dtype), vg)

    out = lax.map(attend_block, jnp.arange(sq // q_block, dtype=jnp.int32))
    return out.transpose(1, 0, 2, 3, 4).reshape(b, sq, h, dh)


def multiscale_pool(p, hist, pos0):
    b, s, c = p.shape
    full = jnp.concatenate([hist.astype(p.dtype), p], axis=1)
    ff = full.astype(jnp.float32)
    cs = jnp.concatenate([jnp.zeros((b, 1, c), jnp.float32), jnp.cumsum(ff, axis=1)], axis=1)
    end = cs[:, POOL_HIST + 1:]
    pos = jnp.arange(s, dtype=jnp.int32) + pos0
    outs = []
    for g, w in enumerate(POOL_WINDOWS):
        sl = slice(g * POOL_GROUP, (g + 1) * POOL_GROUP)
        start = cs[:, POOL_HIST + 1 - w: POOL_HIST + 1 - w + s, sl]
        count = jnp.minimum(pos + 1, w).astype(jnp.float32)[None, :, None]
        outs.append((end[..., sl] - start) / count - ff[:, POOL_HIST:, sl])
    pooled = jnp.concatenate(outs, axis=-1).astype(p.dtype)
    return pooled, full[:, -POOL_HIST:]


def moba_pool_mixer(xn, w_in, w_o, w_pool, pool_scale, pos0, k_past, v_past, pool_hist, q_block):
    b, s, _ = xn.shape
    proj = xn @ w_in
    q, k, v, p = jnp.split(proj, [ATTN_WIDTH, 2 * ATTN_WIDTH, 3 * ATTN_WIDTH], axis=-1)
    q = rotary(q.reshape(b, s, N_ATTN_HEADS, HEAD_DIM), pos0)
    k = rotary(k.reshape(b, s, N_ATTN_HEADS, HEAD_DIM), pos0)
    v = v.reshape(b, s, N_ATTN_HEADS, HEAD_DIM)
    if k_past is None:
        k_all, v_all = k, v
    else:
        k_all = jnp.concatenate([k_past.astype(k.dtype), k], axis=1)
        v_all = jnp.concatenate([v_past.astype(v.dtype), v], axis=1)
    attn = moba_attention(q, k_all, v_all, pos0, q_block).reshape(b, s, ATTN_WIDTH)
    pooled, new_hist = multiscale_pool(p, pool_hist, pos0)
    pooled = jnp.einsum('bsgc,gcd->bsgd', pooled.reshape(b, s, N_POOL_GROUPS, POOL_GROUP), w_pool)
    pooled = pooled.reshape(b, s, POOL_WIDTH) * pool_scale
    out = jnp.concatenate([attn, pooled], axis=-1) @ w_o
    return out, k, v, new_hist


def chunk_gmlp(xn, w_in, ln_g, ln_b, w_s, b_s, w_o):
    b, s, _ = xn.shape
    z = jax.nn.gelu(xn @ w_in)
    u, v = jnp.split(z, 2, axis=-1)
    v = layer_norm(v, ln_g, ln_b)
    L = SGU_CHUNK if s >= SGU_CHUNK else s
    w = jnp.where(jnp.tril(jnp.ones((L, L), dtype=bool)), w_s[:, :L, :L], 0.0).astype(v.dtype)
    vc = v.reshape(b, s // L, L, N_SGU_GROUPS, SGU_GROUP)
    mixed = jnp.einsum('gts,bnsgc->bntgc', w, vc) + b_s[:, :L].T[None, None, :, :, None]
    out = u * mixed.reshape(b, s, SGU_WIDTH)
    return out @ w_o, v


def sq_relu_mlp(xn, w_up, w_down):
    h = jax.nn.relu(xn @ w_up)
    return (h * h) @ w_down


def setup_inputs(seed: int = 0) -> dict:
    key = jax.random.key(seed)
    ks = jax.random.split(key, 21)
    f32 = jnp.float32
    n_pages = PAST_LEN // PAGE_SIZE
    n_phys = (5 * DEC_BATCH * n_pages) // 4

    def nrm(k, shape, scale=1.0):
        return jax.random.normal(k, shape, f32) * scale

    page_table = jax.random.permutation(ks[5], n_phys)[: DEC_BATCH * n_pages]
    page_table = page_table.reshape(DEC_BATCH, n_pages).astype(jnp.int32)
    return {
        'x_prompt': nrm(ks[0], (BATCH, SEQ, D_MODEL)),
        'x_sample': nrm(ks[1], (DEC_BATCH, DEC_SEQ, D_MODEL)),
        'cache_k': nrm(ks[2], (N_EVEN, n_phys, PAGE_SIZE, N_ATTN_HEADS, HEAD_DIM)),
        'cache_v': nrm(ks[3], (N_EVEN, n_phys, PAGE_SIZE, N_ATTN_HEADS, HEAD_DIM)),
        'state_pool': nrm(ks[4], (N_EVEN, DEC_BATCH, POOL_HIST, POOL_WIDTH)),
        'page_table': page_table,
        'norm_mix': 1.0 + 0.02 * nrm(ks[6], (DEPTH, D_MODEL)),
        'w_in_even': nrm(ks[7], (N_EVEN, D_MODEL, 3 * ATTN_WIDTH + POOL_WIDTH), D_MODEL ** -0.5),
        'w_o_even': nrm(ks[8], (N_EVEN, ATTN_WIDTH + POOL_WIDTH, D_MODEL), (ATTN_WIDTH + POOL_WIDTH) ** -0.5),
        'w_pool': nrm(ks[9], (N_EVEN, N_POOL_GROUPS, POOL_GROUP, POOL_GROUP), POOL_GROUP ** -0.5),
        'pool_scale': 1.0 + 0.02 * nrm(ks[10], (N_EVEN, POOL_WIDTH)),
        'w_in_odd': nrm(ks[11], (N_ODD, D_MODEL, 2 * SGU_WIDTH), D_MODEL ** -0.5),
        'sgu_ln_g': 1.0 + 0.02 * nrm(ks[12], (N_ODD, SGU_WIDTH)),
        'sgu_ln_b': 0.02 * nrm(ks[13], (N_ODD, SGU_WIDTH)),
        'sgu_w': nrm(ks[14], (N_ODD, N_SGU_GROUPS, SGU_CHUNK, SGU_CHUNK), SGU_CHUNK ** -0.5),
        'sgu_b': 1.0 + 0.02 * nrm(ks[15], (N_ODD, N_SGU_GROUPS, SGU_CHUNK)),
        'w_o_odd': nrm(ks[16], (N_ODD, SGU_WIDTH, D_MODEL), SGU_WIDTH ** -0.5),
        'norm_ffn': 1.0 + 0.02 * nrm(ks[17], (DEPTH, D_MODEL)),
        'w_ffn_up': nrm(ks[18], (DEPTH, D_MODEL, D_FF), D_MODEL ** -0.5),
        'w_ffn_down': nrm(ks[19], (DEPTH, D_FF, D_MODEL), D_FF ** -0.5),
        'norm_final': 1.0 + 0.02 * nrm(ks[20], (D_MODEL,)),
    }


def reference(x_prompt, x_sample, cache_k, cache_v, state_pool, page_table,
              norm_mix, w_in_even, w_o_even, w_pool, pool_scale,
              w_in_odd, sgu_ln_g, sgu_ln_b, sgu_w, sgu_b, w_o_odd,
              norm_ffn, w_ffn_up, w_ffn_down, norm_final):
    n_seq, n_pages = page_table.shape
    past_len = n_pages * cache_k.shape[2]
    yp, ys = x_prompt, x_sample
    prompt_k, prompt_v, sample_k, sample_v = [], [], [], []
    prompt_pool, sample_pool, sample_sgu = [], [], []
    for l in range(DEPTH):
        hp = rms_norm(yp, norm_mix[l])
        hs = rms_norm(ys, norm_mix[l])
        if l % 2 == 0:
            i = l // 2
            zero_hist = jnp.zeros((yp.shape[0], POOL_HIST, POOL_WIDTH), hp.dtype)
            op, kp_new, vp_new, pool_p = moba_pool_mixer(
                hp, w_in_even[i], w_o_even[i], w_pool[i], pool_scale[i],
                0, None, None, zero_hist, PROMPT_Q_BLOCK)
            k_past = cache_k[i][page_table].reshape(n_seq, past_len, N_ATTN_HEADS, HEAD_DIM)
            v_past = cache_v[i][page_table].reshape(n_seq, past_len, N_ATTN_HEADS, HEAD_DIM)
            os_, ks_new, vs_new, pool_s = moba_pool_mixer(
                hs, w_in_even[i], w_o_even[i], w_pool[i], pool_scale[i],
                past_len, k_past, v_past, state_pool[i], SAMPLE_Q_BLOCK)
            prompt_k.append(kp_new)
            prompt_v.append(vp_new)
            sample_k.append(ks_new)
            sample_v.append(vs_new)
            prompt_pool.append(pool_p)
            sample_pool.append(pool_s)
        else:
            j = l // 2
            op, _ = chunk_gmlp(hp, w_in_odd[j], sgu_ln_g[j], sgu_ln_b[j], sgu_w[j], sgu_b[j], w_o_odd[j])
            os_, v_rows = chunk_gmlp(hs, w_in_odd[j], sgu_ln_g[j], sgu_ln_b[j], sgu_w[j], sgu_b[j], w_o_odd[j])
            sample_sgu.append(v_rows)
        yp = yp + op
        ys = ys + os_
        yp = yp + sq_relu_mlp(rms_norm(yp, norm_ffn[l]), w_ffn_up[l], w_ffn_down[l])
        ys = ys + sq_relu_mlp(rms_norm(ys, norm_ffn[l]), w_ffn_up[l], w_ffn_down[l])
    y_prompt = rms_norm(yp, norm_final)
    y_sample = rms_norm(ys, norm_final)
    return (y_prompt, y_sample,
            jnp.stack(prompt_k), jnp.stack(prompt_v),
            jnp.stack(sample_k), jnp.stack(sample_v),
            jnp.stack(prompt_pool), jnp.stack(sample_pool),
            jnp.stack(sample_sgu))
```

```python
import numpy as np
import concourse.bass as bass
import concourse.mybir as mybir
from concourse.bass_utils import run_bass_kernel_spmd
from contextlib import ExitStack

F32 = mybir.dt.float32
BF16 = mybir.dt.bfloat16
I32 = mybir.dt.int32
ALU = mybir.AluOpType
AF = mybir.ActivationFunctionType
AX = mybir.AxisListType

NT = 2112
TB = [(0, 512), (512, 512), (1024, 512), (1536, 512), (2048, 64)]
TILES = [(i * 128, 128) for i in range(16)] + [(2048, 64)]
NEG = -30000.0
import os
DBG = {k: 1 for k in os.environ.get('KDBG', '').split(',') if k}


class Res:
    __slots__ = ("name", "wr", "acc", "sem", "cnt", "excl")

    def __init__(self, name="", excl=False):
        self.name = name
        self.excl = excl
        self.wr = {}
        self.acc = {}
        self.sem = None
        self.cnt = 0


class Op:
    __slots__ = ("eng", "fns", "deps", "seq", "tkey", "order", "val", "target", "dma", "cclock", "waits")


class Prog:
    ENGS = ["pe", "act", "dve", "pool", "sp"]

    def __init__(self, nc):
        self.nc = nc
        self.ops = []
        self.per_eng = {e: [] for e in self.ENGS}
        self.stores = []
        self.last_by_track = {}
        self.pending = {e: None for e in self.ENGS}

    def barrier(self):
        snap = list(self.last_by_track.values())
        for e in self.ENGS:
            self.pending[e] = snap

    def op(self, eng, fns, reads=(), writes=(), dma=None, store=False):
        o = Op()
        o.eng = eng
        o.fns = list(fns) if isinstance(fns, (list, tuple)) else [fns]
        deps = []
        if self.pending[eng] is not None:
            deps.extend(self.pending[eng])
            self.pending[eng] = None
        reads = list(reads)
        writes = list(writes)
        for r in reads:
            if r.excl and r not in writes:
                writes.append(r)
        for r in reads:
            deps.extend(r.wr.values())
        for w in writes:
            deps.extend(w.acc.values())
        o.deps = deps
        o.dma = dma
        o.target = False
        o.waits = []
        o.seq = len(self.per_eng[eng])
        if dma is not None:
            dma.cnt += 16 * len(o.fns)
            o.tkey = ("dma", id(dma))
            o.order = dma.cnt
            o.val = dma.cnt
        else:
            o.tkey = eng
            o.order = o.seq
            o.val = None
        for r in reads:
            r.acc[o.tkey] = o
        for w in writes:
            w.wr[o.tkey] = o
            w.acc[o.tkey] = o
        self.per_eng[eng].append(o)
        self.ops.append(o)
        self.last_by_track[o.tkey] = o
        if store:
            self.stores.append(o)
        return o

    def finalize(self):
        fin = Op()
        fin.eng = "sp"; fin.fns = []; fin.deps = list(self.stores); fin.dma = None
        fin.target = False; fin.waits = []; fin.seq = len(self.per_eng["sp"])
        fin.tkey = "sp"; fin.order = fin.seq; fin.val = None
        self.per_eng["sp"].append(fin)
        self.ops.append(fin)
        clock = {e: {} for e in self.ENGS}
        for o in self.ops:
            ck = clock[o.eng]
            best = {}
            for d in o.deps:
                if ck.get(d.tkey, -1) >= d.order:
                    continue
                b = best.get(d.tkey)
                if b is None or b.order < d.order:
                    best[d.tkey] = d
            for d in best.values():
                d.target = True
                for k, v in d.cclock.items():
                    if ck.get(k, -1) < v:
                        ck[k] = v
            o.waits = list(best.values())
            cc = dict(ck)
            cc[o.tkey] = o.order
            o.cclock = cc
            o.deps = None
        for e in self.ENGS:
            c = 0
            for o in self.per_eng[e]:
                if o.dma is None:
                    if o.target:
                        c += 1
                    o.val = c
        for o in self.ops:
            o.cclock = None

    def emit(self):
        nc = self.nc
        with ExitStack() as st:
            esem = {e: st.enter_context(nc.semaphore("s_" + e)) for e in ["pe", "act", "dve", "pool", "sp"]}
            n = 0
            for o in self.ops:
                if o.dma is not None and o.dma.sem is None:
                    o.dma.sem = st.enter_context(nc.semaphore("d%d" % n))
                    n += 1
            block = st.enter_context(nc.Block())

            def run(eng_name):
                def body(eng):
                    for o in self.per_eng[eng_name]:
                        for d in o.waits:
                            sem = d.dma.sem if d.dma is not None else esem[d.eng]
                            eng.wait_ge(sem, d.val)
                        last = None
                        for fn in o.fns:
                            last = fn(eng)
                            if o.dma is not None:
                                last.then_inc(o.dma.sem, 16)
                        if o.dma is None and o.target:
                            last.then_inc(esem[eng_name], 1)
                return body

            block.tensor(run("pe"))
            block.scalar(run("act"))
            block.vector(run("dve"))
            block.gpsimd(run("pool"))
            block.sync(run("sp"))


def MM(out, lhsT, rhs, start=True, stop=True):
    return lambda e: e.matmul(out, lhsT=lhsT, rhs=rhs, start=start, stop=stop)


def TR(out, in_, ident):
    return lambda e: e.transpose(out, in_, ident)


def ACT(out, in_, func, **kw):
    return lambda e: e.activation(out=out, in_=in_, func=func, **kw)


def TT(out, in0, in1, op):
    return lambda e: e.tensor_tensor(out=out, in0=in0, in1=in1, op=op)


def TS(out, in0, s1, s2, op0, op1=None):
    if op1 is None:
        return lambda e: e.tensor_scalar(out=out, in0=in0, scalar1=s1, scalar2=None, op0=op0)
    return lambda e: e.tensor_scalar(out=out, in0=in0, scalar1=s1, scalar2=s2, op0=op0, op1=op1)


def STT(out, in0, scalar, in1, op0, op1):
    return lambda e: e.scalar_tensor_tensor(out=out, in0=in0, scalar=scalar, in1=in1, op0=op0, op1=op1)


def CP(out, in_):
    return lambda e: e.tensor_copy(out=out, in_=in_)


def ACP(out, in_):
    return lambda e: e.copy(out=out, in_=in_)


def DMA(out, in_):
    return lambda e: e.dma_start(out=out, in_=in_)


def IDMA(out, in_, idx):
    return lambda e: e.indirect_dma_start(out=out, out_offset=None, in_=in_,
                                          in_offset=bass.IndirectOffsetOnAxis(ap=idx, axis=0))


def RED(out, in_, axis, op):
    return lambda e: e.tensor_reduce(out=out, in_=in_, axis=axis, op=op)


def RECIP(out, in_):
    return lambda e: e.reciprocal(out=out, in_=in_)


def MAX8(out, in_):
    return lambda e: e.max(out=out, in_=in_)


def MEMSET(ap, v):
    return lambda e: e.memset(ap, v)


def _reshape(v, shape):
    if len(shape) == 1:
        return v
    names = ["a", "b", "c", "d"][:len(shape)]
    kw = {names[k]: shape[k] for k in range(1, len(shape))}
    return v.rearrange("p (%s) -> p %s" % (" ".join(names), " ".join(names)), **kw)


class Arena:
    def __init__(self, ap, nwords):
        self.ap = ap
        self.n = nwords
        self.top = 0
        self.peak = 0

    def mark(self):
        return self.top

    def release(self, m):
        self.top = m

    def _take(self, w):
        w = (w + 7) // 8 * 8
        off = self.top
        self.top += w
        self.peak = max(self.peak, self.top)
        assert self.top <= self.n, "arena overflow %d > %d" % (self.top, self.n)
        return off, w

    def f32(self, *shape):
        n = int(np.prod(shape))
        off, w = self._take(n)
        return _reshape(self.ap[:, off:off + n], shape)

    def i32(self, *shape):
        n = int(np.prod(shape))
        off, w = self._take(n)
        return _reshape(self.ap[:, off:off + n].bitcast(I32), shape)

    def bf16(self, *shape):
        n = int(np.prod(shape))
        off, w = self._take((n + 1) // 2)
        return _reshape(self.ap[:, off:off + w].bitcast(BF16)[:, 0:n], shape)


class StopBuild(Exception):
    pass


class Builder:
    def stage(self, name):
        if self.stop_at == name:
            raise StopBuild()

    def __init__(self, npg, n_layers=4, do_sample_attn=True, stop_at=None):
        self.stop_at = stop_at
        self.npg = npg
        self.n_layers = n_layers
        self.do_sample_attn = do_sample_attn
        self.nc = bass.Bass("TRN2", target_bir_lowering=False)

    def dram(self):
        nc = self.nc
        I = {}
        O = {}

        def inp(name, shape, dt=F32):
            I[name] = nc.dram_tensor(name, list(shape), dt, kind="ExternalInput").ap()

        def out(name, shape):
            O[name] = nc.dram_tensor(name, list(shape), F32, kind="ExternalOutput").ap()

        inp("xp", [2048, 1024]); inp("xs", [64, 1024])
        for nm in ["ck0", "ck1", "cv0", "cv1"]:
            inp(nm, [self.npg * 128, 512])
        inp("spool", [2, 240, 512]); inp("ptab", [1, 256], I32)
        inp("nmix", [128, 4, 8]); inp("nffn", [128, 4, 8]); inp("nfin", [1, 1024]); inp("pscale", [128, 2, 4])
        inp("w_in_even", [2, 1024, 2048]); inp("w_o_even", [2, 1024, 1024]); inp("w_pool", [2, 4, 128, 128])
        inp("w_in_odd", [2, 1024, 2048]); inp("ln_g", [2, 1024]); inp("ln_b", [2, 1024])
        inp("sgu_wT", [2, 8, 128, 128]); inp("sgu_w", [2, 8, 128, 128]); inp("sgu_b", [2, 8, 128]); inp("w_o_odd", [2, 1024, 1024])
        inp("w_up", [4, 1024, 4096]); inp("w_dn", [4, 4096, 1024])
        inp("cosT", [128, 17, 32]); inp("sinT", [128, 17, 32]); inp("tri", [128, 128]); inp("smask", [64, 64])
        inp("cand", [128, 4, 64]); inp("invcnt", [128, 4, 16]); inp("iota", [128, 1]); inp("ident", [128, 128])
        inp("ones", [128, 128])
        out("yp", [2048, 1024]); out("ys", [64, 1024])
        out("pk", [2, 2048, 512]); out("pv", [2, 2048, 512]); out("sk", [2, 64, 512]); out("sv", [2, 64, 512])
        out("pps", [2, 15, 512]); out("sps", [2, 16, 15, 512]); out("sgv", [2, 64, 1024])
        self.I, self.O = I, O

    def bank(self):
        b = self._bank
        self._bank = (self._bank + 1) % 8
        return self.ps[b], self.rps[b]

    def load(self, out_ap, in_ap, res, eng="sp"):
        self.P.op(eng, DMA(out_ap, in_ap), writes=[res], dma=res)

    def store(self, out_ap, in_ap, res):
        self.P.op("sp", DMA(out_ap, in_ap), reads=[res], dma=res, store=True)

    def build(self):
        self.dram()
        nc = self.nc
        with ExitStack() as st:
            self.st = st
            AW = 32400
            RES = st.enter_context(nc.sbuf_tensor("RES", [128, 8, NT], F32))
            CW = 3840
            cst = st.enter_context(nc.sbuf_tensor("CST", [128, CW], F32))
            arena_t = st.enter_context(nc.sbuf_tensor("ARENA", [128, AW], F32))
            self.ps = []
            for b in range(8):
                t = st.enter_context(nc.psum_tensor("ps%d" % b, [128, 512], F32))
                self.ps.append(t)
            self.P = Prog(nc)
            P = self.P
            self.rps = [Res("ps%d" % b, excl=True) for b in range(8)]
            self._bank = 0
            self.RES = RES
            self.rRES = [Res("RES%d" % i) for i in range(5)]
            self.A = Arena(arena_t[:, :], AW)
            self.C = Arena(cst[:, :], CW)
            self.consts()
            self.XN = self.A.bf16(8, NT)
            self.rXN = [Res("XN%d" % i) for i in range(5)]
            try:
                self.stage("consts")
                self.load_x()
                self.stage("loadx")
                for l in range(self.n_layers):
                    if l % 2 == 0:
                        self.even_layer(l)
                    else:
                        self.odd_layer(l)
                    self.stage("mixer%d" % l)
                    self.ffn(l)
                    self.stage("ffn%d" % l)
                self.final()
            except StopBuild:
                pass
            P.finalize()
            P.emit()
        return nc

    def consts(self):
        P, C, I = self.P, self.C, self.I
        self.rC = Res("consts")
        rC = self.rC

        def ld(name, shape, src=None):
            t = C.f32(*shape)
            self.load(t, I[name] if src is None else src, rC)
            return t

        def ldb(name, shape, src=None, rows=128):
            t = C.bf16(*shape)
            self.load(t[0:rows], I[name] if src is None else src, rC, eng="pool")
            return t

        self.cosT = ld("cosT", [17, 32]); self.sinT = ld("sinT", [17, 32])
        self.trif = ld("tri", [128]); self.trib = ldb("tri", [128])
        self.smaskb = ldb("smask", [64], rows=64)
        self.cand = ld("cand", [4, 64]); self.invcnt = ld("invcnt", [4, 16]); self.iota = ld("iota", [1])
        self.identf = ld("ident", [128]); self.identb = ldb("ident", [128])
        self.onesb = ldb("ones", [128])
        self.nmix = ld("nmix", [4, 8]); self.nffn = ld("nffn", [4, 8]); self.pscale = ld("pscale", [2, 4])
        self.nfin = ld("nfin", [1024], src=I["nfin"].partition_broadcast(128))
        ptb = C.i32(256)
        self.load(ptb, I["ptab"].partition_broadcast(128), rC)
        pf = C.f32(256)
        self.pidx = C.i32(256)
        P.op("dve", CP(pf, ptb), reads=[rC], writes=[rC])
        P.op("dve", TS(pf, pf, 128.0, self.iota[:, 0:1], ALU.mult, ALU.add), reads=[rC], writes=[rC])
        P.op("dve", CP(self.pidx, pf), reads=[rC], writes=[rC])

    def load_x(self):
        P, A = self.P, self.A
        m = A.mark()
        xt = [A.f32(1024) for _ in range(2)]
        rxt = [Res(), Res()]
        for ti, (c0, r) in enumerate(TILES):
            b = ti % 2
            src = self.I["xp"][c0:c0 + r, :] if ti < 16 else self.I["xs"]
            self.load(xt[b][0:r], src, rxt[b])
            for half in range(2):
                ps, rp = self.bank()
                P.op("pe", [TR(ps[:, k * 128:k * 128 + r], xt[b][0:r, (half * 4 + k) * 128:(half * 4 + k + 1) * 128], self.identf[0:r, 0:r])
                            for k in range(4)], reads=[rxt[b], self.rC], writes=[rp])
                src_v = ps[:, :].rearrange("p (k t) -> p k t", k=4)[:, :, 0:r]
                P.op("act" if half == 0 else "dve",
                     (ACP if half == 0 else CP)(self.RES[:, half * 4:half * 4 + 4, c0:c0 + r], src_v),
                     reads=[rp], writes=[self.rRES[min(c0 // 512, 4)]])
        A.release(m)
        P.barrier()

    def rmsnorm_all(self, gains):
        P, A = self.P, self.A
        m = A.mark()
        sq = [A.bf16(512) for _ in range(2)]
        rsq = [Res(), Res()]
        rs = [A.f32(512) for _ in range(2)]
        rrs = [Res(), Res()]
        k = 0
        for tb, (c0, n) in enumerate(TB):
            ps, rp = self.bank()
            for c in range(8):
                b = k % 2
                k += 1
                P.op("act", ACT(sq[b][:, 0:n], self.RES[:, c, c0:c0 + n], AF.Square), reads=[self.rRES[tb]], writes=[rsq[b]])
                P.op("pe", MM(ps[:, 0:n], self.onesb[:, :], sq[b][:, 0:n], start=(c == 0), stop=(c == 7)),
                     reads=[rsq[b], self.rC], writes=[rp])
            b2 = tb % 2
            P.op("act", ACT(rs[b2][:, 0:n], ps[:, 0:n], AF.Sqrt, bias=1e-6, scale=1.0 / 1024), reads=[rp], writes=[rrs[b2]])
            P.op("dve", RECIP(rs[b2][:, 0:n], rs[b2][:, 0:n]), reads=[rrs[b2]], writes=[rrs[b2]])
            for c in range(8):
                P.op("dve",
                     STT(self.XN[:, c, c0:c0 + n], self.RES[:, c, c0:c0 + n], gains[:, c:c + 1], rs[b2][:, 0:n], ALU.mult, ALU.mult),
                     reads=[self.rRES[tb], rrs[b2], self.rC], writes=[self.rXN[tb]])
        A.release(m)
        P.barrier()

    def wload(self, dst, src, res):
        self.P.op("pool", DMA(dst, src), writes=[res], dma=res)

    def even_layer(self, l):
        P, A, I, O = self.P, self.A, self.I, self.O
        i = l // 2
        self.rmsnorm_all(self.nmix[:, l, :])
        self.stage("norm")
        Win = I["w_in_even"][i].rearrange("(c p) f -> p c f", p=128)
        m0 = A.mark()
        WS = [A.bf16(8, 512)]
        WS.append(WS[0])
        rWS = [Res()]
        rWS.append(rWS[0])
        QsT = A.bf16(4, 64); KsT = A.bf16(4, 64); Vs = A.bf16(512)
        self.attn_setup()
        kmf = A.f32(4, 8); kmT = A.bf16(4, 8); rkm = Res()
        wp = A.bf16(4, 128)
        rwp = Res()
        mP = A.mark()
        POOLED = A.bf16(4, NT)
        rPOOLED = [Res() for _ in range(4)]
        m1 = A.mark()
        L = 16 + 2048
        PG = A.f32(L); Y = [A.f32(L), A.f32(L)]
        FULL = A.f32(16, 19); FY = [A.f32(16, 19), A.f32(16, 19)]
        hist = [A.f32(128), A.f32(128)]
        t16 = A.f32(16)
        pst = A.f32(512)
        nrow = A.f32(512)
        newc = A.f32(64); rnewc = Res()
        rPG, rY, rFULL, rFY, rhist, rt16, rpst, rnrow = Res(), [Res(), Res()], Res(), [Res(), Res()], [Res(), Res()], Res(), Res(), Res()
        P.op("pool", [MEMSET(PG[:, 0:16], 0.0), MEMSET(Y[0][:, 0:16], 0.0), MEMSET(Y[1][:, 0:16], 0.0)], writes=[rPG, rY[0], rY[1]])
        self.wload(WS[0], Win[:, :, 1536:2048], rWS[0])
        psS, rpsS = self.ps[7], self.rps[7]
        psN, rpsN = self.ps[6], self.rps[6]
        self._bank = 0
        wins = [2, 4, 8, 16]
        for g in range(4):
            w = wins[g]
            for tb, (c0, n) in enumerate(TB):
                ps, rp = self.ps[tb % 4], self.rps[tb % 4]
                P.op("pe", [MM(ps[:, 0:n], WS[0][:, c, g * 128:(g + 1) * 128], self.XN[:, c, c0:c0 + n], start=(c == 0), stop=(c == 7))
                            for c in range(8)], reads=[rWS[0], self.rXN[tb]], writes=[rp])
                if tb < 4:
                    P.op("act", ACP(PG[:, 16 + c0:16 + c0 + n], ps[:, 0:n]), reads=[rp], writes=[rPG])
                else:
                    P.op("act", ACP(FULL[:, :, 15:19], ps[:, 0:64].rearrange("p (s t) -> p s t", t=4)), reads=[rp], writes=[rFULL])
            for half in range(2):
                self.load(hist[half][0:120], I["spool"][i, half * 120:(half + 1) * 120, g * 128:(g + 1) * 128], rhist[half])
                ps, rp = self.ps[4 + half], self.rps[4 + half]
                P.op("pe", TR(ps[:, 0:120], hist[half][0:120, :], self.identf[0:120, 0:120]), reads=[rhist[half], self.rC], writes=[rp])
                P.op("dve", CP(FULL[:, half * 8:(half + 1) * 8, 0:15], ps[:, 0:120].rearrange("p (s t) -> p s t", t=15)), reads=[rp], writes=[rFULL])
            src, rsrc = PG, rPG
            s = 1
            k = 0
            while s < w:
                dst, rdst = Y[k % 2], rY[k % 2]
                P.op("dve", TT(dst[:, 16:L], src[:, 16:L], src[:, 16 - s:L - s], ALU.add), reads=[rsrc], writes=[rdst])
                src, rsrc = dst, rdst
                s *= 2
                k += 1
            P.op("dve", STT(POOLED[:, g, 0:2048], src[:, 16:L], 1.0 / w, PG[:, 16:L], ALU.mult, ALU.subtract), reads=[rsrc, rPG], writes=[rPOOLED[g]])
            P.op("dve", TT(t16, src[:, 16:32], self.invcnt[:, g, :], ALU.mult), reads=[rsrc, self.rC], writes=[rt16])
            P.op("dve", TT(POOLED[:, g, 0:16], t16, PG[:, 16:32], ALU.subtract), reads=[rt16, rPG], writes=[rPOOLED[g]])
            P.op("pe", TR(psS[0:16, g * 128:(g + 1) * 128], PG[:, L - 16:L], self.identf[:, :]), reads=[rPG, self.rC], writes=[rpsS])
            src, rsrc = FULL, rFULL
            s = 1
            k = 0
            lo = 0
            while s < w:
                dst, rdst = FY[k % 2], rFY[k % 2]
                lo2 = lo + s
                P.op("dve", TT(dst[:, :, lo2:19], src[:, :, lo2:19], src[:, :, lo2 - s:19 - s], ALU.add), reads=[rsrc], writes=[rdst])
                src, rsrc = dst, rdst
                lo = lo2
                s *= 2
                k += 1
            P.op("dve", STT(POOLED[:, g, 2048:2112].rearrange("p (s t) -> p s t", t=4), src[:, :, 15:19], 1.0 / w, FULL[:, :, 15:19],
                            ALU.mult, ALU.subtract), reads=[rsrc, rFULL], writes=[rPOOLED[g]])
            P.op("dve", CP(newc[:, :].rearrange("p (s t) -> p s t", t=4), FULL[:, :, 15:19]), reads=[rFULL], writes=[rnewc])
            P.op("pe", TR(psN[0:64, g * 128:(g + 1) * 128], newc[:, :], self.identf[:, :]), reads=[rnewc, self.rC], writes=[rpsN])
        P.op("act", ACP(pst[0:16, :], psS[0:16, :]), reads=[rpsS], writes=[rpst])
        self.store(O["pps"][i], pst[1:16, :], rpst)
        P.op("act", ACP(nrow[0:64, :], psN[0:64, :]), reads=[rpsN], writes=[rnrow])
        for s_ in range(16):
            self.store(O["sps"][i, s_, 11:15, :], nrow[4 * s_:4 * s_ + 4, :], rnrow)
        rd2d = Res()
        P.op("sp", DMA(O["sps"][i, :, 0:11, :], I["spool"][i].rearrange("(s r) f -> s r f", r=15)[:, 4:15, :]), dma=rd2d, store=True)
        A.release(m1)
        P.barrier()
        self.stage("pool")
        QT = A.bf16(4, 2048); KT = A.bf16(4, 2048); V = A.bf16(16, 512)
        rQT = [Res() for _ in range(17)]; rKT = [Res() for _ in range(17)]; rV = [Res() for _ in range(17)]
        raw = [A.f32(512)] * 2; rot = [A.f32(512)] * 2
        ta = [A.f32(256)] * 2; tb_ = [A.f32(256)] * 2
        xb = [A.bf16(512)] * 2
        rraw, rrot, rta, rtb, rxb = ([Res()] * 2 for _ in range(5))
        self._bank = 0
        for pi, pname in enumerate(["q", "k", "v"]):
            sl = (pi + 1) % 2
            self.wload(WS[sl], Win[:, :, pi * 512:(pi + 1) * 512], rWS[sl])
            for ti, (c0, r) in enumerate(TILES):
                tbi = min(c0 // 512, 4)
                b = ti % 2
                ps, rp = self.bank()
                P.op("pe", [MM(ps[0:r, :], self.XN[:, c, c0:c0 + r], WS[sl][:, c, :], start=(c == 0), stop=(c == 7)) for c in range(8)],
                     reads=[rWS[sl], self.rXN[tbi]], writes=[rp])
                P.op("act", ACP(raw[b][0:r, :], ps[0:r, :]), reads=[rp], writes=[rraw[b]])
                if pname == "v":
                    dst = O["pv"][i, c0:c0 + r, :] if ti < 16 else O["sv"][i]
                    self.store(dst, raw[b][0:r, :], rraw[b])
                    P.op("dve", CP(V[0:r, ti, :] if ti < 16 else Vs[0:64, :], raw[b][0:r, :]), reads=[rraw[b]], writes=[rV[ti]])
                    continue
                r3 = raw[b][0:r, :].rearrange("p (h d) -> p h d", h=8)
                o3 = rot[b][0:r, :].rearrange("p (h d) -> p h d", h=8)
                x1, x2, o1, o2 = r3[:, :, 0:32], r3[:, :, 32:64], o3[:, :, 0:32], o3[:, :, 32:64]
                cb = self.cosT[0:r, ti, :].unsqueeze(1).to_broadcast([r, 8, 32])
                sb_ = self.sinT[0:r, ti, :].unsqueeze(1).to_broadcast([r, 8, 32])
                ta3 = ta[b][0:r, :].rearrange("p (h d) -> p h d", h=8)
                tb3 = tb_[b][0:r, :].rearrange("p (h d) -> p h d", h=8)
                P.op("dve", TT(o1, x1, cb, ALU.mult), reads=[rraw[b], self.rC], writes=[rrot[b]])
                P.op("dve", TT(ta3, x2, sb_, ALU.mult), reads=[rraw[b], self.rC], writes=[rta[b]])
                P.op("dve", TT(o1, o1, ta3, ALU.subtract), reads=[rrot[b], rta[b]], writes=[rrot[b]])
                P.op("pool", TT(o2, x2, cb, ALU.mult), reads=[rraw[b], self.rC], writes=[rrot[b]])
                P.op("pool", TT(tb3, x1, sb_, ALU.mult), reads=[rraw[b], self.rC], writes=[rtb[b]])
                P.op("pool", TT(o2, o2, tb3, ALU.add), reads=[rrot[b], rtb[b]], writes=[rrot[b]])
                if pname == "k":
                    dst = O["pk"][i, c0:c0 + r, :] if ti < 16 else O["sk"][i]
                    self.store(dst, rot[b][0:r, :], rrot[b])
                P.op("act", ACP(xb[b][0:r, :], rot[b][0:r, :]), reads=[rrot[b]], writes=[rxb[b]])
                ps2, rp2 = self.bank()
                psb = ps2[:, 0:256].bitcast(BF16)
                P.op("pe", [TR(psb[:, pr * 128:pr * 128 + r], xb[b][0:r, pr * 128:(pr + 1) * 128], self.identb[0:r, 0:r]) for pr in range(4)],
                     reads=[rxb[b], self.rC], writes=[rp2])
                dstT = (QT if pname == "q" else KT)
                rdT = (rQT if pname == "q" else rKT)
                dst_ap = dstT[:, :, c0:c0 + r] if ti < 16 else (QsT if pname == "q" else KsT)[:, :, 0:64]
                P.op("dve", CP(dst_ap, psb.rearrange("p (k t) -> p k t", k=4)[:, :, 0:r]), reads=[rp2], writes=[rdT[ti]])
        P.barrier()
        self.stage("qkv")
        CAT = self.XN
        rCAT = [Res() for _ in range(17)]
        self.wload(wp, I["w_pool"][i].rearrange("g c d -> c g d"), rwp)
        self._bank = 0
        for g in range(4):
            for tb, (c0, n) in enumerate(TB):
                ps, rp = self.bank()
                P.op("pe", MM(ps[:, 0:n], wp[:, g, :], POOLED[:, g, c0:c0 + n]), reads=[rwp, rPOOLED[g]], writes=[rp])
                P.op("act", ACT(CAT[:, 4 + g, c0:c0 + n], ps[:, 0:n], AF.Copy, scale=self.pscale[:, i, g:g + 1]), reads=[rp, self.rC],
                     writes=[rCAT[t] for t in range(17) if min(TILES[t][0] // 512, 4) == tb])
        self.stage("wpool")
        P.op("dve", RED(kmf, KT[:, :, 0:2048].rearrange("p k (n t) -> p k n t", t=256), AX.X, ALU.add), reads=rKT[0:16], writes=[rkm])
        P.op("dve", TS(kmT, kmf, 1.0 / 256, None, ALU.mult), reads=[rkm], writes=[rkm])
        for qt in range(16):
            nq = qt // 2
            c0 = qt * 128
            sel = nq >= 4
            tiles = []
            for kt in range(qt + 1):
                own = kt >= 2 * nq
                tiles.append(dict(KT=(lambda h, kt=kt: KT[(h % 2) * 64:(h % 2) * 64 + 64, h // 2, kt * 128:(kt + 1) * 128]),
                                  V=(lambda h, kt=kt: V[:, kt, h * 64:(h + 1) * 64]), nk=128,
                                  slot=(kt // 2 if (sel and not own) else (7 if sel else 0)),
                                  mask=(self.trib[:, :] if kt == qt else None), res=[rKT[kt], rV[kt]]))
            self.attend(128, lambda h, c0=c0: QT[(h % 2) * 64:(h % 2) * 64 + 64, h // 2, c0:c0 + 128], [rQT[qt]], tiles,
                        sel, nq, kmT, rkm, self.cand[:, nq - 4, :] if sel else None, CAT, c0, rCAT[qt])
            self.stage("attn%d" % qt)
        self.stage("pattn")
        P.barrier()
        A.release(mP)
        if self.do_sample_attn:
            Kg = [A.bf16(16, 512) for _ in range(2)]; rKg = [Res(), Res()]
            Vg = A.bf16(16, 512); rVg = Res()
            KgT = A.bf16(4, 2048); rKgT = Res()
            ck = I["ck%d" % i]; cv = I["cv%d" % i]

            def gatherK(s):
                for j in range(16):
                    P.op("pool", IDMA(Kg[s % 2][:, j, :], ck, self.pidx[:, s * 16 + j:s * 16 + j + 1]), reads=[self.rC], writes=[rKg[s % 2]], dma=rKg[s % 2])
            gatherK(0)
            for s in range(16):
                if s + 1 < 16:
                    gatherK(s + 1)
                for j in range(16):
                    P.op("pool", IDMA(Vg[:, j, :], cv, self.pidx[:, s * 16 + j:s * 16 + j + 1]), reads=[self.rC], writes=[rVg], dma=rVg)
                for pr in range(4):
                    for half in range(2):
                        ps, rp = self.ps[6 + (pr * 2 + half) % 2], self.rps[6 + (pr * 2 + half) % 2]
                        psb = ps[:, :].bitcast(BF16)
                        P.op("pe", [TR(psb[:, k * 128:(k + 1) * 128], Kg[s % 2][:, half * 8 + k, pr * 128:(pr + 1) * 128], self.identb[:, :]) for k in range(8)],
                             reads=[rKg[s % 2], self.rC], writes=[rp])
                        P.op("act" if half == 0 else "dve", (ACP if half == 0 else CP)(KgT[:, pr, half * 1024:(half + 1) * 1024], psb), reads=[rp], writes=[rKgT])
                P.op("dve", RED(kmf, KgT[:, :, :].rearrange("p k (n t) -> p k n t", t=256), AX.X, ALU.add), reads=[rKgT], writes=[rkm])
                P.op("dve", TS(kmT, kmf, 1.0 / 256, None, ALU.mult), reads=[rkm], writes=[rkm])
                tiles = []
                for kt in range(16):
                    tiles.append(dict(KT=(lambda h, kt=kt: KgT[(h % 2) * 64:(h % 2) * 64 + 64, h // 2, kt * 128:(kt + 1) * 128]),
                                      V=(lambda h, kt=kt: Vg[:, kt, h * 64:(h + 1) * 64]), nk=128, slot=kt // 2, mask=None, res=[rKgT, rVg]))
                tiles.append(dict(KT=(lambda h: KsT[(h % 2) * 64:(h % 2) * 64 + 64, h // 2, 0:64]),
                                  V=(lambda h: Vs[0:64, h * 64:(h + 1) * 64]), nk=64, slot=8,
                                  mask=self.smaskb[0:64, 4 * s:4 * s + 4], res=[rKT[16], rV[16]]))
                c0 = 2048 + 4 * s
                self.attend(4, lambda h, s=s: QsT[(h % 2) * 64:(h % 2) * 64 + 64, h // 2, 4 * s:4 * s + 4], [rQT[16]], tiles,
                            True, 8, kmT, rkm, None, CAT, c0, rCAT[16])
        else:
            P.op("pool", MEMSET(CAT[:, 0:4, 2048:2112], 0.0), writes=[rCAT[16]])
        A.release(m0)
        P.barrier()
        WS = [A.bf16(8, 512) for _ in range(2)]
        rWS = [Res(), Res()]
        Wo = I["w_o_even"][i].rearrange("(c p) f -> p c f", p=128)
        self.out_proj(Wo, WS, rWS, CAT, rCAT)
        A.release(m0)
        P.barrier()

    def out_proj(self, Wo, WS, rWS, SRC, rSRC17):
        P = self.P
        self._bank = 0
        self.wload(WS[0], Wo[:, :, 0:512], rWS[0])
        self.wload(WS[1], Wo[:, :, 512:1024], rWS[1])
        for ph in range(2):
            for tb, (c0, n) in enumerate(TB):
                rs = [rSRC17[t] for t in range(17) if min(TILES[t][0] // 512, 4) == tb]
                for dcl in range(4):
                    dc = ph * 4 + dcl
                    ps, rp = self.bank()
                    P.op("pe", [MM(ps[:, 0:n], WS[ph][:, k, dcl * 128:(dcl + 1) * 128], SRC[:, k, c0:c0 + n], start=(k == 0), stop=(k == 7)) for k in range(8)],
                         reads=[rWS[ph]] + rs, writes=[rp])
                    P.op("dve", TT(self.RES[:, dc, c0:c0 + n], self.RES[:, dc, c0:c0 + n], ps[:, 0:n], ALU.add), reads=[rp, self.rRES[tb]], writes=[self.rRES[tb]])

    def attn_setup(self):
        A = self.A
        self.PT = [A.bf16(512) for _ in range(2)]; self.rPT = [Res(), Res()]
        self.acc = A.f32(8, 65); self.racc = Res()
        self.gs = A.f32(64); self.mx = A.f32(8, 8); self.sel = A.f32(8, 9); self.rsel = Res()
        self.rec = A.f32(8); self.obf = A.bf16(512); self.robf = Res()
        self._pt = 0
        self._ob = 0

    def attend(self, M, QTf, rQ, tiles, sel, nq, kmT, rkm, cand, CAT, c0, rcat):
        P = self.P
        S = [(self.ps[0], self.rps[0]), (self.ps[1], self.rps[1])]
        Ob = [(self.ps[2], self.rps[2]), (self.ps[3], self.rps[3])]
        Gb, rGb = self.ps[6], self.rps[6]
        acc, racc = self.acc, self.racc
        if sel and not DBG.get("nogate"):
            Gb2, rGb2 = self.ps[7], self.rps[7]
            for par, (gb_, rgb_) in enumerate([(Gb, rGb), (Gb2, rGb2)]):
                P.op("pe", [MM(gb_[0:M, (h // 2) * 8:(h // 2) * 8 + 8], QTf(h), kmT[(h % 2) * 64:(h % 2) * 64 + 64, h // 2, :])
                            for h in range(par, 8, 2)], reads=rQ + [rkm], writes=[rgb_])
            gs4 = self.gs[0:M, :].rearrange("p (k q n) -> p k q n", k=4, q=2)
            for par, (gb_, rgb_) in enumerate([(Gb, rGb), (Gb2, rGb2)]):
                src = gb_[0:M, 0:32].rearrange("p (k n) -> p k n", k=4)
                if cand is not None:
                    P.op("dve", TT(gs4[:, :, par, :], src, cand[0:M, :].rearrange("p (k q n) -> p k q n", k=4, q=2)[:, :, par, :], ALU.add),
                         reads=[rgb_, self.rC], writes=[self.rsel])
                else:
                    P.op("dve", CP(gs4[:, :, par, :], src), reads=[rgb_], writes=[self.rsel])
            for h in range(8):
                P.op("dve", MAX8(self.mx[0:M, h, :], self.gs[0:M, h * 8:(h + 1) * 8]), reads=[self.rsel], writes=[self.rsel])
            P.op("dve", TT(self.sel[0:M, :, 0:8], self.gs[0:M, :].rearrange("p (h n) -> p h n", h=8),
                           self.mx[0:M, :, 2:3].to_broadcast([M, 8, 8]), ALU.is_ge), reads=[self.rsel], writes=[self.rsel])
        G = 512 // M if M >= 128 else 16
        groups = [tiles[k:k + G] for k in range(0, len(tiles), G)]
        g2 = []
        for grp in groups:
            a = [t for t in grp if t["nk"] == 128]
            b = [t for t in grp if t["nk"] != 128]
            if a:
                g2.append(a)
            if b:
                g2.append(b)
        groups = g2
        first_in_slot = {}
        last_in_slot = {}
        for ti, t in enumerate(tiles):
            first_in_slot.setdefault(t["slot"], ti)
            last_in_slot[t["slot"]] = ti
        for h in range(8):
            ob, rob = Ob[self._ob % 2]
            Db, rDb = self.ps[4 + self._ob % 2], self.rps[4 + self._ob % 2]
            dofs = 0
            self._ob += 1
            ti = 0
            for grp in groups:
                sb_, rsb = S[self._pt % 2]
                pt, rpt = self.PT[self._pt % 2], self.rPT[self._pt % 2]
                self._pt += 1
                nk = grp[0]["nk"]
                rr = []
                for t in grp:
                    rr += t["res"]
                P.op("pe", [MM(sb_[0:nk, k * M:(k + 1) * M], t["KT"](h), QTf(h)) for k, t in enumerate(grp)], reads=rQ + rr, writes=[rsb])
                W = len(grp) * M
                P.op("act", ACT(pt[0:nk, 0:W], sb_[0:nk, 0:W], AF.Exp, scale=0.125), reads=[rsb], writes=[rpt])
                for k, t in enumerate(grp):
                    if t["mask"] is not None:
                        P.op("pool", TT(pt[0:nk, k * M:(k + 1) * M], pt[0:nk, k * M:(k + 1) * M], t["mask"], ALU.mult), reads=[rpt, self.rC], writes=[rpt])
                mms = []
                for k, t in enumerate(grp):
                    sl = t["slot"]
                    st_, sp_ = (first_in_slot[sl] == ti), (last_in_slot[sl] == ti)
                    mms.append(MM(ob[0:M, (sl % 8) * 64:(sl % 8) * 64 + 64] if sl < 8 else Db[0:M, 64 + dofs * 4:64 + dofs * 4 + 64],
                                  pt[0:nk, k * M:(k + 1) * M], t["V"](h), start=st_, stop=sp_))
                    mms.append(MM(Db[0:M, dofs + sl:dofs + sl + 1], pt[0:nk, k * M:(k + 1) * M], self.onesb[0:nk, 0:1], start=st_, stop=sp_))
                    ti += 1
                P.op("pe", mms, reads=[rpt, self.rC] + rr, writes=[rob, rDb])
            own = 8 if (sel and nq == 8) else (7 if sel else 0)
            own_ap = ob[0:M, own * 64:own * 64 + 64] if own < 8 else Db[0:M, 64 + dofs * 4:64 + dofs * 4 + 64]
            P.op("act", ACP(acc[0:M, h, 0:64], own_ap), reads=[rob, rDb], writes=[racc])
            P.op("act", ACP(acc[0:M, h, 64:65], Db[0:M, dofs + own:dofs + own + 1]), reads=[rDb], writes=[racc])
            if sel and not DBG.get("nocombo"):
                for nb in range(nq):
                    P.op("dve", STT(acc[0:M, h, 0:64], ob[0:M, nb * 64:nb * 64 + 64], self.sel[0:M, h, nb:nb + 1], acc[0:M, h, 0:64], ALU.mult, ALU.add),
                         reads=[rob, self.rsel, racc], writes=[racc])
                    P.op("dve", STT(acc[0:M, h, 64:65], Db[0:M, dofs + nb:dofs + nb + 1], self.sel[0:M, h, nb:nb + 1], acc[0:M, h, 64:65], ALU.mult, ALU.add),
                         reads=[rDb, self.rsel, racc], writes=[racc])
        P.op("dve", RECIP(self.rec[0:M, :], acc[0:M, :, 64]), reads=[racc], writes=[self.robf])
        P.op("dve", TT(self.obf[0:M, :].rearrange("p (h d) -> p h d", h=8), acc[0:M, :, 0:64],
                       self.rec[0:M, :].unsqueeze(2).to_broadcast([M, 8, 64]), ALU.mult), reads=[racc, self.robf], writes=[self.robf])
        ps, rp = self.ps[7], self.rps[7]
        psb = ps[:, 0:256].bitcast(BF16)
        P.op("pe", [TR(psb[:, pr * 128:pr * 128 + M], self.obf[0:M, pr * 128:(pr + 1) * 128], self.identb[0:M, 0:M]) for pr in range(4)],
             reads=[self.robf, self.rC], writes=[rp])
        P.op("act", ACP(CAT[:, 0:4, c0:c0 + M], psb.rearrange("p (k t) -> p k t", k=4)[:, :, 0:M]), reads=[rp], writes=[rcat])

    def odd_layer(self, l):
        P, A, I, O = self.P, self.A, self.I, self.O
        j = l // 2
        self.rmsnorm_all(self.nmix[:, l, :])
        Win = I["w_in_odd"][j].rearrange("(c p) f -> p c f", p=128)
        m0 = A.mark()
        WS = [A.bf16(8, 512) for _ in range(2)]; rWS = [Res(), Res()]
        U = A.bf16(8, NT); rU = [Res() for _ in range(17)]
        lng = A.f32(1024); lnb = A.f32(1024); rln = Res()
        self.load(lng, I["ln_g"][j:j + 1, :].partition_broadcast(128), rln)
        self.load(lnb, I["ln_b"][j:j + 1, :].partition_broadcast(128), rln)
        wtf = A.f32(8, 128); WTm = A.bf16(8, 128); rwt = Res()
        self.load(wtf, I["sgu_wT"][j].rearrange("g s t -> s g t"), rwt)
        P.op("dve", TT(WTm, wtf, self.trif.unsqueeze(1).to_broadcast([128, 8, 128]), ALU.mult), reads=[rwt, self.rC], writes=[rwt])
        sbb = A.bf16(8, 128); rsbb = Res()
        self.wload(sbb[0:1], I["sgu_b"][j:j + 1].rearrange("o g t -> o g t"), rsbb)
        wsm = A.f32(8, 4, 4); bsm = A.f32(8, 4); rsm = Res()
        for g_ in range(8):
            self.load(wsm[:, g_, :, :], I["sgu_w"][j][g_, 0:4, 0:4].partition_broadcast(128), rsm)
        self.load(bsm, I["sgu_b"][j][:, 0:4].partition_broadcast(128), rsm)
        self._bank = 0
        for ph in range(2):
            self.wload(WS[ph], Win[:, :, ph * 512:(ph + 1) * 512], rWS[ph])
        for ph in range(2):
            for tb, (c0, n) in enumerate(TB):
                for fl in range(4):
                    fc = ph * 4 + fl
                    ps, rp = self.bank()
                    P.op("pe", [MM(ps[:, 0:n], WS[ph][:, c, fl * 128:(fl + 1) * 128], self.XN[:, c, c0:c0 + n], start=(c == 0), stop=(c == 7)) for c in range(8)],
                         reads=[rWS[ph], self.rXN[tb]], writes=[rp])
                    P.op("act", ACT(U[:, fc, c0:c0 + n], ps[:, 0:n], AF.Gelu_apprx_tanh), reads=[rp],
                         writes=[rU[t] for t in range(17) if min(TILES[t][0] // 512, 4) == tb])
        for ph in range(2):
            self.wload(WS[ph], Win[:, :, 1024 + ph * 512:1024 + (ph + 1) * 512], rWS[ph])
        vg = [A.f32(1024) for _ in range(2)]; rvg = [Res(), Res()]
        vln = [A.f32(1024) for _ in range(2)]; rvln = [Res(), Res()]
        vlb = [A.bf16(1024) for _ in range(2)]; rvlb = [Res(), Res()]
        st1 = [A.f32(4) for _ in range(2)]; rst = [Res(), Res()]
        vT = A.f32(8, 64); mixT = A.f32(8, 64); rvT = Res(); rmix = Res()
        for ti, (c0, r) in enumerate(TILES):
            b = ti % 2
            tbi = min(c0 // 512, 4)
            for ph in range(2):
                ps, rp = self.bank()
                P.op("pe", [MM(ps[0:r, :], self.XN[:, c, c0:c0 + r], WS[ph][:, c, :], start=(c == 0), stop=(c == 7)) for c in range(8)],
                     reads=[rWS[ph], self.rXN[tbi]], writes=[rp])
                P.op("act", ACT(vg[b][0:r, ph * 512:(ph + 1) * 512], ps[0:r, :], AF.Gelu_apprx_tanh), reads=[rp], writes=[rvg[b]])
            s1 = st1[b]
            P.op("dve", RED(s1[0:r, 0:1], vg[b][0:r, :], AX.X, ALU.add), reads=[rvg[b]], writes=[rst[b]])
            P.op("dve", TS(s1[0:r, 1:2], s1[0:r, 0:1], 1.0 / 1024, None, ALU.mult), reads=[rst[b]], writes=[rst[b]])
            P.op("dve", TS(vg[b][0:r, :], vg[b][0:r, :], s1[0:r, 1:2], None, ALU.subtract), reads=[rst[b], rvg[b]], writes=[rvg[b]])
            P.op("act", ACT(vln[b][0:r, :], vg[b][0:r, :], AF.Square, accum_out=s1[0:r, 2:3]), reads=[rvg[b], rst[b]], writes=[rvln[b], rst[b]])
            P.op("act", ACT(s1[0:r, 3:4], s1[0:r, 2:3], AF.Sqrt, bias=1e-5, scale=1.0 / 1024), reads=[rst[b]], writes=[rst[b]])
            P.op("dve", RECIP(s1[0:r, 3:4], s1[0:r, 3:4]), reads=[rst[b]], writes=[rst[b]])
            P.op("dve", STT(vln[b][0:r, :], vg[b][0:r, :], s1[0:r, 3:4], lng[0:r, :], ALU.mult, ALU.mult), reads=[rvg[b], rst[b], rln], writes=[rvln[b]])
            P.op("pool", TT(vln[b][0:r, :], vln[b][0:r, :], lnb[0:r, :], ALU.add), reads=[rvln[b], rln], writes=[rvln[b]])
            if ti == 16:
                self.store(O["sgv"][j], vln[b][0:64, :], rvln[b])
                for half in range(2):
                    ps, rp = self.bank()
                    P.op("pe", [TR(ps[:, k * 64:(k + 1) * 64], vln[b][0:64, (half * 4 + k) * 128:(half * 4 + k + 1) * 128], self.identf[0:64, 0:64]) for k in range(4)],
                         reads=[rvln[b], self.rC], writes=[rp])
                    P.op("act", ACP(vT[:, half * 4:half * 4 + 4, :], ps[:, 0:256].rearrange("p (k t) -> p k t", k=4)), reads=[rp], writes=[rvT])
                v4 = vT[:, :, :].rearrange("p g (s t) -> p g s t", t=4)
                m4 = mixT[:, :, :].rearrange("p g (s t) -> p g s t", t=4)
                for g in range(8):
                    for t in range(4):
                        P.op("dve", TS(m4[:, g, :, t], v4[:, g, :, 0], wsm[:, g, t, 0:1], bsm[:, g, t:t + 1], ALU.mult, ALU.add), reads=[rvT, rsm], writes=[rmix])
                        for s_ in range(1, t + 1):
                            P.op("dve", STT(m4[:, g, :, t], v4[:, g, :, s_], wsm[:, g, t, s_:s_ + 1], m4[:, g, :, t], ALU.mult, ALU.add), reads=[rvT, rsm, rmix], writes=[rmix])
                P.op("dve", TT(U[:, :, 2048:2112], U[:, :, 2048:2112], mixT[:, :, :], ALU.mult), reads=[rmix, rU[16]], writes=[rU[16]])
                continue
            P.op("act", ACP(vlb[b][:, :], vln[b][:, :]), reads=[rvln[b]], writes=[rvlb[b]])
            for half in range(2):
                ps, rp = self.bank()
                mms = []
                for k in range(4):
                    g = half * 4 + k
                    mms.append(MM(ps[:, k * 128:(k + 1) * 128], vlb[b][:, g * 128:(g + 1) * 128], WTm[:, g, :], start=True, stop=False))
                    mms.append(MM(ps[:, k * 128:(k + 1) * 128], self.onesb[0:1, :], sbb[0:1, g, :], start=False, stop=True))
                P.op("pe", mms, reads=[rvlb[b], rwt, rsbb, self.rC], writes=[rp])
                P.op("dve", TT(U[:, half * 4:half * 4 + 4, c0:c0 + 128], U[:, half * 4:half * 4 + 4, c0:c0 + 128],
                               ps[:, :].rearrange("p (k t) -> p k t", k=4), ALU.mult), reads=[rp, rU[ti]], writes=[rU[ti]])
        Wo = I["w_o_odd"][j].rearrange("(c p) f -> p c f", p=128)
        self.out_proj(Wo, WS, rWS, U, rU)
        A.release(m0)
        P.barrier()

    def ffn(self, l):
        P, A, I = self.P, self.A, self.I
        self.rmsnorm_all(self.nffn[:, l, :])
        m0 = A.mark()
        WU = [A.bf16(8, 512) for _ in range(2)]; WD = [A.bf16(4, 1024) for _ in range(2)]
        rW = [Res(), Res()]
        H = [A.bf16(4, 512) for _ in range(2)]; rH = [Res(), Res()]
        tmp = [A.f32(512) for _ in range(2)]; rtmp = [Res(), Res()]
        Wup = I["w_up"][l].rearrange("(c p) f -> p c f", p=128)
        Wdn = I["w_dn"][l].rearrange("(c p) f -> p c f", p=128)

        def wl(fg):
            s = fg % 2
            P.op("pool", [DMA(WU[s], Wup[:, :, fg * 512:(fg + 1) * 512]), DMA(WD[s], Wdn[:, fg * 4:(fg + 1) * 4, :])], writes=[rW[s]], dma=rW[s])
        wl(0)
        self._bank = 0
        kk = 0
        hh = 0
        for fg in range(8):
            if fg + 1 < 8:
                wl(fg + 1)
            s = fg % 2
            for tb, (c0, n) in enumerate(TB):
                hb = hh % 2
                hh += 1
                for fc in range(4):
                    ps, rp = self.bank()
                    P.op("pe", [MM(ps[:, 0:n], WU[s][:, c, fc * 128:(fc + 1) * 128], self.XN[:, c, c0:c0 + n], start=(c == 0), stop=(c == 7)) for c in range(8)],
                         reads=[rW[s], self.rXN[tb]], writes=[rp])
                    tb_ = kk % 2
                    kk += 1
                    P.op("act", ACT(tmp[tb_][:, 0:n], ps[:, 0:n], AF.Relu), reads=[rp], writes=[rtmp[tb_]])
                    P.op("pool", TT(H[hb][:, fc, 0:n], tmp[tb_][:, 0:n], tmp[tb_][:, 0:n], ALU.mult), reads=[rtmp[tb_]], writes=[rH[hb]])
                for dc in range(8):
                    ps, rp = self.bank()
                    P.op("pe", [MM(ps[:, 0:n], WD[s][:, fc, dc * 128:(dc + 1) * 128], H[hb][:, fc, 0:n], start=(fc == 0), stop=(fc == 3)) for fc in range(4)],
                         reads=[rW[s], rH[hb]], writes=[rp])
                    P.op("dve", TT(self.RES[:, dc, c0:c0 + n], self.RES[:, dc, c0:c0 + n], ps[:, 0:n], ALU.add), reads=[rp, self.rRES[tb]], writes=[self.rRES[tb]])
        A.release(m0)
        P.barrier()

    def final(self):
        P, A, O = self.P, self.A, self.O
        m0 = A.mark()
        yt = [A.f32(1024) for _ in range(2)]; ryt = [Res(), Res()]
        yo = [A.f32(1024) for _ in range(2)]; ryo = [Res(), Res()]
        junk = A.f32(1024); rjunk = Res()
        st_ = [A.f32(2) for _ in range(2)]; rst = [Res(), Res()]
        self._bank = 0
        for ti, (c0, r) in enumerate(TILES):
            b = ti % 2
            tbi = min(c0 // 512, 4)
            for half in range(2):
                ps, rp = self.bank()
                P.op("pe", [TR(ps[0:r, k * 128:(k + 1) * 128], self.RES[:, half * 4 + k, c0:c0 + r], self.identf[:, :]) for k in range(4)],
                     reads=[self.rRES[tbi], self.rC], writes=[rp])
                P.op("act" if half == 0 else "dve", (ACP if half == 0 else CP)(yt[b][0:r, half * 512:(half + 1) * 512], ps[0:r, :]), reads=[rp], writes=[ryt[b]])
            P.op("act", ACT(junk[0:r, :], yt[b][0:r, :], AF.Square, accum_out=st_[b][0:r, 0:1]), reads=[ryt[b]], writes=[rjunk, rst[b]])
            P.op("act", ACT(st_[b][0:r, 1:2], st_[b][0:r, 0:1], AF.Sqrt, bias=1e-6, scale=1.0 / 1024), reads=[rst[b]], writes=[rst[b]])
            P.op("dve", RECIP(st_[b][0:r, 1:2], st_[b][0:r, 1:2]), reads=[rst[b]], writes=[rst[b]])
            P.op("dve", STT(yo[b][0:r, :], yt[b][0:r, :], st_[b][0:r, 1:2], self.nfin[0:r, :], ALU.mult, ALU.mult), reads=[ryt[b], rst[b], self.rC], writes=[ryo[b]])
            dst = O["yp"][c0:c0 + r, :] if ti < 16 else O["ys"]
            self.store(dst, yo[b][0:r, :], ryo[b])
        A.release(m0)


def _host_consts():
    pos = np.zeros((128, 17), np.float32)
    for i in range(16):
        pos[:, i] = i * 128 + np.arange(128)
    pos[:, 16] = 2048 + (np.arange(128) % 4)
    inv = (10000.0 ** (-np.arange(0, 64, 2, dtype=np.float32) / 64)).astype(np.float32)
    ang = pos[:, :, None].astype(np.float32) * inv[None, None, :]
    cosT = np.cos(ang).astype(np.float32)
    sinT = np.sin(ang).astype(np.float32)
    k = np.arange(128)
    tri = (k[:, None] <= k[None, :]).astype(np.float32)
    kk = np.arange(64)
    smask = np.zeros((64, 64), np.float32)
    for s in range(16):
        for q in range(4):
            smask[:, 4 * s + q] = ((kk // 4 == s) & (kk % 4 <= q)).astype(np.float32)
    cand = np.zeros((128, 4, 64), np.float32)
    for a, nq in enumerate([4, 5, 6, 7]):
        c = np.where(np.arange(8) < nq, 0.0, -1e30).astype(np.float32)
        cand[:, a, :] = np.tile(c, 8)[None, :]
    invcnt = np.zeros((128, 4, 16), np.float32)
    for g, w in enumerate([2, 4, 8, 16]):
        invcnt[:, g, :] = (1.0 / np.minimum(np.arange(16) + 1, w)).astype(np.float32)[None, :]
    return dict(cosT=cosT, sinT=sinT, tri=tri, smask=smask, cand=cand, invcnt=invcnt,
                iota=np.arange(128, dtype=np.float32)[:, None], ident=np.eye(128, dtype=np.float32),
                ones=np.ones((128, 128), np.float32))


def _core_inputs(c, inp, ck, cv, ptab_c, consts):
    f = np.ascontiguousarray
    d = dict(consts)
    d["xp"] = f(inp["x_prompt"][c])
    d["xs"] = f(inp["x_sample"][16 * c:16 * c + 16].reshape(64, 1024))
    d["ck0"], d["ck1"], d["cv0"], d["cv1"] = ck[0], ck[1], cv[0], cv[1]
    d["spool"] = f(inp["state_pool"][:, 16 * c:16 * c + 16].reshape(2, 240, 512))
    d["ptab"] = f(ptab_c.reshape(1, 256).astype(np.int32))
    d["nmix"] = f(inp["norm_mix"].reshape(4, 8, 128).transpose(2, 0, 1))
    d["nffn"] = f(inp["norm_ffn"].reshape(4, 8, 128).transpose(2, 0, 1))
    d["nfin"] = f(inp["norm_final"].reshape(1, 1024))
    d["pscale"] = f(inp["pool_scale"].reshape(2, 4, 128).transpose(2, 0, 1))
    for a, b in [("w_in_even", "w_in_even"), ("w_o_even", "w_o_even"), ("w_pool", "w_pool"), ("w_in_odd", "w_in_odd"),
                 ("ln_g", "sgu_ln_g"), ("ln_b", "sgu_ln_b"), ("sgu_w", "sgu_w"), ("sgu_b", "sgu_b"), ("w_o_odd", "w_o_odd"),
                 ("w_up", "w_ffn_up"), ("w_dn", "w_ffn_down")]:
        d[a] = inp[b]
    d["sgu_wT"] = f(inp["sgu_w"].transpose(0, 1, 3, 2))
    return d


_NC_CACHE = {}


def _get_nc(npg, **kw):
    key = (npg, tuple(sorted(kw.items())))
    if key not in _NC_CACHE:
        _NC_CACHE[key] = Builder(npg, **kw).build()
    return _NC_CACHE[key]


def _assemble(results, cores):
    n = len(cores)
    yp = np.stack([r["yp"] for r in results])
    ys = np.concatenate([r["ys"].reshape(16, 4, 1024) for r in results])
    pk = np.stack([r["pk"].reshape(2, 2048, 8, 64) for r in results], axis=1)
    pv = np.stack([r["pv"].reshape(2, 2048, 8, 64) for r in results], axis=1)
    sk = np.concatenate([r["sk"].reshape(2, 16, 4, 8, 64) for r in results], axis=1)
    sv = np.concatenate([r["sv"].reshape(2, 16, 4, 8, 64) for r in results], axis=1)
    pps = np.stack([r["pps"] for r in results], axis=1)
    sps = np.concatenate([r["sps"] for r in results], axis=1)
    sgv = np.concatenate([r["sgv"].reshape(2, 16, 4, 1024) for r in results], axis=1)
    return (yp, ys, pk, pv, sk, sv, pps, sps, sgv)


def kernel(**inp):
    inp = {k: np.asarray(v) for k, v in inp.items()}
    consts = _host_consts()
    npg = inp["cache_k"].shape[1]
    ck = [np.ascontiguousarray(inp["cache_k"][i].reshape(npg * 128, 512)) for i in range(2)]
    cv = [np.ascontiguousarray(inp["cache_v"][i].reshape(npg * 128, 512)) for i in range(2)]
    nc = _get_nc(npg)
    in_maps = [_core_inputs(c, inp, ck, cv, inp["page_table"][16 * c:16 * c + 16], consts) for c in range(8)]
    res = run_bass_kernel_spmd(nc, in_maps, core_ids=list(range(8)))
    return _assemble(res.results, list(range(8)))
```

```python
import numpy as np
import concourse.bass as bass
import concourse.mybir as mybir
from concourse.bass_utils import run_bass_kernel_spmd
from contextlib import ExitStack

F32 = mybir.dt.float32
BF16 = mybir.dt.bfloat16
I32 = mybir.dt.int32
ALU = mybir.AluOpType
AF = mybir.ActivationFunctionType
AX = mybir.AxisListType

NT = 2112
TB = [(0, 512), (512, 512), (1024, 512), (1536, 512), (2048, 64)]
TILES = [(i * 128, 128) for i in range(16)] + [(2048, 64)]
NEG = -30000.0
import os
DBG = {k: 1 for k in os.environ.get('KDBG', '').split(',') if k}


class Res:
    __slots__ = ("name", "wr", "acc", "sem", "cnt", "excl")

    def __init__(self, name="", excl=False):
        self.name = name
        self.excl = excl
        self.wr = {}
        self.acc = {}
        self.sem = None
        self.cnt = 0


class Op:
    __slots__ = ("eng", "fns", "deps", "seq", "tkey", "order", "val", "target", "dma", "cclock", "waits")


class Prog:
    ENGS = ["pe", "act", "dve", "pool", "sp"]

    def __init__(self, nc):
        self.nc = nc
        self.ops = []
        self.per_eng = {e: [] for e in self.ENGS}
        self.stores = []
        self.last_by_track = {}
        self.pending = {e: None for e in self.ENGS}

    def barrier(self):
        snap = list(self.last_by_track.values())
        for e in self.ENGS:
            self.pending[e] = snap

    def op(self, eng, fns, reads=(), writes=(), dma=None, store=False):
        o = Op()
        o.eng = eng
        o.fns = list(fns) if isinstance(fns, (list, tuple)) else [fns]
        deps = []
        if self.pending[eng] is not None:
            deps.extend(self.pending[eng])
            self.pending[eng] = None
        reads = list(reads)
        writes = list(writes)
        for r in reads:
            if r.excl and r not in writes:
                writes.append(r)
        for r in reads:
            deps.extend(r.wr.values())
        for w in writes:
            deps.extend(w.acc.values())
        o.deps = deps
        o.dma = dma
        o.target = False
        o.waits = []
        o.seq = len(self.per_eng[eng])
        if dma is not None:
            dma.cnt += 16 * len(o.fns)
            o.tkey = ("dma", id(dma))
            o.order = dma.cnt
            o.val = dma.cnt
        else:
            o.tkey = eng
            o.order = o.seq
            o.val = None
        for r in reads:
            r.acc[o.tkey] = o
        for w in writes:
            w.wr[o.tkey] = o
            w.acc[o.tkey] = o
        self.per_eng[eng].append(o)
        self.ops.append(o)
        self.last_by_track[o.tkey] = o
        if store:
            self.stores.append(o)
        return o

    def finalize(self):
        fin = Op()
        fin.eng = "sp"; fin.fns = []; fin.deps = list(self.stores); fin.dma = None
        fin.target = False; fin.waits = []; fin.seq = len(self.per_eng["sp"])
        fin.tkey = "sp"; fin.order = fin.seq; fin.val = None
        self.per_eng["sp"].append(fin)
        self.ops.append(fin)
        clock = {e: {} for e in self.ENGS}
        for o in self.ops:
            ck = clock[o.eng]
            best = {}
            for d in o.deps:
                if ck.get(d.tkey, -1) >= d.order:
                    continue
                b = best.get(d.tkey)
                if b is None or b.order < d.order:
                    best[d.tkey] = d
            for d in best.values():
                d.target = True
                for k, v in d.cclock.items():
                    if ck.get(k, -1) < v:
                        ck[k] = v
            o.waits = list(best.values())
            cc = dict(ck)
            cc[o.tkey] = o.order
            o.cclock = cc
            o.deps = None
        for e in self.ENGS:
            c = 0
            for o in self.per_eng[e]:
                if o.dma is None:
                    if o.target:
                        c += 1
                    o.val = c
        for o in self.ops:
            o.cclock = None

    def emit(self):
        nc = self.nc
        with ExitStack() as st:
            esem = {e: st.enter_context(nc.semaphore("s_" + e)) for e in ["pe", "act", "dve", "pool", "sp"]}
            n = 0
            for o in self.ops:
                if o.dma is not None and o.dma.sem is None:
                    o.dma.sem = st.enter_context(nc.semaphore("d%d" % n))
                    n += 1
            block = st.enter_context(nc.Block())

            def run(eng_name):
                def body(eng):
                    for o in self.per_eng[eng_name]:
                        for d in o.waits:
                            sem = d.dma.sem if d.dma is not None else esem[d.eng]
                            eng.wait_ge(sem, d.val)
                        last = None
                        for fn in o.fns:
                            last = fn(eng)
                            if o.dma is not None:
                                last.then_inc(o.dma.sem, 16)
                        if o.dma is None and o.target:
                            last.then_inc(esem[eng_name], 1)
                return body

            block.tensor(run("pe"))
            block.scalar(run("act"))
            block.vector(run("dve"))
            block.gpsimd(run("pool"))
            block.sync(run("sp"))


def MM(out, lhsT, rhs, start=True, stop=True):
    return lambda e: e.matmul(out, lhsT=lhsT, rhs=rhs, start=start, stop=stop)


def TR(out, in_, ident):
    return lambda e: e.transpose(out, in_, ident)


def ACT(out, in_, func, **kw):
    return lambda e: e.activation(out=out, in_=in_, func=func, **kw)


def TT(out, in0, in1, op):
    return lambda e: e.tensor_tensor(out=out, in0=in0, in1=in1, op=op)


def TS(out, in0, s1, s2, op0, op1=None):
    if op1 is None:
        return lambda e: e.tensor_scalar(out=out, in0=in0, scalar1=s1, scalar2=None, op0=op0)
    return lambda e: e.tensor_scalar(out=out, in0=in0, scalar1=s1, scalar2=s2, op0=op0, op1=op1)


def STT(out, in0, scalar, in1, op0, op1):
    return lambda e: e.scalar_tensor_tensor(out=out, in0=in0, scalar=scalar, in1=in1, op0=op0, op1=op1)


def CP(out, in_):
    return lambda e: e.tensor_copy(out=out, in_=in_)


def ACP(out, in_):
    return lambda e: e.copy(out=out, in_=in_)


def DMA(out, in_):
    return lambda e: e.dma_start(out=out, in_=in_)


def IDMA(out, in_, idx):
    return lambda e: e.indirect_dma_start(out=out, out_offset=None, in_=in_,
                                          in_offset=bass.IndirectOffsetOnAxis(ap=idx, axis=0))


def RED(out, in_, axis, op):
    return lambda e: e.tensor_reduce(out=out, in_=in_, axis=axis, op=op)


def RECIP(out, in_):
    return lambda e: e.reciprocal(out=out, in_=in_)


def MAX8(out, in_):
    return lambda e: e.max(out=out, in_=in_)


def MEMSET(ap, v):
    return lambda e: e.memset(ap, v)


def _reshape(v, shape):
    if len(shape) == 1:
        return v
    names = ["a", "b", "c", "d"][:len(shape)]
    kw = {names[k]: shape[k] for k in range(1, len(shape))}
    return v.rearrange("p (%s) -> p %s" % (" ".join(names), " ".join(names)), **kw)


class Arena:
    def __init__(self, ap, nwords):
        self.ap = ap
        self.n = nwords
        self.top = 0
        self.peak = 0

    def mark(self):
        return self.top

    def release(self, m):
        self.top = m

    def _take(self, w):
        w = (w + 7) // 8 * 8
        off = self.top
        self.top += w
        self.peak = max(self.peak, self.top)
        assert self.top <= self.n, "arena overflow %d > %d" % (self.top, self.n)
        return off, w

    def f32(self, *shape):
        n = int(np.prod(shape))
        off, w = self._take(n)
        return _reshape(self.ap[:, off:off + n], shape)

    def i32(self, *shape):
        n = int(np.prod(shape))
        off, w = self._take(n)
        return _reshape(self.ap[:, off:off + n].bitcast(I32), shape)

    def bf16(self, *shape):
        n = int(np.prod(shape))
        off, w = self._take((n + 1) // 2)
        return _reshape(self.ap[:, off:off + w].bitcast(BF16)[:, 0:n], shape)


class StopBuild(Exception):
    pass


class Builder:
    def stage(self, name):
        if self.stop_at == name:
            raise StopBuild()

    def __init__(self, npg, n_layers=4, do_sample_attn=True, stop_at=None):
        self.stop_at = stop_at
        self.npg = npg
        self.n_layers = n_layers
        self.do_sample_attn = do_sample_attn
        self.nc = bass.Bass("TRN2", target_bir_lowering=False)

    def dram(self):
        nc = self.nc
        I = {}
        O = {}

        def inp(name, shape, dt=F32):
            I[name] = nc.dram_tensor(name, list(shape), dt, kind="ExternalInput").ap()

        def out(name, shape):
            O[name] = nc.dram_tensor(name, list(shape), F32, kind="ExternalOutput").ap()

        inp("xp", [2048, 1024]); inp("xs", [64, 1024])
        for nm in ["ck0", "ck1", "cv0", "cv1"]:
            inp(nm, [self.npg * 128, 512])
        inp("spool", [2, 240, 512]); inp("ptab", [1, 256], I32)
        inp("nmix", [128, 4, 8]); inp("nffn", [128, 4, 8]); inp("nfin", [1, 1024]); inp("pscale", [128, 2, 4])
        inp("w_in_even", [2, 1024, 2048]); inp("w_o_even", [2, 1024, 1024]); inp("w_pool", [2, 4, 128, 128])
        inp("w_in_odd", [2, 1024, 2048]); inp("ln_g", [2, 1024]); inp("ln_b", [2, 1024])
        inp("sgu_wT", [2, 8, 128, 128]); inp("sgu_w", [2, 8, 128, 128]); inp("sgu_b", [2, 8, 128]); inp("w_o_odd", [2, 1024, 1024])
        inp("w_up", [4, 1024, 4096]); inp("w_dn", [4, 4096, 1024])
        inp("cosT", [128, 17, 32]); inp("sinT", [128, 17, 32]); inp("tri", [128, 128]); inp("smask", [64, 64])
        inp("cand", [128, 4, 64]); inp("invcnt", [128, 4, 16]); inp("iota", [128, 1]); inp("iota2", [128, 1]); inp("ident", [128, 128])
        inp("ones", [128, 128])
        out("yp", [2048, 1024]); out("ys", [64, 1024])
        out("pk", [2, 2048, 512]); out("pv", [2, 2048, 512]); out("sk", [2, 64, 512]); out("sv", [2, 64, 512])
        out("pps", [2, 15, 512]); out("sps", [2, 16, 15, 512]); out("sgv", [2, 64, 1024])
        self.I, self.O = I, O

    def bank(self):
        b = self._bank
        self._bank = (self._bank + 1) % 8
        return self.ps[b], self.rps[b]

    def load(self, out_ap, in_ap, res, eng="sp"):
        self.P.op(eng, DMA(out_ap, in_ap), writes=[res], dma=res)

    def store(self, out_ap, in_ap, res):
        self.P.op("sp", DMA(out_ap, in_ap), reads=[res], dma=res, store=True)

    def build(self):
        self.dram()
        nc = self.nc
        with ExitStack() as st:
            self.st = st
            AW = 32500
            RES = st.enter_context(nc.sbuf_tensor("RES", [128, 8, NT], F32))
            CW = 3776
            cst = st.enter_context(nc.sbuf_tensor("CST", [128, CW], F32))
            arena_t = st.enter_context(nc.sbuf_tensor("ARENA", [128, AW], F32))
            self.ps = []
            for b in range(8):
                t = st.enter_context(nc.psum_tensor("ps%d" % b, [128, 512], F32))
                self.ps.append(t)
            self.P = Prog(nc)
            P = self.P
            self.rps = [Res("ps%d" % b, excl=True) for b in range(8)]
            self._bank = 0
            self.RES = RES
            self.rRES = [Res("RES%d" % i) for i in range(5)]
            self.A = Arena(arena_t[:, :], AW)
            self.C = Arena(cst[:, :], CW)
            self.consts()
            self.XN = self.A.bf16(8, NT)
            self.rXN = [Res("XN%d" % i) for i in range(5)]
            try:
                self.stage("consts")
                self.load_x()
                self.stage("loadx")
                for l in range(self.n_layers):
                    if l % 2 == 0:
                        self.even_layer(l)
                    else:
                        self.odd_layer(l)
                    self.stage("mixer%d" % l)
                    self.ffn(l)
                    self.stage("ffn%d" % l)
                self.final()
            except StopBuild:
                pass
            P.finalize()
            P.emit()
        return nc

    def consts(self):
        P, C, I = self.P, self.C, self.I
        self.rC = Res("consts")
        rC = self.rC

        def ld(name, shape, src=None):
            t = C.f32(*shape)
            self.load(t, I[name] if src is None else src, rC)
            return t

        def ldb(name, shape, src=None, rows=128):
            t = C.bf16(*shape)
            self.load(t[0:rows], I[name] if src is None else src, rC, eng="pool")
            return t

        self.cosT = ld("cosT", [17, 32]); self.sinT = ld("sinT", [17, 32])
        self.trif = ld("tri", [128]); self.trib = ldb("tri", [128])
        self.smaskb = ldb("smask", [64], rows=64)
        self.cand = ld("cand", [4, 64]); self.invcnt = ld("invcnt", [4, 16]); self.iota = ld("iota", [1])
        self.identf = ld("ident", [128]); self.identb = ldb("ident", [128])
        self.onesb = ldb("ones", [128])
        self.nmix = ld("nmix", [4, 8]); self.nffn = ld("nffn", [4, 8]); self.pscale = ld("pscale", [2, 4])
        self.nfin = ld("nfin", [1024], src=I["nfin"].partition_broadcast(128))
        ptb = C.i32(256)
        self.load(ptb, I["ptab"].partition_broadcast(128), rC)
        io2 = ld("iota2", [1])
        pf0 = C.f32(256)
        pf2 = C.f32(128)
        self.pidx2 = C.i32(128)
        P.op("dve", CP(pf0, ptb), reads=[rC], writes=[rC])
        pv_ = pf0.rearrange("p (m j) -> p m j", j=2)
        P.op("dve", TS(pf2[0:64, :], pv_[0:64, :, 0], 64.0, io2[0:64, 0:1], ALU.mult, ALU.add), reads=[rC], writes=[rC])
        P.op("dve", TS(pf2[64:128, :], pv_[64:128, :, 1], 64.0, io2[64:128, 0:1], ALU.mult, ALU.add), reads=[rC], writes=[rC])
        P.op("dve", CP(self.pidx2, pf2), reads=[rC], writes=[rC])

    def load_x(self):
        P, A = self.P, self.A
        m = A.mark()
        xt = [A.f32(1024) for _ in range(2)]
        rxt = [Res(), Res()]
        for ti, (c0, r) in enumerate(TILES):
            b = ti % 2
            src = self.I["xp"][c0:c0 + r, :] if ti < 16 else self.I["xs"]
            self.load(xt[b][0:r], src, rxt[b])
            for half in range(2):
                ps, rp = self.bank()
                P.op("pe", [TR(ps[:, k * 128:k * 128 + r], xt[b][0:r, (half * 4 + k) * 128:(half * 4 + k + 1) * 128], self.identf[0:r, 0:r])
                            for k in range(4)], reads=[rxt[b], self.rC], writes=[rp])
                src_v = ps[:, :].rearrange("p (k t) -> p k t", k=4)[:, :, 0:r]
                P.op("act" if half == 0 else "dve",
                     (ACP if half == 0 else CP)(self.RES[:, half * 4:half * 4 + 4, c0:c0 + r], src_v),
                     reads=[rp], writes=[self.rRES[min(c0 // 512, 4)]])
        A.release(m)
        P.barrier()

    def rmsnorm_all(self, gains):
        P, A = self.P, self.A
        m = A.mark()
        sq = [A.bf16(512) for _ in range(2)]
        rsq = [Res(), Res()]
        rs = [A.f32(512) for _ in range(2)]
        rrs = [Res(), Res()]
        k = 0
        for tb, (c0, n) in enumerate(TB):
            ps, rp = self.bank()
            for c in range(8):
                b = k % 2
                k += 1
                P.op("act", ACT(sq[b][:, 0:n], self.RES[:, c, c0:c0 + n], AF.Square), reads=[self.rRES[tb]], writes=[rsq[b]])
                P.op("pe", MM(ps[:, 0:n], self.onesb[:, :], sq[b][:, 0:n], start=(c == 0), stop=(c == 7)),
                     reads=[rsq[b], self.rC], writes=[rp])
            b2 = tb % 2
            P.op("act", ACT(rs[b2][:, 0:n], ps[:, 0:n], AF.Sqrt, bias=1e-6, scale=1.0 / 1024), reads=[rp], writes=[rrs[b2]])
            P.op("dve", RECIP(rs[b2][:, 0:n], rs[b2][:, 0:n]), reads=[rrs[b2]], writes=[rrs[b2]])
            for c in range(8):
                P.op("dve",
                     STT(self.XN[:, c, c0:c0 + n], self.RES[:, c, c0:c0 + n], gains[:, c:c + 1], rs[b2][:, 0:n], ALU.mult, ALU.mult),
                     reads=[self.rRES[tb], rrs[b2], self.rC], writes=[self.rXN[tb]])
        A.release(m)
        P.barrier()

    def wload(self, dst, src, res):
        self.P.op("pool", DMA(dst, src), writes=[res], dma=res)

    def even_layer(self, l):
        P, A, I, O = self.P, self.A, self.I, self.O
        i = l // 2
        self.rmsnorm_all(self.nmix[:, l, :])
        self.stage("norm")
        Win = I["w_in_even"][i].rearrange("(c p) f -> p c f", p=128)
        m0 = A.mark()
        QsT = A.bf16(4, 64); KsT = A.bf16(4, 64); Vs = A.bf16(512)
        self.attn_setup()
        kmf = A.f32(4, 8); kmT = A.bf16(4, 8); rkm = Res()
        wp = A.bf16(4, 128)
        rwp = Res()
        mP = A.mark()
        WS = [A.bf16(8, 512)]
        WS.append(WS[0])
        rWS = [Res()]
        rWS.append(rWS[0])
        POOLED = A.bf16(4, NT)
        rPOOLED = [Res() for _ in range(4)]
        m1 = A.mark()
        L = 16 + 2048
        PG = A.f32(L); Y = [A.f32(L), A.f32(L)]
        FULL = A.f32(16, 19); FY = [A.f32(16, 19), A.f32(16, 19)]
        hist = [A.f32(128), A.f32(128)]
        t16 = A.f32(16)
        pst = A.f32(512)
        nrow = A.f32(512)
        newc = A.f32(64); rnewc = Res()
        rPG, rY, rFULL, rFY, rhist, rt16, rpst, rnrow = Res(), [Res(), Res()], Res(), [Res(), Res()], [Res(), Res()], Res(), Res(), Res()
        P.op("pool", [MEMSET(PG[:, 0:16], 0.0), MEMSET(Y[0][:, 0:16], 0.0), MEMSET(Y[1][:, 0:16], 0.0)], writes=[rPG, rY[0], rY[1]])
        self.wload(WS[0], Win[:, :, 1536:2048], rWS[0])
        psS, rpsS = self.ps[7], self.rps[7]
        psN, rpsN = self.ps[6], self.rps[6]
        self._bank = 0
        wins = [2, 4, 8, 16]
        for g in range(4):
            w = wins[g]
            for tb, (c0, n) in enumerate(TB):
                ps, rp = self.ps[tb % 4], self.rps[tb % 4]
                P.op("pe", [MM(ps[:, 0:n], WS[0][:, c, g * 128:(g + 1) * 128], self.XN[:, c, c0:c0 + n], start=(c == 0), stop=(c == 7))
                            for c in range(8)], reads=[rWS[0], self.rXN[tb]], writes=[rp])
                if tb < 4:
                    P.op("act", ACP(PG[:, 16 + c0:16 + c0 + n], ps[:, 0:n]), reads=[rp], writes=[rPG])
                else:
                    P.op("act", ACP(FULL[:, :, 15:19], ps[:, 0:64].rearrange("p (s t) -> p s t", t=4)), reads=[rp], writes=[rFULL])
            for half in range(2):
                self.load(hist[half][0:120], I["spool"][i, half * 120:(half + 1) * 120, g * 128:(g + 1) * 128], rhist[half])
                ps, rp = self.ps[4 + half], self.rps[4 + half]
                P.op("pe", TR(ps[:, 0:120], hist[half][0:120, :], self.identf[0:120, 0:120]), reads=[rhist[half], self.rC], writes=[rp])
                P.op("dve", CP(FULL[:, half * 8:(half + 1) * 8, 0:15], ps[:, 0:120].rearrange("p (s t) -> p s t", t=15)), reads=[rp], writes=[rFULL])
            src, rsrc = PG, rPG
            s = 1
            k = 0
            while s < w:
                dst, rdst = Y[k % 2], rY[k % 2]
                P.op("dve", TT(dst[:, 16:L], src[:, 16:L], src[:, 16 - s:L - s], ALU.add), reads=[rsrc], writes=[rdst])
                src, rsrc = dst, rdst
                s *= 2
                k += 1
            P.op("dve", STT(POOLED[:, g, 0:2048], src[:, 16:L], 1.0 / w, PG[:, 16:L], ALU.mult, ALU.subtract), reads=[rsrc, rPG], writes=[rPOOLED[g]])
            P.op("dve", TT(t16, src[:, 16:32], self.invcnt[:, g, :], ALU.mult), reads=[rsrc, self.rC], writes=[rt16])
            P.op("dve", TT(POOLED[:, g, 0:16], t16, PG[:, 16:32], ALU.subtract), reads=[rt16, rPG], writes=[rPOOLED[g]])
            P.op("pe", TR(psS[0:16, g * 128:(g + 1) * 128], PG[:, L - 16:L], self.identf[:, :]), reads=[rPG, self.rC], writes=[rpsS])
            src, rsrc = FULL, rFULL
            s = 1
            k = 0
            lo = 0
            while s < w:
                dst, rdst = FY[k % 2], rFY[k % 2]
                lo2 = lo + s
                P.op("dve", TT(dst[:, :, lo2:19], src[:, :, lo2:19], src[:, :, lo2 - s:19 - s], ALU.add), reads=[rsrc], writes=[rdst])
                src, rsrc = dst, rdst
                lo = lo2
                s *= 2
                k += 1
            P.op("dve", STT(POOLED[:, g, 2048:2112].rearrange("p (s t) -> p s t", t=4), src[:, :, 15:19], 1.0 / w, FULL[:, :, 15:19],
                            ALU.mult, ALU.subtract), reads=[rsrc, rFULL], writes=[rPOOLED[g]])
            P.op("dve", CP(newc[:, :].rearrange("p (s t) -> p s t", t=4), FULL[:, :, 15:19]), reads=[rFULL], writes=[rnewc])
            P.op("pe", TR(psN[0:64, g * 128:(g + 1) * 128], newc[:, :], self.identf[:, :]), reads=[rnewc, self.rC], writes=[rpsN])
        P.op("act", ACP(pst[0:16, :], psS[0:16, :]), reads=[rpsS], writes=[rpst])
        self.store(O["pps"][i], pst[1:16, :], rpst)
        P.op("act", ACP(nrow[0:64, :], psN[0:64, :]), reads=[rpsN], writes=[rnrow])
        for s_ in range(16):
            self.store(O["sps"][i, s_, 11:15, :], nrow[4 * s_:4 * s_ + 4, :], rnrow)
        rd2d = Res()
        P.op("sp", DMA(O["sps"][i, :, 0:11, :], I["spool"][i].rearrange("(s r) f -> s r f", r=15)[:, 4:15, :]), dma=rd2d, store=True)
        A.release(m1)
        P.barrier()
        self.stage("pool")
        QT = A.bf16(4, 2048); KT = A.bf16(4, 2048); V = A.bf16(16, 512)
        rQT = [Res() for _ in range(17)]; rKT = [Res() for _ in range(17)]; rV = [Res() for _ in range(17)]
        raw = [A.f32(512)] * 2; rot = [A.f32(512)] * 2
        ta = [A.f32(256)] * 2; tb_ = [A.f32(256)] * 2
        xb = [A.bf16(512)] * 2
        rraw, rrot, rta, rtb, rxb = ([Res()] * 2 for _ in range(5))
        self._bank = 0
        for pi, pname in enumerate(["q", "k", "v"]):
            sl = (pi + 1) % 2
            self.wload(WS[sl], Win[:, :, pi * 512:(pi + 1) * 512], rWS[sl])
            for ti, (c0, r) in enumerate(TILES):
                tbi = min(c0 // 512, 4)
                b = ti % 2
                ps, rp = self.bank()
                P.op("pe", [MM(ps[0:r, :], self.XN[:, c, c0:c0 + r], WS[sl][:, c, :], start=(c == 0), stop=(c == 7)) for c in range(8)],
                     reads=[rWS[sl], self.rXN[tbi]], writes=[rp])
                P.op("act", ACP(raw[b][0:r, :], ps[0:r, :]), reads=[rp], writes=[rraw[b]])
                if pname == "v":
                    dst = O["pv"][i, c0:c0 + r, :] if ti < 16 else O["sv"][i]
                    self.store(dst, raw[b][0:r, :], rraw[b])
                    P.op("dve", CP(V[0:r, ti, :] if ti < 16 else Vs[0:64, :], raw[b][0:r, :]), reads=[rraw[b]], writes=[rV[ti]])
                    continue
                r3 = raw[b][0:r, :].rearrange("p (h d) -> p h d", h=8)
                o3 = rot[b][0:r, :].rearrange("p (h d) -> p h d", h=8)
                x1, x2, o1, o2 = r3[:, :, 0:32], r3[:, :, 32:64], o3[:, :, 0:32], o3[:, :, 32:64]
                cb = self.cosT[0:r, ti, :].unsqueeze(1).to_broadcast([r, 8, 32])
                sb_ = self.sinT[0:r, ti, :].unsqueeze(1).to_broadcast([r, 8, 32])
                ta3 = ta[b][0:r, :].rearrange("p (h d) -> p h d", h=8)
                tb3 = tb_[b][0:r, :].rearrange("p (h d) -> p h d", h=8)
                P.op("dve", TT(o1, x1, cb, ALU.mult), reads=[rraw[b], self.rC], writes=[rrot[b]])
                P.op("dve", TT(ta3, x2, sb_, ALU.mult), reads=[rraw[b], self.rC], writes=[rta[b]])
                P.op("dve", TT(o1, o1, ta3, ALU.subtract), reads=[rrot[b], rta[b]], writes=[rrot[b]])
                P.op("pool", TT(o2, x2, cb, ALU.mult), reads=[rraw[b], self.rC], writes=[rrot[b]])
                P.op("pool", TT(tb3, x1, sb_, ALU.mult), reads=[rraw[b], self.rC], writes=[rtb[b]])
                P.op("pool", TT(o2, o2, tb3, ALU.add), reads=[rrot[b], rtb[b]], writes=[rrot[b]])
                if pname == "k":
                    dst = O["pk"][i, c0:c0 + r, :] if ti < 16 else O["sk"][i]
                    self.store(dst, rot[b][0:r, :], rrot[b])
                P.op("act", ACP(xb[b][0:r, :], rot[b][0:r, :]), reads=[rrot[b]], writes=[rxb[b]])
                ps2, rp2 = self.bank()
                psb = ps2[:, 0:256].bitcast(BF16)
                P.op("pe", [TR(psb[:, pr * 128:pr * 128 + r], xb[b][0:r, pr * 128:(pr + 1) * 128], self.identb[0:r, 0:r]) for pr in range(4)],
                     reads=[rxb[b], self.rC], writes=[rp2])
                dstT = (QT if pname == "q" else KT)
                rdT = (rQT if pname == "q" else rKT)
                dst_ap = dstT[:, :, c0:c0 + r] if ti < 16 else (QsT if pname == "q" else KsT)[:, :, 0:64]
                P.op("dve", CP(dst_ap, psb.rearrange("p (k t) -> p k t", k=4)[:, :, 0:r]), reads=[rp2], writes=[rdT[ti]])
        P.barrier()
        self.stage("qkv")
        CAT = self.XN
        rCAT = [Res() for _ in range(17)]
        self.wload(wp, I["w_pool"][i].rearrange("g c d -> c g d"), rwp)
        self._bank = 0
        for g in range(4):
            for tb, (c0, n) in enumerate(TB):
                ps, rp = self.bank()
                P.op("pe", MM(ps[:, 0:n], wp[:, g, :], POOLED[:, g, c0:c0 + n]), reads=[rwp, rPOOLED[g]], writes=[rp])
                P.op("act", ACT(CAT[:, 4 + g, c0:c0 + n], ps[:, 0:n], AF.Copy, scale=self.pscale[:, i, g:g + 1]), reads=[rp, self.rC],
                     writes=[rCAT[t] for t in range(17) if min(TILES[t][0] // 512, 4) == tb])
        self.stage("wpool")
        P.op("dve", RED(kmf, KT[:, :, 0:2048].rearrange("p k (n t) -> p k n t", t=256), AX.X, ALU.add), reads=rKT[0:16], writes=[rkm])
        P.op("dve", TS(kmT, kmf, 1.0 / 256, None, ALU.mult), reads=[rkm], writes=[rkm])
        for qt in range(16):
            nq = qt // 2
            c0 = qt * 128
            sel = nq >= 4
            tiles = []
            for kt in range(qt + 1):
                own = kt >= 2 * nq
                tiles.append(dict(KT=(lambda h, kt=kt: KT[(h % 2) * 64:(h % 2) * 64 + 64, h // 2, kt * 128:(kt + 1) * 128]),
                                  V=(lambda h, kt=kt: V[:, kt, h * 64:(h + 1) * 64]), nk=128,
                                  slot=(kt // 2 if (sel and not own) else (7 if sel else 0)),
                                  mask=(self.trib[:, :] if kt == qt else None), res=[rKT[kt], rV[kt]]))
            self.attend(128, lambda h, c0=c0: QT[(h % 2) * 64:(h % 2) * 64 + 64, h // 2, c0:c0 + 128], [rQT[qt]], tiles,
                        sel, nq, kmT, rkm, self.cand[:, nq - 4, :] if sel else None, CAT, c0, rCAT[qt])
            self.stage("attn%d" % qt)
        self.stage("pattn")
        P.barrier()
        A.release(mP)
        if self.do_sample_attn:
            Kgf = [A.bf16(8192) for _ in range(2)]; rKg = [Res(), Res()]
            Kg = [k_.rearrange("p (j f) -> p j f", j=16) for k_ in Kgf]
            Vgf = [A.bf16(8192) for _ in range(2)]; rVg = [Res(), Res()]
            Vg = [v_.rearrange("p (j f) -> p j f", j=16) for v_ in Vgf]
            KgT = A.bf16(4, 2048); rKgT = Res()
            ck = I["ck%d" % i].rearrange("(r two) f -> r (two f)", two=2); cv = I["cv%d" % i].rearrange("(r two) f -> r (two f)", two=2)

            def gatherK(s):
                if DBG.get("nogather"):
                    return
                for nb in range(8):
                    P.op("pool", IDMA(Kgf[s % 2][:, nb * 1024:(nb + 1) * 1024], ck, self.pidx2[:, s * 8 + nb:s * 8 + nb + 1]), reads=[self.rC], writes=[rKg[s % 2]], dma=rKg[s % 2])
                for nb in range(8):
                    P.op("pool", IDMA(Vgf[s % 2][:, nb * 1024:(nb + 1) * 1024], cv, self.pidx2[:, s * 8 + nb:s * 8 + nb + 1]), reads=[self.rC], writes=[rVg[s % 2]], dma=rVg[s % 2])
            gatherK(0)
            for s in range(16):
                if s + 1 < 16:
                    gatherK(s + 1)
                for pr in range(4):
                    for half in range(2):
                        ps, rp = self.ps[6 + (pr * 2 + half) % 2], self.rps[6 + (pr * 2 + half) % 2]
                        psb = ps[:, :].bitcast(BF16)
                        P.op("pe", [TR(psb[:, k * 128:(k + 1) * 128], Kg[s % 2][:, half * 8 + k, pr * 128:(pr + 1) * 128], self.identb[:, :]) for k in range(8)],
                             reads=[rKg[s % 2], self.rC], writes=[rp])
                        P.op("act" if half == 0 else "dve", (ACP if half == 0 else CP)(KgT[:, pr, half * 1024:(half + 1) * 1024], psb), reads=[rp], writes=[rKgT])
                P.op("dve", RED(kmf, KgT[:, :, :].rearrange("p k (n t) -> p k n t", t=256), AX.X, ALU.add), reads=[rKgT], writes=[rkm])
                P.op("dve", TS(kmT, kmf, 1.0 / 256, None, ALU.mult), reads=[rkm], writes=[rkm])
                tiles = []
                for kt in range(16):
                    tiles.append(dict(KT=(lambda h, kt=kt: KgT[(h % 2) * 64:(h % 2) * 64 + 64, h // 2, kt * 128:(kt + 1) * 128]),
                                      V=(lambda h, kt=kt, s=s: Vg[s % 2][:, kt, h * 64:(h + 1) * 64]), nk=128, slot=kt // 2, mask=None, res=[rKgT, rVg[s % 2]]))
                tiles.append(dict(KT=(lambda h: KsT[(h % 2) * 64:(h % 2) * 64 + 64, h // 2, 0:64]),
                                  V=(lambda h: Vs[0:64, h * 64:(h + 1) * 64]), nk=64, slot=8,
                                  mask=self.smaskb[0:64, 4 * s:4 * s + 4], res=[rKT[16], rV[16]]))
                c0 = 2048 + 4 * s
                if DBG.get("noattn"):
                    continue
                self.attend(4, lambda h, s=s: QsT[(h % 2) * 64:(h % 2) * 64 + 64, h // 2, 4 * s:4 * s + 4], [rQT[16]], tiles,
                            True, 8, kmT, rkm, None, CAT, c0, rCAT[16])
        else:
            P.op("pool", MEMSET(CAT[:, 0:4, 2048:2112], 0.0), writes=[rCAT[16]])
        A.release(m0)
        P.barrier()
        WS = [A.bf16(8, 512) for _ in range(2)]
        rWS = [Res(), Res()]
        Wo = I["w_o_even"][i].rearrange("(c p) f -> p c f", p=128)
        self.out_proj(Wo, WS, rWS, CAT, rCAT)
        A.release(m0)
        P.barrier()

    def out_proj(self, Wo, WS, rWS, SRC, rSRC17):
        P = self.P
        self._bank = 0
        self.wload(WS[0], Wo[:, :, 0:512], rWS[0])
        self.wload(WS[1], Wo[:, :, 512:1024], rWS[1])
        for ph in range(2):
            for tb, (c0, n) in enumerate(TB):
                rs = [rSRC17[t] for t in range(17) if min(TILES[t][0] // 512, 4) == tb]
                for dcl in range(4):
                    dc = ph * 4 + dcl
                    ps, rp = self.bank()
                    P.op("pe", [MM(ps[:, 0:n], WS[ph][:, k, dcl * 128:(dcl + 1) * 128], SRC[:, k, c0:c0 + n], start=(k == 0), stop=(k == 7)) for k in range(8)],
                         reads=[rWS[ph]] + rs, writes=[rp])
                    P.op("dve", TT(self.RES[:, dc, c0:c0 + n], self.RES[:, dc, c0:c0 + n], ps[:, 0:n], ALU.add), reads=[rp, self.rRES[tb]], writes=[self.rRES[tb]])

    def attn_setup(self):
        A = self.A
        self.PT = [A.bf16(512) for _ in range(2)]; self.rPT = [Res(), Res()]
        self.acc = A.f32(8, 65); self.racc = Res()
        self.gs = A.f32(64); self.mx = A.f32(8, 8); self.sel = A.f32(8, 9); self.rsel = Res()
        self.rec = A.f32(8); self.obf = A.bf16(512); self.robf = Res()
        self.ctmp = [A.f32(8, 65) for _ in range(2)]; self.cred = [A.f32(65) for _ in range(2)]; self.rct = [Res(), Res()]
        self._pt = 0
        self._ob = 0

    def attend(self, M, QTf, rQ, tiles, sel, nq, kmT, rkm, cand, CAT, c0, rcat):
        P = self.P
        S = [(self.ps[0], self.rps[0]), (self.ps[1], self.rps[1])]
        Ob = [(self.ps[2], self.rps[2]), (self.ps[3], self.rps[3])]
        Gb, rGb = self.ps[6], self.rps[6]
        acc, racc = self.acc, self.racc
        if sel and not DBG.get("nogate"):
            Gb2, rGb2 = self.ps[7], self.rps[7]
            for par, (gb_, rgb_) in enumerate([(Gb, rGb), (Gb2, rGb2)]):
                P.op("pe", [MM(gb_[0:M, (h // 2) * 8:(h // 2) * 8 + 8], QTf(h), kmT[(h % 2) * 64:(h % 2) * 64 + 64, h // 2, :])
                            for h in range(par, 8, 2)], reads=rQ + [rkm], writes=[rgb_])
            gs4 = self.gs[0:M, :].rearrange("p (k q n) -> p k q n", k=4, q=2)
            for par, (gb_, rgb_) in enumerate([(Gb, rGb), (Gb2, rGb2)]):
                src = gb_[0:M, 0:32].rearrange("p (k n) -> p k n", k=4)
                if cand is not None:
                    P.op("dve", TT(gs4[:, :, par, :], src, cand[0:M, :].rearrange("p (k q n) -> p k q n", k=4, q=2)[:, :, par, :], ALU.add),
                         reads=[rgb_, self.rC], writes=[self.rsel])
                else:
                    P.op("dve", CP(gs4[:, :, par, :], src), reads=[rgb_], writes=[self.rsel])
            for h in range(8):
                P.op("dve", MAX8(self.mx[0:M, h, :], self.gs[0:M, h * 8:(h + 1) * 8]), reads=[self.rsel], writes=[self.rsel])
            P.op("dve", TT(self.sel[0:M, :, 0:8], self.gs[0:M, :].rearrange("p (h n) -> p h n", h=8),
                           self.mx[0:M, :, 2:3].to_broadcast([M, 8, 8]), ALU.is_ge), reads=[self.rsel], writes=[self.rsel])
        G = 512 // M if M >= 128 else 16
        groups = [tiles[k:k + G] for k in range(0, len(tiles), G)]
        g2 = []
        for grp in groups:
            a = [t for t in grp if t["nk"] == 128]
            b = [t for t in grp if t["nk"] != 128]
            if a:
                g2.append(a)
            if b:
                g2.append(b)
        groups = g2
        first_in_slot = {}
        last_in_slot = {}
        for ti, t in enumerate(tiles):
            first_in_slot.setdefault(t["slot"], ti)
            last_in_slot[t["slot"]] = ti
        for h in range(8):
            ob, rob = Ob[self._ob % 2]
            Db, rDb = self.ps[4 + self._ob % 2], self.rps[4 + self._ob % 2]
            dofs = 0
            self._ob += 1
            ti = 0
            for grp in groups:
                sb_, rsb = S[self._pt % 2]
                pt, rpt = self.PT[self._pt % 2], self.rPT[self._pt % 2]
                self._pt += 1
                nk = grp[0]["nk"]
                rr = []
                for t in grp:
                    rr += t["res"]
                P.op("pe", [MM(sb_[0:nk, k * M:(k + 1) * M], t["KT"](h), QTf(h)) for k, t in enumerate(grp)], reads=rQ + rr, writes=[rsb])
                W = len(grp) * M
                P.op("act", ACT(pt[0:nk, 0:W], sb_[0:nk, 0:W], AF.Exp, scale=0.125), reads=[rsb], writes=[rpt])
                for k, t in enumerate(grp):
                    if t["mask"] is not None:
                        P.op("pool", TT(pt[0:nk, k * M:(k + 1) * M], pt[0:nk, k * M:(k + 1) * M], t["mask"], ALU.mult), reads=[rpt, self.rC], writes=[rpt])
                mms = []
                for k, t in enumerate(grp):
                    sl = t["slot"]
                    st_, sp_ = (first_in_slot[sl] == ti), (last_in_slot[sl] == ti)
                    mms.append(MM(ob[0:M, (sl % 8) * 64:(sl % 8) * 64 + 64] if sl < 8 else Db[0:M, 64 + dofs * 4:64 + dofs * 4 + 64],
                                  pt[0:nk, k * M:(k + 1) * M], t["V"](h), start=st_, stop=sp_))
                    mms.append(MM(Db[0:M, dofs + sl:dofs + sl + 1], pt[0:nk, k * M:(k + 1) * M], self.onesb[0:nk, 0:1], start=st_, stop=sp_))
                    ti += 1
                P.op("pe", mms, reads=[rpt, self.rC] + rr, writes=[rob, rDb])
            own = 8 if (sel and nq == 8) else (7 if sel else 0)
            own_ap = ob[0:M, own * 64:own * 64 + 64] if own < 8 else Db[0:M, 64 + dofs * 4:64 + dofs * 4 + 64]
            P.op("act", ACP(acc[0:M, h, 0:64], own_ap), reads=[rob, rDb], writes=[racc])
            P.op("act", ACP(acc[0:M, h, 64:65], Db[0:M, dofs + own:dofs + own + 1]), reads=[rDb], writes=[racc])
            if sel and not DBG.get("nocombo"):
                ct, cr, rct = self.ctmp[h % 2], self.cred[h % 2], self.rct[h % 2]
                P.op("dve", TT(ct[0:M, 0:nq, 0:64], ob[0:M, 0:nq * 64].rearrange("p (n d) -> p n d", d=64),
                               self.sel[0:M, h, 0:nq].unsqueeze(2).to_broadcast([M, nq, 64]), ALU.mult), reads=[rob, self.rsel], writes=[rct])
                P.op("dve", TT(ct[0:M, 0:nq, 64], Db[0:M, 0:nq], self.sel[0:M, h, 0:nq], ALU.mult), reads=[rDb, self.rsel], writes=[rct])
                P.op("dve", RED(cr[0:M, :], ct[0:M, 0:nq, :].rearrange("p n d -> p d n"), AX.X, ALU.add), reads=[rct], writes=[rct])
                P.op("dve", TT(acc[0:M, h, :], acc[0:M, h, :], cr[0:M, :], ALU.add), reads=[rct, racc], writes=[racc])
        P.op("dve", RECIP(self.rec[0:M, :], acc[0:M, :, 64]), reads=[racc], writes=[self.robf])
        P.op("dve", TT(self.obf[0:M, :].rearrange("p (h d) -> p h d", h=8), acc[0:M, :, 0:64],
                       self.rec[0:M, :].unsqueeze(2).to_broadcast([M, 8, 64]), ALU.mult), reads=[racc, self.robf], writes=[self.robf])
        ps, rp = self.ps[7], self.rps[7]
        psb = ps[:, 0:256].bitcast(BF16)
        P.op("pe", [TR(psb[:, pr * 128:pr * 128 + M], self.obf[0:M, pr * 128:(pr + 1) * 128], self.identb[0:M, 0:M]) for pr in range(4)],
             reads=[self.robf, self.rC], writes=[rp])
        P.op("act", ACP(CAT[:, 0:4, c0:c0 + M], psb.rearrange("p (k t) -> p k t", k=4)[:, :, 0:M]), reads=[rp], writes=[rcat])

    def odd_layer(self, l):
        P, A, I, O = self.P, self.A, self.I, self.O
        j = l // 2
        self.rmsnorm_all(self.nmix[:, l, :])
        Win = I["w_in_odd"][j].rearrange("(c p) f -> p c f", p=128)
        m0 = A.mark()
        WS = [A.bf16(8, 512) for _ in range(2)]; rWS = [Res(), Res()]
        U = A.bf16(8, NT); rU = [Res() for _ in range(17)]
        lng = A.f32(1024); lnb = A.f32(1024); rln = Res()
        self.load(lng, I["ln_g"][j:j + 1, :].partition_broadcast(128), rln)
        self.load(lnb, I["ln_b"][j:j + 1, :].partition_broadcast(128), rln)
        wtf = A.f32(8, 128); WTm = A.bf16(8, 128); rwt = Res()
        self.load(wtf, I["sgu_wT"][j].rearrange("g s t -> s g t"), rwt)
        P.op("dve", TT(WTm, wtf, self.trif.unsqueeze(1).to_broadcast([128, 8, 128]), ALU.mult), reads=[rwt, self.rC], writes=[rwt])
        sbb = A.bf16(8, 128); rsbb = Res()
        self.wload(sbb[0:1], I["sgu_b"][j:j + 1].rearrange("o g t -> o g t"), rsbb)
        wsm = A.f32(8, 4, 4); bsm = A.f32(8, 4); rsm = Res()
        for g_ in range(8):
            self.load(wsm[:, g_, :, :], I["sgu_w"][j][g_, 0:4, 0:4].partition_broadcast(128), rsm)
        self.load(bsm, I["sgu_b"][j][:, 0:4].partition_broadcast(128), rsm)
        self._bank = 0
        for ph in range(2):
            self.wload(WS[ph], Win[:, :, ph * 512:(ph + 1) * 512], rWS[ph])
        for ph in range(2):
            for tb, (c0, n) in enumerate(TB):
                for fl in range(4):
                    fc = ph * 4 + fl
                    ps, rp = self.bank()
                    P.op("pe", [MM(ps[:, 0:n], WS[ph][:, c, fl * 128:(fl + 1) * 128], self.XN[:, c, c0:c0 + n], start=(c == 0), stop=(c == 7)) for c in range(8)],
                         reads=[rWS[ph], self.rXN[tb]], writes=[rp])
                    P.op("act", ACT(U[:, fc, c0:c0 + n], ps[:, 0:n], AF.Gelu_apprx_tanh), reads=[rp],
                         writes=[rU[t] for t in range(17) if min(TILES[t][0] // 512, 4) == tb])
        for ph in range(2):
            self.wload(WS[ph], Win[:, :, 1024 + ph * 512:1024 + (ph + 1) * 512], rWS[ph])
        vg = [A.f32(1024) for _ in range(2)]; rvg = [Res(), Res()]
        vln = [A.f32(1024) for _ in range(2)]; rvln = [Res(), Res()]
        vlb = [A.bf16(1024) for _ in range(2)]; rvlb = [Res(), Res()]
        st1 = [A.f32(4) for _ in range(2)]; rst = [Res(), Res()]
        vT = A.f32(8, 64); mixT = A.f32(8, 64); rvT = Res(); rmix = Res()
        for ti, (c0, r) in enumerate(TILES):
            b = ti % 2
            tbi = min(c0 // 512, 4)
            for ph in range(2):
                ps, rp = self.bank()
                P.op("pe", [MM(ps[0:r, :], self.XN[:, c, c0:c0 + r], WS[ph][:, c, :], start=(c == 0), stop=(c == 7)) for c in range(8)],
                     reads=[rWS[ph], self.rXN[tbi]], writes=[rp])
                P.op("act", ACT(vg[b][0:r, ph * 512:(ph + 1) * 512], ps[0:r, :], AF.Gelu_apprx_tanh), reads=[rp], writes=[rvg[b]])
            s1 = st1[b]
            P.op("dve", RED(s1[0:r, 0:1], vg[b][0:r, :], AX.X, ALU.add), reads=[rvg[b]], writes=[rst[b]])
            P.op("dve", TS(s1[0:r, 1:2], s1[0:r, 0:1], 1.0 / 1024, None, ALU.mult), reads=[rst[b]], writes=[rst[b]])
            P.op("dve", TS(vg[b][0:r, :], vg[b][0:r, :], s1[0:r, 1:2], None, ALU.subtract), reads=[rst[b], rvg[b]], writes=[rvg[b]])
            P.op("act", ACT(vln[b][0:r, :], vg[b][0:r, :], AF.Square, accum_out=s1[0:r, 2:3]), reads=[rvg[b], rst[b]], writes=[rvln[b], rst[b]])
            P.op("act", ACT(s1[0:r, 3:4], s1[0:r, 2:3], AF.Sqrt, bias=1e-5, scale=1.0 / 1024), reads=[rst[b]], writes=[rst[b]])
            P.op("dve", RECIP(s1[0:r, 3:4], s1[0:r, 3:4]), reads=[rst[b]], writes=[rst[b]])
            P.op("dve", STT(vln[b][0:r, :], vg[b][0:r, :], s1[0:r, 3:4], lng[0:r, :], ALU.mult, ALU.mult), reads=[rvg[b], rst[b], rln], writes=[rvln[b]])
            P.op("pool", TT(vln[b][0:r, :], vln[b][0:r, :], lnb[0:r, :], ALU.add), reads=[rvln[b], rln], writes=[rvln[b]])
            if ti == 16:
                self.store(O["sgv"][j], vln[b][0:64, :], rvln[b])
                for half in range(2):
                    ps, rp = self.bank()
                    P.op("pe", [TR(ps[:, k * 64:(k + 1) * 64], vln[b][0:64, (half * 4 + k) * 128:(half * 4 + k + 1) * 128], self.identf[0:64, 0:64]) for k in range(4)],
                         reads=[rvln[b], self.rC], writes=[rp])
                    P.op("act", ACP(vT[:, half * 4:half * 4 + 4, :], ps[:, 0:256].rearrange("p (k t) -> p k t", k=4)), reads=[rp], writes=[rvT])
                v4 = vT[:, :, :].rearrange("p g (s t) -> p g s t", t=4)
                m4 = mixT[:, :, :].rearrange("p g (s t) -> p g s t", t=4)
                for g in range(8):
                    for t in range(4):
                        P.op("dve", TS(m4[:, g, :, t], v4[:, g, :, 0], wsm[:, g, t, 0:1], bsm[:, g, t:t + 1], ALU.mult, ALU.add), reads=[rvT, rsm], writes=[rmix])
                        for s_ in range(1, t + 1):
                            P.op("dve", STT(m4[:, g, :, t], v4[:, g, :, s_], wsm[:, g, t, s_:s_ + 1], m4[:, g, :, t], ALU.mult, ALU.add), reads=[rvT, rsm, rmix], writes=[rmix])
                P.op("dve", TT(U[:, :, 2048:2112], U[:, :, 2048:2112], mixT[:, :, :], ALU.mult), reads=[rmix, rU[16]], writes=[rU[16]])
                continue
            P.op("act", ACP(vlb[b][:, :], vln[b][:, :]), reads=[rvln[b]], writes=[rvlb[b]])
            for half in range(2):
                ps, rp = self.bank()
                mms = []
                for k in range(4):
                    g = half * 4 + k
                    mms.append(MM(ps[:, k * 128:(k + 1) * 128], vlb[b][:, g * 128:(g + 1) * 128], WTm[:, g, :], start=True, stop=False))
                    mms.append(MM(ps[:, k * 128:(k + 1) * 128], self.onesb[0:1, :], sbb[0:1, g, :], start=False, stop=True))
                P.op("pe", mms, reads=[rvlb[b], rwt, rsbb, self.rC], writes=[rp])
                P.op("dve", TT(U[:, half * 4:half * 4 + 4, c0:c0 + 128], U[:, half * 4:half * 4 + 4, c0:c0 + 128],
                               ps[:, :].rearrange("p (k t) -> p k t", k=4), ALU.mult), reads=[rp, rU[ti]], writes=[rU[ti]])
        Wo = I["w_o_odd"][j].rearrange("(c p) f -> p c f", p=128)
        self.out_proj(Wo, WS, rWS, U, rU)
        A.release(m0)
        P.barrier()

    def ffn(self, l):
        P, A, I = self.P, self.A, self.I
        self.rmsnorm_all(self.nffn[:, l, :])
        m0 = A.mark()
        WU = [A.bf16(8, 512) for _ in range(2)]; WD = [A.bf16(4, 1024) for _ in range(2)]
        rW = [Res(), Res()]
        H = [A.bf16(4, 512) for _ in range(2)]; rH = [Res(), Res()]
        tmp = [A.f32(512) for _ in range(2)]; rtmp = [Res(), Res()]
        Wup = I["w_up"][l].rearrange("(c p) f -> p c f", p=128)
        Wdn = I["w_dn"][l].rearrange("(c p) f -> p c f", p=128)

        def wl(fg):
            s = fg % 2
            P.op("pool", [DMA(WU[s], Wup[:, :, fg * 512:(fg + 1) * 512]), DMA(WD[s], Wdn[:, fg * 4:(fg + 1) * 4, :])], writes=[rW[s]], dma=rW[s])
        wl(0)
        self._bank = 0
        kk = 0
        hh = 0
        for fg in range(8):
            if fg + 1 < 8:
                wl(fg + 1)
            s = fg % 2
            for tb, (c0, n) in enumerate(TB):
                hb = hh % 2
                hh += 1
                for fc in range(4):
                    ps, rp = self.bank()
                    P.op("pe", [MM(ps[:, 0:n], WU[s][:, c, fc * 128:(fc + 1) * 128], self.XN[:, c, c0:c0 + n], start=(c == 0), stop=(c == 7)) for c in range(8)],
                         reads=[rW[s], self.rXN[tb]], writes=[rp])
                    tb_ = kk % 2
                    kk += 1
                    P.op("act", ACT(tmp[tb_][:, 0:n], ps[:, 0:n], AF.Relu), reads=[rp], writes=[rtmp[tb_]])
                    P.op("pool", TT(H[hb][:, fc, 0:n], tmp[tb_][:, 0:n], tmp[tb_][:, 0:n], ALU.mult), reads=[rtmp[tb_]], writes=[rH[hb]])
                for dc in range(8):
                    ps, rp = self.bank()
                    P.op("pe", [MM(ps[:, 0:n], WD[s][:, fc, dc * 128:(dc + 1) * 128], H[hb][:, fc, 0:n], start=(fc == 0), stop=(fc == 3)) for fc in range(4)],
                         reads=[rW[s], rH[hb]], writes=[rp])
                    P.op("dve", TT(self.RES[:, dc, c0:c0 + n], self.RES[:, dc, c0:c0 + n], ps[:, 0:n], ALU.add), reads=[rp, self.rRES[tb]], writes=[self.rRES[tb]])
        A.release(m0)
        P.barrier()

    def final(self):
        P, A, O = self.P, self.A, self.O
        m0 = A.mark()
        yt = [A.f32(1024) for _ in range(2)]; ryt = [Res(), Res()]
        yo = [A.f32(1024) for _ in range(2)]; ryo = [Res(), Res()]
        junk = A.f32(1024); rjunk = Res()
        st_ = [A.f32(2) for _ in range(2)]; rst = [Res(), Res()]
        self._bank = 0
        for ti, (c0, r) in enumerate(TILES):
            b = ti % 2
            tbi = min(c0 // 512, 4)
            for half in range(2):
                ps, rp = self.bank()
                P.op("pe", [TR(ps[0:r, k * 128:(k + 1) * 128], self.RES[:, half * 4 + k, c0:c0 + r], self.identf[:, :]) for k in range(4)],
                     reads=[self.rRES[tbi], self.rC], writes=[rp])
                P.op("act" if half == 0 else "dve", (ACP if half == 0 else CP)(yt[b][0:r, half * 512:(half + 1) * 512], ps[0:r, :]), reads=[rp], writes=[ryt[b]])
            P.op("act", ACT(junk[0:r, :], yt[b][0:r, :], AF.Square, accum_out=st_[b][0:r, 0:1]), reads=[ryt[b]], writes=[rjunk, rst[b]])
            P.op("act", ACT(st_[b][0:r, 1:2], st_[b][0:r, 0:1], AF.Sqrt, bias=1e-6, scale=1.0 / 1024), reads=[rst[b]], writes=[rst[b]])
            P.op("dve", RECIP(st_[b][0:r, 1:2], st_[b][0:r, 1:2]), reads=[rst[b]], writes=[rst[b]])
            P.op("dve", STT(yo[b][0:r, :], yt[b][0:r, :], st_[b][0:r, 1:2], self.nfin[0:r, :], ALU.mult, ALU.mult), reads=[ryt[b], rst[b], self.rC], writes=[ryo[b]])
            dst = O["yp"][c0:c0 + r, :] if ti < 16 else O["ys"]
            self.store(dst, yo[b][0:r, :], ryo[b])
        A.release(m0)


def _host_consts():
    pos = np.zeros((128, 17), np.float32)
    for i in range(16):
        pos[:, i] = i * 128 + np.arange(128)
    pos[:, 16] = 2048 + (np.arange(128) % 4)
    inv = (10000.0 ** (-np.arange(0, 64, 2, dtype=np.float32) / 64)).astype(np.float32)
    ang = pos[:, :, None].astype(np.float32) * inv[None, None, :]
    cosT = np.cos(ang).astype(np.float32)
    sinT = np.sin(ang).astype(np.float32)
    k = np.arange(128)
    tri = (k[:, None] <= k[None, :]).astype(np.float32)
    kk = np.arange(64)
    smask = np.zeros((64, 64), np.float32)
    for s in range(16):
        for q in range(4):
            smask[:, 4 * s + q] = ((kk // 4 == s) & (kk % 4 <= q)).astype(np.float32)
    cand = np.zeros((128, 4, 64), np.float32)
    for a, nq in enumerate([4, 5, 6, 7]):
        c = np.where(np.arange(8) < nq, 0.0, -1e30).astype(np.float32)
        cand[:, a, :] = np.tile(c, 8)[None, :]
    invcnt = np.zeros((128, 4, 16), np.float32)
    for g, w in enumerate([2, 4, 8, 16]):
        invcnt[:, g, :] = (1.0 / np.minimum(np.arange(16) + 1, w)).astype(np.float32)[None, :]
    return dict(cosT=cosT, sinT=sinT, tri=tri, smask=smask, cand=cand, invcnt=invcnt,
                iota=np.arange(128, dtype=np.float32)[:, None], iota2=(np.arange(128) % 64).astype(np.float32)[:, None], ident=np.eye(128, dtype=np.float32),
                ones=np.ones((128, 128), np.float32))


def _core_inputs(c, inp, ck, cv, ptab_c, consts):
    f = np.ascontiguousarray
    d = dict(consts)
    d["xp"] = f(inp["x_prompt"][c])
    d["xs"] = f(inp["x_sample"][16 * c:16 * c + 16].reshape(64, 1024))
    d["ck0"], d["ck1"], d["cv0"], d["cv1"] = ck[0], ck[1], cv[0], cv[1]
    d["spool"] = f(inp["state_pool"][:, 16 * c:16 * c + 16].reshape(2, 240, 512))
    d["ptab"] = f(ptab_c.reshape(1, 256).astype(np.int32))
    d["nmix"] = f(inp["norm_mix"].reshape(4, 8, 128).transpose(2, 0, 1))
    d["nffn"] = f(inp["norm_ffn"].reshape(4, 8, 128).transpose(2, 0, 1))
    d["nfin"] = f(inp["norm_final"].reshape(1, 1024))
    d["pscale"] = f(inp["pool_scale"].reshape(2, 4, 128).transpose(2, 0, 1))
    for a, b in [("w_in_even", "w_in_even"), ("w_o_even", "w_o_even"), ("w_pool", "w_pool"), ("w_in_odd", "w_in_odd"),
                 ("ln_g", "sgu_ln_g"), ("ln_b", "sgu_ln_b"), ("sgu_w", "sgu_w"), ("sgu_b", "sgu_b"), ("w_o_odd", "w_o_odd"),
                 ("w_up", "w_ffn_up"), ("w_dn", "w_ffn_down")]:
        d[a] = inp[b]
    d["sgu_wT"] = f(inp["sgu_w"].transpose(0, 1, 3, 2))
    return d


_NC_CACHE = {}


def _get_nc(npg, **kw):
    key = (npg, tuple(sorted(kw.items())))
    if key not in _NC_CACHE:
        _NC_CACHE[key] = Builder(npg, **kw).build()
    return _NC_CACHE[key]


def _assemble(results, cores):
    n = len(cores)
    yp = np.stack([r["yp"] for r in results])
    ys = np.concatenate([r["ys"].reshape(16, 4, 1024) for r in results])
    pk = np.stack([r["pk"].reshape(2, 2048, 8, 64) for r in results], axis=1)
    pv = np.stack([r["pv"].reshape(2, 2048, 8, 64) for r in results], axis=1)
    sk = np.concatenate([r["sk"].reshape(2, 16, 4, 8, 64) for r in results], axis=1)
    sv = np.concatenate([r["sv"].reshape(2, 16, 4, 8, 64) for r in results], axis=1)
    pps = np.stack([r["pps"] for r in results], axis=1)
    sps = np.concatenate([r["sps"] for r in results], axis=1)
    sgv = np.concatenate([r["sgv"].reshape(2, 16, 4, 1024) for r in results], axis=1)
    return (yp, ys, pk, pv, sk, sv, pps, sps, sgv)


def kernel(**inp):
    inp = {k: np.asarray(v) for k, v in inp.items()}
    consts = _host_consts()
    npg = inp["cache_k"].shape[1]
    ck = [np.ascontiguousarray(inp["cache_k"][i].reshape(npg * 128, 512)) for i in range(2)]
    cv = [np.ascontiguousarray(inp["cache_v"][i].reshape(npg * 128, 512)) for i in range(2)]
    nc = _get_nc(npg)
    in_maps = [_core_inputs(c, inp, ck, cv, inp["page_table"][16 * c:16 * c + 16], consts) for c in range(8)]
    res = run_bass_kernel_spmd(nc, in_maps, core_ids=list(range(8)))
    return _assemble(res.results, list(range(8)))
```

```python
import numpy as np
import concourse.bass as bass
import concourse.mybir as mybir
from concourse.bass_utils import run_bass_kernel_spmd
from contextlib import ExitStack

F32 = mybir.dt.float32
BF16 = mybir.dt.bfloat16
I32 = mybir.dt.int32
ALU = mybir.AluOpType
AF = mybir.ActivationFunctionType
AX = mybir.AxisListType

NT = 2112
TB = [(0, 512), (512, 512), (1024, 512), (1536, 512), (2048, 64)]
TILES = [(i * 128, 128) for i in range(16)] + [(2048, 64)]
NEG = -30000.0
import os
DBG = {k: 1 for k in os.environ.get('KDBG', '').split(',') if k}


class Res:
    __slots__ = ("name", "wr", "acc", "sem", "cnt", "excl")

    def __init__(self, name="", excl=False):
        self.name = name
        self.excl = excl
        self.wr = {}
        self.acc = {}
        self.sem = None
        self.cnt = 0


class Op:
    __slots__ = ("eng", "fns", "deps", "seq", "tkey", "order", "val", "target", "dma", "cclock", "waits")


class Prog:
    ENGS = ["pe", "act", "dve", "pool", "sp"]

    def __init__(self, nc):
        self.nc = nc
        self.ops = []
        self.per_eng = {e: [] for e in self.ENGS}
        self.stores = []
        self.last_by_track = {}
        self.pending = {e: None for e in self.ENGS}

    def barrier(self):
        snap = list(self.last_by_track.values())
        for e in self.ENGS:
            self.pending[e] = snap

    def op(self, eng, fns, reads=(), writes=(), dma=None, store=False):
        o = Op()
        o.eng = eng
        o.fns = list(fns) if isinstance(fns, (list, tuple)) else [fns]
        deps = []
        if self.pending[eng] is not None:
            deps.extend(self.pending[eng])
            self.pending[eng] = None
        reads = list(reads)
        writes = list(writes)
        for r in reads:
            if r.excl and r not in writes:
                writes.append(r)
        for r in reads:
            deps.extend(r.wr.values())
        for w in writes:
            deps.extend(w.acc.values())
        o.deps = deps
        o.dma = dma
        o.target = False
        o.waits = []
        o.seq = len(self.per_eng[eng])
        if dma is not None:
            dma.cnt += 16 * len(o.fns)
            o.tkey = ("dma", id(dma))
            o.order = dma.cnt
            o.val = dma.cnt
        else:
            o.tkey = eng
            o.order = o.seq
            o.val = None
        for r in reads:
            r.acc[o.tkey] = o
        for w in writes:
            w.wr[o.tkey] = o
            w.acc[o.tkey] = o
        self.per_eng[eng].append(o)
        self.ops.append(o)
        self.last_by_track[o.tkey] = o
        if store:
            self.stores.append(o)
        return o

    def finalize(self):
        fin = Op()
        fin.eng = "sp"; fin.fns = []; fin.deps = list(self.stores); fin.dma = None
        fin.target = False; fin.waits = []; fin.seq = len(self.per_eng["sp"])
        fin.tkey = "sp"; fin.order = fin.seq; fin.val = None
        self.per_eng["sp"].append(fin)
        self.ops.append(fin)
        clock = {e: {} for e in self.ENGS}
        for o in self.ops:
            ck = clock[o.eng]
            best = {}
            for d in o.deps:
                if ck.get(d.tkey, -1) >= d.order:
                    continue
                b = best.get(d.tkey)
                if b is None or b.order < d.order:
                    best[d.tkey] = d
            for d in best.values():
                d.target = True
                for k, v in d.cclock.items():
                    if ck.get(k, -1) < v:
                        ck[k] = v
            o.waits = list(best.values())
            cc = dict(ck)
            cc[o.tkey] = o.order
            o.cclock = cc
            o.deps = None
        for e in self.ENGS:
            c = 0
            for o in self.per_eng[e]:
                if o.dma is None:
                    if o.target:
                        c += 1
                    o.val = c
        for o in self.ops:
            o.cclock = None

    def emit(self):
        nc = self.nc
        with ExitStack() as st:
            esem = {e: st.enter_context(nc.semaphore("s_" + e)) for e in ["pe", "act", "dve", "pool", "sp"]}
            n = 0
            for o in self.ops:
                if o.dma is not None and o.dma.sem is None:
                    o.dma.sem = st.enter_context(nc.semaphore("d%d" % n))
                    n += 1
            block = st.enter_context(nc.Block())

            def run(eng_name):
                def body(eng):
                    for o in self.per_eng[eng_name]:
                        for d in o.waits:
                            sem = d.dma.sem if d.dma is not None else esem[d.eng]
                            eng.wait_ge(sem, d.val)
                        last = None
                        for fn in o.fns:
                            last = fn(eng)
                            if o.dma is not None:
                                last.then_inc(o.dma.sem, 16)
                        if o.dma is None and o.target:
                            last.then_inc(esem[eng_name], 1)
                return body

            block.tensor(run("pe"))
            block.scalar(run("act"))
            block.vector(run("dve"))
            block.gpsimd(run("pool"))
            block.sync(run("sp"))


def MM(out, lhsT, rhs, start=True, stop=True):
    return lambda e: e.matmul(out, lhsT=lhsT, rhs=rhs, start=start, stop=stop)


def TR(out, in_, ident):
    return lambda e: e.transpose(out, in_, ident)


def ACT(out, in_, func, **kw):
    return lambda e: e.activation(out=out, in_=in_, func=func, **kw)


def TT(out, in0, in1, op):
    return lambda e: e.tensor_tensor(out=out, in0=in0, in1=in1, op=op)


def TS(out, in0, s1, s2, op0, op1=None):
    if op1 is None:
        return lambda e: e.tensor_scalar(out=out, in0=in0, scalar1=s1, scalar2=None, op0=op0)
    return lambda e: e.tensor_scalar(out=out, in0=in0, scalar1=s1, scalar2=s2, op0=op0, op1=op1)


def STT(out, in0, scalar, in1, op0, op1):
    return lambda e: e.scalar_tensor_tensor(out=out, in0=in0, scalar=scalar, in1=in1, op0=op0, op1=op1)


def CP(out, in_):
    return lambda e: e.tensor_copy(out=out, in_=in_)


def ACP(out, in_):
    return lambda e: e.copy(out=out, in_=in_)


def DMA(out, in_):
    return lambda e: e.dma_start(out=out, in_=in_)


def IDMA(out, in_, idx):
    return lambda e: e.indirect_dma_start(out=out, out_offset=None, in_=in_,
                                          in_offset=bass.IndirectOffsetOnAxis(ap=idx, axis=0))


def RED(out, in_, axis, op):
    return lambda e: e.tensor_reduce(out=out, in_=in_, axis=axis, op=op)


def RECIP(out, in_):
    return lambda e: e.reciprocal(out=out, in_=in_)


def MAX8(out, in_):
    return lambda e: e.max(out=out, in_=in_)


def MEMSET(ap, v):
    return lambda e: e.memset(ap, v)


def _reshape(v, shape):
    if len(shape) == 1:
        return v
    names = ["a", "b", "c", "d"][:len(shape)]
    kw = {names[k]: shape[k] for k in range(1, len(shape))}
    return v.rearrange("p (%s) -> p %s" % (" ".join(names), " ".join(names)), **kw)


class Arena:
    def __init__(self, ap, nwords):
        self.ap = ap
        self.n = nwords
        self.top = 0
        self.peak = 0

    def mark(self):
        return self.top

    def release(self, m):
        self.top = m

    def _take(self, w):
        w = (w + 7) // 8 * 8
        off = self.top
        self.top += w
        self.peak = max(self.peak, self.top)
        assert self.top <= self.n, "arena overflow %d > %d" % (self.top, self.n)
        return off, w

    def f32(self, *shape):
        n = int(np.prod(shape))
        off, w = self._take(n)
        return _reshape(self.ap[:, off:off + n], shape)

    def i32(self, *shape):
        n = int(np.prod(shape))
        off, w = self._take(n)
        return _reshape(self.ap[:, off:off + n].bitcast(I32), shape)

    def bf16(self, *shape):
        n = int(np.prod(shape))
        off, w = self._take((n + 1) // 2)
        return _reshape(self.ap[:, off:off + w].bitcast(BF16)[:, 0:n], shape)


class StopBuild(Exception):
    pass


class Builder:
    def stage(self, name):
        if self.stop_at == name:
            raise StopBuild()

    def __init__(self, npg, n_layers=4, do_sample_attn=True, stop_at=None):
        self.stop_at = stop_at
        self.npg = npg
        self.n_layers = n_layers
        self.do_sample_attn = do_sample_attn
        self.nc = bass.Bass("TRN2", target_bir_lowering=False)

    def dram(self):
        nc = self.nc
        I = {}
        O = {}

        def inp(name, shape, dt=F32):
            I[name] = nc.dram_tensor(name, list(shape), dt, kind="ExternalInput").ap()

        def out(name, shape):
            O[name] = nc.dram_tensor(name, list(shape), F32, kind="ExternalOutput").ap()

        inp("xp", [2048, 1024]); inp("xs", [64, 1024])
        for nm in ["ck0", "ck1", "cv0", "cv1"]:
            inp(nm, [self.npg * 128, 512])
        inp("spool", [2, 240, 512]); inp("ptab", [1, 256], I32)
        inp("nmix", [128, 4, 8]); inp("nffn", [128, 4, 8]); inp("nfin", [1, 1024]); inp("pscale", [128, 2, 4])
        inp("w_in_even", [2, 1024, 2048]); inp("w_o_even", [2, 1024, 1024]); inp("w_pool", [2, 4, 128, 128])
        inp("w_in_odd", [2, 1024, 2048]); inp("ln_g", [2, 1024]); inp("ln_b", [2, 1024])
        inp("sgu_wT", [2, 8, 128, 128]); inp("sgu_w", [2, 8, 128, 128]); inp("sgu_b", [2, 8, 128]); inp("w_o_odd", [2, 1024, 1024])
        inp("w_up", [4, 1024, 4096]); inp("w_dn", [4, 4096, 1024])
        inp("cosT", [128, 17, 32]); inp("sinT", [128, 17, 32]); inp("tri", [128, 128]); inp("smask", [64, 64])
        inp("cand", [128, 4, 64]); inp("invcnt", [128, 4, 16]); inp("iota", [128, 1]); inp("iota2", [128, 1]); inp("ident", [128, 128])
        inp("ones", [128, 128])
        out("yp", [2048, 1024]); out("ys", [64, 1024])
        out("pk", [2, 2048, 512]); out("pv", [2, 2048, 512]); out("sk", [2, 64, 512]); out("sv", [2, 64, 512])
        out("pps", [2, 15, 512]); out("sps", [2, 16, 15, 512]); out("sgv", [2, 64, 1024])
        self.I, self.O = I, O

    def bank(self):
        b = self._bank
        self._bank = (self._bank + 1) % 8
        return self.ps[b], self.rps[b]

    def load(self, out_ap, in_ap, res, eng="sp"):
        self.P.op(eng, DMA(out_ap, in_ap), writes=[res], dma=res)

    def store(self, out_ap, in_ap, res):
        self.P.op("sp", DMA(out_ap, in_ap), reads=[res], dma=res, store=True)

    def build(self):
        self.dram()
        nc = self.nc
        with ExitStack() as st:
            self.st = st
            AW = 32500
            RES = st.enter_context(nc.sbuf_tensor("RES", [128, 8, NT], F32))
            CW = 3776
            cst = st.enter_context(nc.sbuf_tensor("CST", [128, CW], F32))
            arena_t = st.enter_context(nc.sbuf_tensor("ARENA", [128, AW], F32))
            self.ps = []
            for b in range(8):
                t = st.enter_context(nc.psum_tensor("ps%d" % b, [128, 512], F32))
                self.ps.append(t)
            self.P = Prog(nc)
            P = self.P
            self.rps = [Res("ps%d" % b, excl=True) for b in range(8)]
            self._bank = 0
            self.RES = RES
            self.rRES = [Res("RES%d" % i) for i in range(5)]
            self.A = Arena(arena_t[:, :], AW)
            self.C = Arena(cst[:, :], CW)
            self.consts()
            self.XN = self.A.bf16(8, NT)
            self.rXN = [Res("XN%d" % i) for i in range(5)]
            try:
                self.stage("consts")
                self.load_x()
                self.stage("loadx")
                for l in range(self.n_layers):
                    if l % 2 == 0:
                        self.even_layer(l)
                    else:
                        self.odd_layer(l)
                    self.stage("mixer%d" % l)
                    self.ffn(l)
                    self.stage("ffn%d" % l)
                self.final()
            except StopBuild:
                pass
            P.finalize()
            P.emit()
        return nc

    def consts(self):
        P, C, I = self.P, self.C, self.I
        self.rC = Res("consts")
        rC = self.rC

        def ld(name, shape, src=None):
            t = C.f32(*shape)
            self.load(t, I[name] if src is None else src, rC)
            return t

        def ldb(name, shape, src=None, rows=128):
            t = C.bf16(*shape)
            self.load(t[0:rows], I[name] if src is None else src, rC, eng="pool")
            return t

        self.cosT = ld("cosT", [17, 32]); self.sinT = ld("sinT", [17, 32])
        self.trif = ld("tri", [128]); self.trib = ldb("tri", [128])
        self.smaskb = ldb("smask", [64], rows=64)
        self.cand = ld("cand", [4, 64]); self.invcnt = ld("invcnt", [4, 16]); self.iota = ld("iota", [1])
        self.identf = ld("ident", [128]); self.identb = ldb("ident", [128])
        self.onesb = ldb("ones", [128])
        self.nmix = ld("nmix", [4, 8]); self.nffn = ld("nffn", [4, 8]); self.pscale = ld("pscale", [2, 4])
        self.nfin = ld("nfin", [1024], src=I["nfin"].partition_broadcast(128))
        ptb = C.i32(256)
        self.load(ptb, I["ptab"].partition_broadcast(128), rC)
        io2 = ld("iota2", [1])
        pf0 = C.f32(256)
        pf2 = C.f32(128)
        self.pidx2 = C.i32(128)
        P.op("dve", CP(pf0, ptb), reads=[rC], writes=[rC])
        pv_ = pf0.rearrange("p (m j) -> p m j", j=2)
        P.op("dve", TS(pf2[0:64, :], pv_[0:64, :, 0], 64.0, io2[0:64, 0:1], ALU.mult, ALU.add), reads=[rC], writes=[rC])
        P.op("dve", TS(pf2[64:128, :], pv_[64:128, :, 1], 64.0, io2[64:128, 0:1], ALU.mult, ALU.add), reads=[rC], writes=[rC])
        P.op("dve", CP(self.pidx2, pf2), reads=[rC], writes=[rC])

    def load_x(self):
        P, A = self.P, self.A
        m = A.mark()
        xt = [A.f32(1024) for _ in range(2)]
        rxt = [Res(), Res()]
        for ti, (c0, r) in enumerate(TILES):
            b = ti % 2
            src = self.I["xp"][c0:c0 + r, :] if ti < 16 else self.I["xs"]
            self.load(xt[b][0:r], src, rxt[b])
            for half in range(2):
                ps, rp = self.bank()
                P.op("pe", [TR(ps[:, k * 128:k * 128 + r], xt[b][0:r, (half * 4 + k) * 128:(half * 4 + k + 1) * 128], self.identf[0:r, 0:r])
                            for k in range(4)], reads=[rxt[b], self.rC], writes=[rp])
                src_v = ps[:, :].rearrange("p (k t) -> p k t", k=4)[:, :, 0:r]
                P.op("act" if half == 0 else "dve",
                     (ACP if half == 0 else CP)(self.RES[:, half * 4:half * 4 + 4, c0:c0 + r], src_v),
                     reads=[rp], writes=[self.rRES[min(c0 // 512, 4)]])
        A.release(m)
        P.barrier()

    def rmsnorm_all(self, gains):
        P, A = self.P, self.A
        m = A.mark()
        sq = [A.bf16(512) for _ in range(2)]
        rsq = [Res(), Res()]
        rs = [A.f32(512) for _ in range(2)]
        rrs = [Res(), Res()]
        k = 0
        for tb, (c0, n) in enumerate(TB):
            ps, rp = self.bank()
            for c in range(8):
                b = k % 2
                k += 1
                P.op("act", ACT(sq[b][:, 0:n], self.RES[:, c, c0:c0 + n], AF.Square), reads=[self.rRES[tb]], writes=[rsq[b]])
                P.op("pe", MM(ps[:, 0:n], self.onesb[:, :], sq[b][:, 0:n], start=(c == 0), stop=(c == 7)),
                     reads=[rsq[b], self.rC], writes=[rp])
            b2 = tb % 2
            P.op("act", ACT(rs[b2][:, 0:n], ps[:, 0:n], AF.Sqrt, bias=1e-6, scale=1.0 / 1024), reads=[rp], writes=[rrs[b2]])
            P.op("dve", RECIP(rs[b2][:, 0:n], rs[b2][:, 0:n]), reads=[rrs[b2]], writes=[rrs[b2]])
            for c in range(8):
                P.op("dve",
                     STT(self.XN[:, c, c0:c0 + n], self.RES[:, c, c0:c0 + n], gains[:, c:c + 1], rs[b2][:, 0:n], ALU.mult, ALU.mult),
                     reads=[self.rRES[tb], rrs[b2], self.rC], writes=[self.rXN[tb]])
        A.release(m)
        P.barrier()

    def wload(self, dst, src, res):
        self.P.op("pool", DMA(dst, src), writes=[res], dma=res)

    def even_layer(self, l):
        P, A, I, O = self.P, self.A, self.I, self.O
        i = l // 2
        self.rmsnorm_all(self.nmix[:, l, :])
        self.stage("norm")
        Win = I["w_in_even"][i].rearrange("(c p) f -> p c f", p=128)
        m0 = A.mark()
        QsT = A.bf16(4, 64); KsT = A.bf16(4, 64); Vs = A.bf16(512)
        self.attn_setup()
        kmf = A.f32(4, 8); kmT = A.bf16(4, 8); rkm = Res()
        wp = A.bf16(4, 128)
        rwp = Res()
        mP = A.mark()
        WS = [A.bf16(8, 512)]
        WS.append(WS[0])
        rWS = [Res()]
        rWS.append(rWS[0])
        POOLED = A.bf16(4, NT)
        rPOOLED = [Res() for _ in range(4)]
        m1 = A.mark()
        L = 16 + 2048
        PG = A.f32(L); Y = [A.f32(L), A.f32(L)]
        FULL = A.f32(16, 19); FY = [A.f32(16, 19), A.f32(16, 19)]
        hist = [A.f32(128), A.f32(128)]
        t16 = A.f32(16)
        pst = A.f32(512)
        nrow = A.f32(512)
        newc = A.f32(64); rnewc = Res()
        rPG, rY, rFULL, rFY, rhist, rt16, rpst, rnrow = Res(), [Res(), Res()], Res(), [Res(), Res()], [Res(), Res()], Res(), Res(), Res()
        P.op("pool", [MEMSET(PG[:, 0:16], 0.0), MEMSET(Y[0][:, 0:16], 0.0), MEMSET(Y[1][:, 0:16], 0.0)], writes=[rPG, rY[0], rY[1]])
        self.wload(WS[0], Win[:, :, 1536:2048], rWS[0])
        psS, rpsS = self.ps[7], self.rps[7]
        psN, rpsN = self.ps[6], self.rps[6]
        self._bank = 0
        wins = [2, 4, 8, 16]
        for g in range(4):
            w = wins[g]
            for tb, (c0, n) in enumerate(TB):
                ps, rp = self.ps[tb % 4], self.rps[tb % 4]
                P.op("pe", [MM(ps[:, 0:n], WS[0][:, c, g * 128:(g + 1) * 128], self.XN[:, c, c0:c0 + n], start=(c == 0), stop=(c == 7))
                            for c in range(8)], reads=[rWS[0], self.rXN[tb]], writes=[rp])
                if tb < 4:
                    P.op("act", ACP(PG[:, 16 + c0:16 + c0 + n], ps[:, 0:n]), reads=[rp], writes=[rPG])
                else:
                    P.op("act", ACP(FULL[:, :, 15:19], ps[:, 0:64].rearrange("p (s t) -> p s t", t=4)), reads=[rp], writes=[rFULL])
            for half in range(2):
                self.load(hist[half][0:120], I["spool"][i, half * 120:(half + 1) * 120, g * 128:(g + 1) * 128], rhist[half])
                ps, rp = self.ps[4 + half], self.rps[4 + half]
                P.op("pe", TR(ps[:, 0:120], hist[half][0:120, :], self.identf[0:120, 0:120]), reads=[rhist[half], self.rC], writes=[rp])
                P.op("dve", CP(FULL[:, half * 8:(half + 1) * 8, 0:15], ps[:, 0:120].rearrange("p (s t) -> p s t", t=15)), reads=[rp], writes=[rFULL])
            src, rsrc = PG, rPG
            s = 1
            k = 0
            while s < w:
                dst, rdst = Y[k % 2], rY[k % 2]
                P.op("dve", TT(dst[:, 16:L], src[:, 16:L], src[:, 16 - s:L - s], ALU.add), reads=[rsrc], writes=[rdst])
                src, rsrc = dst, rdst
                s *= 2
                k += 1
            P.op("dve", STT(POOLED[:, g, 0:2048], src[:, 16:L], 1.0 / w, PG[:, 16:L], ALU.mult, ALU.subtract), reads=[rsrc, rPG], writes=[rPOOLED[g]])
            P.op("dve", TT(t16, src[:, 16:32], self.invcnt[:, g, :], ALU.mult), reads=[rsrc, self.rC], writes=[rt16])
            P.op("dve", TT(POOLED[:, g, 0:16], t16, PG[:, 16:32], ALU.subtract), reads=[rt16, rPG], writes=[rPOOLED[g]])
            P.op("pe", TR(psS[0:16, g * 128:(g + 1) * 128], PG[:, L - 16:L], self.identf[:, :]), reads=[rPG, self.rC], writes=[rpsS])
            src, rsrc = FULL, rFULL
            s = 1
            k = 0
            lo = 0
            while s < w:
                dst, rdst = FY[k % 2], rFY[k % 2]
                lo2 = lo + s
                P.op("dve", TT(dst[:, :, lo2:19], src[:, :, lo2:19], src[:, :, lo2 - s:19 - s], ALU.add), reads=[rsrc], writes=[rdst])
                src, rsrc = dst, rdst
                lo = lo2
                s *= 2
                k += 1
            P.op("dve", STT(POOLED[:, g, 2048:2112].rearrange("p (s t) -> p s t", t=4), src[:, :, 15:19], 1.0 / w, FULL[:, :, 15:19],
                            ALU.mult, ALU.subtract), reads=[rsrc, rFULL], writes=[rPOOLED[g]])
            P.op("dve", CP(newc[:, :].rearrange("p (s t) -> p s t", t=4), FULL[:, :, 15:19]), reads=[rFULL], writes=[rnewc])
            P.op("pe", TR(psN[0:64, g * 128:(g + 1) * 128], newc[:, :], self.identf[:, :]), reads=[rnewc, self.rC], writes=[rpsN])
        P.op("act", ACP(pst[0:16, :], psS[0:16, :]), reads=[rpsS], writes=[rpst])
        self.store(O["pps"][i], pst[1:16, :], rpst)
        P.op("act", ACP(nrow[0:64, :], psN[0:64, :]), reads=[rpsN], writes=[rnrow])
        for s_ in range(16):
            self.store(O["sps"][i, s_, 11:15, :], nrow[4 * s_:4 * s_ + 4, :], rnrow)
        rd2d = Res()
        P.op("sp", DMA(O["sps"][i, :, 0:11, :], I["spool"][i].rearrange("(s r) f -> s r f", r=15)[:, 4:15, :]), dma=rd2d, store=True)
        A.release(m1)
        P.barrier()
        self.stage("pool")
        QT = A.bf16(4, 2048); KT = A.bf16(4, 2048); V = A.bf16(16, 512)
        rQT = [Res() for _ in range(17)]; rKT = [Res() for _ in range(17)]; rV = [Res() for _ in range(17)]
        raw = [A.f32(512)] * 2; rot = [A.f32(512)] * 2
        ta = [A.f32(256)] * 2; tb_ = [A.f32(256)] * 2
        xb = [A.bf16(512)] * 2
        rraw, rrot, rta, rtb, rxb = ([Res()] * 2 for _ in range(5))
        self._bank = 0
        for pi, pname in enumerate(["q", "k", "v"]):
            sl = (pi + 1) % 2
            self.wload(WS[sl], Win[:, :, pi * 512:(pi + 1) * 512], rWS[sl])
            for ti, (c0, r) in enumerate(TILES):
                tbi = min(c0 // 512, 4)
                b = ti % 2
                ps, rp = self.bank()
                P.op("pe", [MM(ps[0:r, :], self.XN[:, c, c0:c0 + r], WS[sl][:, c, :], start=(c == 0), stop=(c == 7)) for c in range(8)],
                     reads=[rWS[sl], self.rXN[tbi]], writes=[rp])
                P.op("act", ACP(raw[b][0:r, :], ps[0:r, :]), reads=[rp], writes=[rraw[b]])
                if pname == "v":
                    dst = O["pv"][i, c0:c0 + r, :] if ti < 16 else O["sv"][i]
                    self.store(dst, raw[b][0:r, :], rraw[b])
                    P.op("dve", CP(V[0:r, ti, :] if ti < 16 else Vs[0:64, :], raw[b][0:r, :]), reads=[rraw[b]], writes=[rV[ti]])
                    continue
                r3 = raw[b][0:r, :].rearrange("p (h d) -> p h d", h=8)
                o3 = rot[b][0:r, :].rearrange("p (h d) -> p h d", h=8)
                x1, x2, o1, o2 = r3[:, :, 0:32], r3[:, :, 32:64], o3[:, :, 0:32], o3[:, :, 32:64]
                cb = self.cosT[0:r, ti, :].unsqueeze(1).to_broadcast([r, 8, 32])
                sb_ = self.sinT[0:r, ti, :].unsqueeze(1).to_broadcast([r, 8, 32])
                ta3 = ta[b][0:r, :].rearrange("p (h d) -> p h d", h=8)
                tb3 = tb_[b][0:r, :].rearrange("p (h d) -> p h d", h=8)
                P.op("dve", TT(o1, x1, cb, ALU.mult), reads=[rraw[b], self.rC], writes=[rrot[b]])
                P.op("dve", TT(ta3, x2, sb_, ALU.mult), reads=[rraw[b], self.rC], writes=[rta[b]])
                P.op("dve", TT(o1, o1, ta3, ALU.subtract), reads=[rrot[b], rta[b]], writes=[rrot[b]])
                P.op("pool", TT(o2, x2, cb, ALU.mult), reads=[rraw[b], self.rC], writes=[rrot[b]])
                P.op("pool", TT(tb3, x1, sb_, ALU.mult), reads=[rraw[b], self.rC], writes=[rtb[b]])
                P.op("pool", TT(o2, o2, tb3, ALU.add), reads=[rrot[b], rtb[b]], writes=[rrot[b]])
                if pname == "k":
                    dst = O["pk"][i, c0:c0 + r, :] if ti < 16 else O["sk"][i]
                    self.store(dst, rot[b][0:r, :], rrot[b])
                P.op("act", ACP(xb[b][0:r, :], rot[b][0:r, :]), reads=[rrot[b]], writes=[rxb[b]])
                ps2, rp2 = self.bank()
                psb = ps2[:, 0:256].bitcast(BF16)
                P.op("pe", [TR(psb[:, pr * 128:pr * 128 + r], xb[b][0:r, pr * 128:(pr + 1) * 128], self.identb[0:r, 0:r]) for pr in range(4)],
                     reads=[rxb[b], self.rC], writes=[rp2])
                dstT = (QT if pname == "q" else KT)
                rdT = (rQT if pname == "q" else rKT)
                dst_ap = dstT[:, :, c0:c0 + r] if ti < 16 else (QsT if pname == "q" else KsT)[:, :, 0:64]
                P.op("dve", CP(dst_ap, psb.rearrange("p (k t) -> p k t", k=4)[:, :, 0:r]), reads=[rp2], writes=[rdT[ti]])
        P.barrier()
        self.stage("qkv")
        CAT = self.XN
        rCAT = [Res() for _ in range(17)]
        self.wload(wp, I["w_pool"][i].rearrange("g c d -> c g d"), rwp)
        self._bank = 0
        for g in range(4):
            for tb, (c0, n) in enumerate(TB):
                ps, rp = self.bank()
                P.op("pe", MM(ps[:, 0:n], wp[:, g, :], POOLED[:, g, c0:c0 + n]), reads=[rwp, rPOOLED[g]], writes=[rp])
                P.op("act", ACT(CAT[:, 4 + g, c0:c0 + n], ps[:, 0:n], AF.Copy, scale=self.pscale[:, i, g:g + 1]), reads=[rp, self.rC],
                     writes=[rCAT[t] for t in range(17) if min(TILES[t][0] // 512, 4) == tb])
        self.stage("wpool")
        P.op("dve", RED(kmf, KT[:, :, 0:2048].rearrange("p k (n t) -> p k n t", t=256), AX.X, ALU.add), reads=rKT[0:16], writes=[rkm])
        P.op("dve", TS(kmT, kmf, 1.0 / 256, None, ALU.mult), reads=[rkm], writes=[rkm])
        for qt in range(16):
            nq = qt // 2
            c0 = qt * 128
            sel = nq >= 4
            tiles = []
            for kt in range(qt + 1):
                own = kt >= 2 * nq
                tiles.append(dict(KT=(lambda h, kt=kt: KT[(h % 2) * 64:(h % 2) * 64 + 64, h // 2, kt * 128:(kt + 1) * 128]),
                                  V=(lambda h, kt=kt: V[:, kt, h * 64:(h + 1) * 64]), nk=128,
                                  slot=(kt // 2 if (sel and not own) else (7 if sel else 0)),
                                  mask=(self.trib[:, :] if kt == qt else None), res=[rKT[kt], rV[kt]]))
            self.attend(128, lambda h, c0=c0: QT[(h % 2) * 64:(h % 2) * 64 + 64, h // 2, c0:c0 + 128], [rQT[qt]], tiles,
                        sel, nq, kmT, rkm, self.cand[:, nq - 4, :] if sel else None, CAT, c0, rCAT[qt])
            self.stage("attn%d" % qt)
        self.stage("pattn")
        P.barrier()
        A.release(mP)
        if self.do_sample_attn:
            Kgf = [A.bf16(8192) for _ in range(2)]; rKg = [Res(), Res()]
            Kg = [k_.rearrange("p (j f) -> p j f", j=16) for k_ in Kgf]
            Vgf = [A.bf16(8192) for _ in range(2)]; rVg = [Res(), Res()]
            Vg = [v_.rearrange("p (j f) -> p j f", j=16) for v_ in Vgf]
            KgT = A.bf16(4, 2048); rKgT = Res()
            ck = I["ck%d" % i].rearrange("(r two) f -> r (two f)", two=2); cv = I["cv%d" % i].rearrange("(r two) f -> r (two f)", two=2)

            def gatherK(s):
                if DBG.get("nogather"):
                    return
                for nb in range(8):
                    P.op("pool", IDMA(Kgf[s % 2][:, nb * 1024:(nb + 1) * 1024], ck, self.pidx2[:, s * 8 + nb:s * 8 + nb + 1]), reads=[self.rC], writes=[rKg[s % 2]], dma=rKg[s % 2])
                for nb in range(8):
                    P.op("pool", IDMA(Vgf[s % 2][:, nb * 1024:(nb + 1) * 1024], cv, self.pidx2[:, s * 8 + nb:s * 8 + nb + 1]), reads=[self.rC], writes=[rVg[s % 2]], dma=rVg[s % 2])
            gatherK(0)
            for s in range(16):
                if s + 1 < 16:
                    gatherK(s + 1)
                for pr in range(4):
                    for half in range(2):
                        ps, rp = self.ps[6 + (pr * 2 + half) % 2], self.rps[6 + (pr * 2 + half) % 2]
                        psb = ps[:, :].bitcast(BF16)
                        P.op("pe", [TR(psb[:, k * 128:(k + 1) * 128], Kg[s % 2][:, half * 8 + k, pr * 128:(pr + 1) * 128], self.identb[:, :]) for k in range(8)],
                             reads=[rKg[s % 2], self.rC], writes=[rp])
                        P.op("act" if half == 0 else "dve", (ACP if half == 0 else CP)(KgT[:, pr, half * 1024:(half + 1) * 1024], psb), reads=[rp], writes=[rKgT])
                P.op("dve", RED(kmf, KgT[:, :, :].rearrange("p k (n t) -> p k n t", t=256), AX.X, ALU.add), reads=[rKgT], writes=[rkm])
                P.op("dve", TS(kmT, kmf, 1.0 / 256, None, ALU.mult), reads=[rkm], writes=[rkm])
                tiles = []
                for kt in range(16):
                    tiles.append(dict(KT=(lambda h, kt=kt: KgT[(h % 2) * 64:(h % 2) * 64 + 64, h // 2, kt * 128:(kt + 1) * 128]),
                                      V=(lambda h, kt=kt, s=s: Vg[s % 2][:, kt, h * 64:(h + 1) * 64]), nk=128, slot=kt // 2, mask=None, res=[rKgT, rVg[s % 2]]))
                tiles.append(dict(KT=(lambda h: KsT[(h % 2) * 64:(h % 2) * 64 + 64, h // 2, 0:64]),
                                  V=(lambda h: Vs[0:64, h * 64:(h + 1) * 64]), nk=64, slot=8,
                                  mask=self.smaskb[0:64, 4 * s:4 * s + 4], res=[rKT[16], rV[16]]))
                c0 = 2048 + 4 * s
                if DBG.get("noattn"):
                    continue
                self.attend(4, lambda h, s=s: QsT[(h % 2) * 64:(h % 2) * 64 + 64, h // 2, 4 * s:4 * s + 4], [rQT[16]], tiles,
                            True, 8, kmT, rkm, None, CAT, c0, rCAT[16])
        else:
            P.op("pool", MEMSET(CAT[:, 0:4, 2048:2112], 0.0), writes=[rCAT[16]])
        A.release(m0)
        P.barrier()
        WS = [A.bf16(8, 512) for _ in range(2)]
        rWS = [Res(), Res()]
        Wo = I["w_o_even"][i].rearrange("(c p) f -> p c f", p=128)
        self.out_proj(Wo, WS, rWS, CAT, rCAT)
        A.release(m0)
        P.barrier()

    def out_proj(self, Wo, WS, rWS, SRC, rSRC17):
        P = self.P
        self._bank = 0
        self.wload(WS[0], Wo[:, :, 0:512], rWS[0])
        self.wload(WS[1], Wo[:, :, 512:1024], rWS[1])
        for ph in range(2):
            for tb, (c0, n) in enumerate(TB):
                rs = [rSRC17[t] for t in range(17) if min(TILES[t][0] // 512, 4) == tb]
                for dcl in range(4):
                    dc = ph * 4 + dcl
                    ps, rp = self.bank()
                    P.op("pe", [MM(ps[:, 0:n], WS[ph][:, k, dcl * 128:(dcl + 1) * 128], SRC[:, k, c0:c0 + n], start=(k == 0), stop=(k == 7)) for k in range(8)],
                         reads=[rWS[ph]] + rs, writes=[rp])
                    P.op("dve", TT(self.RES[:, dc, c0:c0 + n], self.RES[:, dc, c0:c0 + n], ps[:, 0:n], ALU.add), reads=[rp, self.rRES[tb]], writes=[self.rRES[tb]])

    def attn_setup(self):
        A = self.A
        self.PT = [A.bf16(512) for _ in range(2)]; self.rPT = [Res(), Res()]
        self.acc = A.f32(8, 65); self.racc = Res()
        self.gs = A.f32(64); self.mx = A.f32(8, 8); self.sel = A.f32(8, 9); self.rsel = Res()
        self.rec = A.f32(8); self.obf = A.bf16(512); self.robf = Res()
        self.ctmp = [A.f32(8, 65) for _ in range(2)]; self.cred = [A.f32(65) for _ in range(2)]; self.rct = [Res(), Res()]
        self._pt = 0
        self._ob = 0

    def attend(self, M, QTf, rQ, tiles, sel, nq, kmT, rkm, cand, CAT, c0, rcat):
        P = self.P
        S = [(self.ps[0], self.rps[0]), (self.ps[1], self.rps[1])]
        Ob = [(self.ps[2], self.rps[2]), (self.ps[3], self.rps[3])]
        Gb, rGb = self.ps[6], self.rps[6]
        acc, racc = self.acc, self.racc
        if sel and not DBG.get("nogate"):
            Gb2, rGb2 = self.ps[7], self.rps[7]
            for par, (gb_, rgb_) in enumerate([(Gb, rGb), (Gb2, rGb2)]):
                P.op("pe", [MM(gb_[0:M, (h // 2) * 8:(h // 2) * 8 + 8], QTf(h), kmT[(h % 2) * 64:(h % 2) * 64 + 64, h // 2, :])
                            for h in range(par, 8, 2)], reads=rQ + [rkm], writes=[rgb_])
            gs4 = self.gs[0:M, :].rearrange("p (k q n) -> p k q n", k=4, q=2)
            for par, (gb_, rgb_) in enumerate([(Gb, rGb), (Gb2, rGb2)]):
                src = gb_[0:M, 0:32].rearrange("p (k n) -> p k n", k=4)
                if cand is not None:
                    P.op("dve", TT(gs4[:, :, par, :], src, cand[0:M, :].rearrange("p (k q n) -> p k q n", k=4, q=2)[:, :, par, :], ALU.add),
                         reads=[rgb_, self.rC], writes=[self.rsel])
                else:
                    P.op("dve", CP(gs4[:, :, par, :], src), reads=[rgb_], writes=[self.rsel])
            for h in range(8):
                P.op("dve", MAX8(self.mx[0:M, h, :], self.gs[0:M, h * 8:(h + 1) * 8]), reads=[self.rsel], writes=[self.rsel])
            P.op("dve", TT(self.sel[0:M, :, 0:8], self.gs[0:M, :].rearrange("p (h n) -> p h n", h=8),
                           self.mx[0:M, :, 2:3].to_broadcast([M, 8, 8]), ALU.is_ge), reads=[self.rsel], writes=[self.rsel])
        G = 512 // M if M >= 128 else 16
        groups = [tiles[k:k + G] for k in range(0, len(tiles), G)]
        g2 = []
        for grp in groups:
            a = [t for t in grp if t["nk"] == 128]
            b = [t for t in grp if t["nk"] != 128]
            if a:
                g2.append(a)
            if b:
                g2.append(b)
        groups = g2
        first_in_slot = {}
        last_in_slot = {}
        for ti, t in enumerate(tiles):
            first_in_slot.setdefault(t["slot"], ti)
            last_in_slot[t["slot"]] = ti
        for h in range(8):
            ob, rob = Ob[self._ob % 2]
            Db, rDb = self.ps[4 + self._ob % 2], self.rps[4 + self._ob % 2]
            dofs = 0
            self._ob += 1
            ti = 0
            for grp in groups:
                sb_, rsb = S[self._pt % 2]
                pt, rpt = self.PT[self._pt % 2], self.rPT[self._pt % 2]
                self._pt += 1
                nk = grp[0]["nk"]
                rr = []
                for t in grp:
                    rr += t["res"]
                P.op("pe", [MM(sb_[0:nk, k * M:(k + 1) * M], t["KT"](h), QTf(h)) for k, t in enumerate(grp)], reads=rQ + rr, writes=[rsb])
                W = len(grp) * M
                P.op("act", ACT(pt[0:nk, 0:W], sb_[0:nk, 0:W], AF.Exp, scale=0.125), reads=[rsb], writes=[rpt])
                for k, t in enumerate(grp):
                    if t["mask"] is not None:
                        P.op("pool" if M >= 128 else "dve", TT(pt[0:nk, k * M:(k + 1) * M], pt[0:nk, k * M:(k + 1) * M], t["mask"], ALU.mult), reads=[rpt, self.rC], writes=[rpt])
                mms = []
                for k, t in enumerate(grp):
                    sl = t["slot"]
                    st_, sp_ = (first_in_slot[sl] == ti), (last_in_slot[sl] == ti)
                    mms.append(MM(ob[0:M, (sl % 8) * 64:(sl % 8) * 64 + 64] if sl < 8 else Db[0:M, 64 + dofs * 4:64 + dofs * 4 + 64],
                                  pt[0:nk, k * M:(k + 1) * M], t["V"](h), start=st_, stop=sp_))
                    mms.append(MM(Db[0:M, dofs + sl:dofs + sl + 1], pt[0:nk, k * M:(k + 1) * M], self.onesb[0:nk, 0:1], start=st_, stop=sp_))
                    ti += 1
                P.op("pe", mms, reads=[rpt, self.rC] + rr, writes=[rob, rDb])
            own = 8 if (sel and nq == 8) else (7 if sel else 0)
            own_ap = ob[0:M, own * 64:own * 64 + 64] if own < 8 else Db[0:M, 64 + dofs * 4:64 + dofs * 4 + 64]
            P.op("act", ACP(acc[0:M, h, 0:64], own_ap), reads=[rob, rDb], writes=[racc])
            P.op("act", ACP(acc[0:M, h, 64:65], Db[0:M, dofs + own:dofs + own + 1]), reads=[rDb], writes=[racc])
            if sel and not DBG.get("nocombo"):
                ct, cr, rct = self.ctmp[h % 2], self.cred[h % 2], self.rct[h % 2]
                P.op("dve", TT(ct[0:M, 0:nq, 0:64], ob[0:M, 0:nq * 64].rearrange("p (n d) -> p n d", d=64),
                               self.sel[0:M, h, 0:nq].unsqueeze(2).to_broadcast([M, nq, 64]), ALU.mult), reads=[rob, self.rsel], writes=[rct])
                P.op("dve", TT(ct[0:M, 0:nq, 64], Db[0:M, 0:nq], self.sel[0:M, h, 0:nq], ALU.mult), reads=[rDb, self.rsel], writes=[rct])
                P.op("dve", RED(cr[0:M, :], ct[0:M, 0:nq, :].rearrange("p n d -> p d n"), AX.X, ALU.add), reads=[rct], writes=[rct])
                P.op("dve", TT(acc[0:M, h, :], acc[0:M, h, :], cr[0:M, :], ALU.add), reads=[rct, racc], writes=[racc])
        P.op("dve", RECIP(self.rec[0:M, :], acc[0:M, :, 64]), reads=[racc], writes=[self.robf])
        P.op("dve", TT(self.obf[0:M, :].rearrange("p (h d) -> p h d", h=8), acc[0:M, :, 0:64],
                       self.rec[0:M, :].unsqueeze(2).to_broadcast([M, 8, 64]), ALU.mult), reads=[racc, self.robf], writes=[self.robf])
        ps, rp = self.ps[7], self.rps[7]
        psb = ps[:, 0:256].bitcast(BF16)
        P.op("pe", [TR(psb[:, pr * 128:pr * 128 + M], self.obf[0:M, pr * 128:(pr + 1) * 128], self.identb[0:M, 0:M]) for pr in range(4)],
             reads=[self.robf, self.rC], writes=[rp])
        P.op("act", ACP(CAT[:, 0:4, c0:c0 + M], psb.rearrange("p (k t) -> p k t", k=4)[:, :, 0:M]), reads=[rp], writes=[rcat])

    def odd_layer(self, l):
        P, A, I, O = self.P, self.A, self.I, self.O
        j = l // 2
        self.rmsnorm_all(self.nmix[:, l, :])
        Win = I["w_in_odd"][j].rearrange("(c p) f -> p c f", p=128)
        m0 = A.mark()
        WS = [A.bf16(8, 512) for _ in range(2)]; rWS = [Res(), Res()]
        U = A.bf16(8, NT); rU = [Res() for _ in range(17)]
        lng = A.f32(1024); lnb = A.f32(1024); rln = Res()
        self.load(lng, I["ln_g"][j:j + 1, :].partition_broadcast(128), rln)
        self.load(lnb, I["ln_b"][j:j + 1, :].partition_broadcast(128), rln)
        wtf = A.f32(8, 128); WTm = A.bf16(8, 128); rwt = Res()
        self.load(wtf, I["sgu_wT"][j].rearrange("g s t -> s g t"), rwt)
        P.op("dve", TT(WTm, wtf, self.trif.unsqueeze(1).to_broadcast([128, 8, 128]), ALU.mult), reads=[rwt, self.rC], writes=[rwt])
        sbb = A.bf16(8, 128); rsbb = Res()
        self.wload(sbb[0:1], I["sgu_b"][j:j + 1].rearrange("o g t -> o g t"), rsbb)
        wsm = A.f32(8, 4, 4); bsm = A.f32(8, 4); rsm = Res()
        for g_ in range(8):
            self.load(wsm[:, g_, :, :], I["sgu_w"][j][g_, 0:4, 0:4].partition_broadcast(128), rsm)
        self.load(bsm, I["sgu_b"][j][:, 0:4].partition_broadcast(128), rsm)
        self._bank = 0
        for ph in range(2):
            self.wload(WS[ph], Win[:, :, ph * 512:(ph + 1) * 512], rWS[ph])
        for ph in range(2):
            for tb, (c0, n) in enumerate(TB):
                for fl in range(4):
                    fc = ph * 4 + fl
                    ps, rp = self.bank()
                    P.op("pe", [MM(ps[:, 0:n], WS[ph][:, c, fl * 128:(fl + 1) * 128], self.XN[:, c, c0:c0 + n], start=(c == 0), stop=(c == 7)) for c in range(8)],
                         reads=[rWS[ph], self.rXN[tb]], writes=[rp])
                    P.op("act", ACT(U[:, fc, c0:c0 + n], ps[:, 0:n], AF.Gelu_apprx_tanh), reads=[rp],
                         writes=[rU[t] for t in range(17) if min(TILES[t][0] // 512, 4) == tb])
        for ph in range(2):
            self.wload(WS[ph], Win[:, :, 1024 + ph * 512:1024 + (ph + 1) * 512], rWS[ph])
        vg = [A.f32(1024) for _ in range(2)]; rvg = [Res(), Res()]
        vln = [A.f32(1024) for _ in range(2)]; rvln = [Res(), Res()]
        vlb = [A.bf16(1024) for _ in range(2)]; rvlb = [Res(), Res()]
        st1 = [A.f32(4) for _ in range(2)]; rst = [Res(), Res()]
        vT = A.f32(8, 64); mixT = A.f32(8, 64); rvT = Res(); rmix = Res()
        for ti, (c0, r) in enumerate(TILES):
            b = ti % 2
            tbi = min(c0 // 512, 4)
            for ph in range(2):
                ps, rp = self.bank()
                P.op("pe", [MM(ps[0:r, :], self.XN[:, c, c0:c0 + r], WS[ph][:, c, :], start=(c == 0), stop=(c == 7)) for c in range(8)],
                     reads=[rWS[ph], self.rXN[tbi]], writes=[rp])
                P.op("act", ACT(vg[b][0:r, ph * 512:(ph + 1) * 512], ps[0:r, :], AF.Gelu_apprx_tanh), reads=[rp], writes=[rvg[b]])
            s1 = st1[b]
            P.op("dve", RED(s1[0:r, 0:1], vg[b][0:r, :], AX.X, ALU.add), reads=[rvg[b]], writes=[rst[b]])
            P.op("dve", TS(s1[0:r, 1:2], s1[0:r, 0:1], 1.0 / 1024, None, ALU.mult), reads=[rst[b]], writes=[rst[b]])
            P.op("dve", TS(vg[b][0:r, :], vg[b][0:r, :], s1[0:r, 1:2], None, ALU.subtract), reads=[rst[b], rvg[b]], writes=[rvg[b]])
            P.op("act", ACT(vln[b][0:r, :], vg[b][0:r, :], AF.Square, accum_out=s1[0:r, 2:3]), reads=[rvg[b], rst[b]], writes=[rvln[b], rst[b]])
            P.op("act", ACT(s1[0:r, 3:4], s1[0:r, 2:3], AF.Sqrt, bias=1e-5, scale=1.0 / 1024), reads=[rst[b]], writes=[rst[b]])
            P.op("dve", RECIP(s1[0:r, 3:4], s1[0:r, 3:4]), reads=[rst[b]], writes=[rst[b]])
            P.op("dve", STT(vln[b][0:r, :], vg[b][0:r, :], s1[0:r, 3:4], lng[0:r, :], ALU.mult, ALU.mult), reads=[rvg[b], rst[b], rln], writes=[rvln[b]])
            P.op("pool", TT(vln[b][0:r, :], vln[b][0:r, :], lnb[0:r, :], ALU.add), reads=[rvln[b], rln], writes=[rvln[b]])
            if ti == 16:
                self.store(O["sgv"][j], vln[b][0:64, :], rvln[b])
                for half in range(2):
                    ps, rp = self.bank()
                    P.op("pe", [TR(ps[:, k * 64:(k + 1) * 64], vln[b][0:64, (half * 4 + k) * 128:(half * 4 + k + 1) * 128], self.identf[0:64, 0:64]) for k in range(4)],
                         reads=[rvln[b], self.rC], writes=[rp])
                    P.op("act", ACP(vT[:, half * 4:half * 4 + 4, :], ps[:, 0:256].rearrange("p (k t) -> p k t", k=4)), reads=[rp], writes=[rvT])
                v4 = vT[:, :, :].rearrange("p g (s t) -> p g s t", t=4)
                m4 = mixT[:, :, :].rearrange("p g (s t) -> p g s t", t=4)
                for g in range(8):
                    for t in range(4):
                        P.op("dve", TS(m4[:, g, :, t], v4[:, g, :, 0], wsm[:, g, t, 0:1], bsm[:, g, t:t + 1], ALU.mult, ALU.add), reads=[rvT, rsm], writes=[rmix])
                        for s_ in range(1, t + 1):
                            P.op("dve", STT(m4[:, g, :, t], v4[:, g, :, s_], wsm[:, g, t, s_:s_ + 1], m4[:, g, :, t], ALU.mult, ALU.add), reads=[rvT, rsm, rmix], writes=[rmix])
                P.op("dve", TT(U[:, :, 2048:2112], U[:, :, 2048:2112], mixT[:, :, :], ALU.mult), reads=[rmix, rU[16]], writes=[rU[16]])
                continue
            P.op("act", ACP(vlb[b][:, :], vln[b][:, :]), reads=[rvln[b]], writes=[rvlb[b]])
            for half in range(2):
                ps, rp = self.bank()
                mms = []
                for k in range(4):
                    g = half * 4 + k
                    mms.append(MM(ps[:, k * 128:(k + 1) * 128], vlb[b][:, g * 128:(g + 1) * 128], WTm[:, g, :], start=True, stop=False))
                    mms.append(MM(ps[:, k * 128:(k + 1) * 128], self.onesb[0:1, :], sbb[0:1, g, :], start=False, stop=True))
                P.op("pe", mms, reads=[rvlb[b], rwt, rsbb, self.rC], writes=[rp])
                P.op("dve", TT(U[:, half * 4:half * 4 + 4, c0:c0 + 128], U[:, half * 4:half * 4 + 4, c0:c0 + 128],
                               ps[:, :].rearrange("p (k t) -> p k t", k=4), ALU.mult), reads=[rp, rU[ti]], writes=[rU[ti]])
        Wo = I["w_o_odd"][j].rearrange("(c p) f -> p c f", p=128)
        self.out_proj(Wo, WS, rWS, U, rU)
        A.release(m0)
        P.barrier()

    def ffn(self, l):
        P, A, I = self.P, self.A, self.I
        self.rmsnorm_all(self.nffn[:, l, :])
        m0 = A.mark()
        WU = [A.bf16(8, 512) for _ in range(2)]; WD = [A.bf16(4, 1024) for _ in range(2)]
        rW = [Res(), Res()]
        H = [A.bf16(4, 512) for _ in range(2)]; rH = [Res(), Res()]
        tmp = [A.f32(512) for _ in range(2)]; rtmp = [Res(), Res()]
        Wup = I["w_up"][l].rearrange("(c p) f -> p c f", p=128)
        Wdn = I["w_dn"][l].rearrange("(c p) f -> p c f", p=128)

        def wl(fg):
            s = fg % 2
            P.op("pool", [DMA(WU[s], Wup[:, :, fg * 512:(fg + 1) * 512]), DMA(WD[s], Wdn[:, fg * 4:(fg + 1) * 4, :])], writes=[rW[s]], dma=rW[s])
        self._bank = 0
        st = dict(kk=0)

        def up(k, fg, tb):
            s = fg % 2
            c0, n = TB[tb]
            hb = k % 2
            for fc in range(4):
                ps, rp = self.bank()
                P.op("pe", [MM(ps[:, 0:n], WU[s][:, c, fc * 128:(fc + 1) * 128], self.XN[:, c, c0:c0 + n], start=(c == 0), stop=(c == 7)) for c in range(8)],
                     reads=[rW[s], self.rXN[tb]], writes=[rp])
                tb_ = st["kk"] % 2
                st["kk"] += 1
                P.op("act", ACT(tmp[tb_][:, 0:n], ps[:, 0:n], AF.Relu), reads=[rp], writes=[rtmp[tb_]])
                P.op("pool", TT(H[hb][:, fc, 0:n], tmp[tb_][:, 0:n], tmp[tb_][:, 0:n], ALU.mult), reads=[rtmp[tb_]], writes=[rH[hb]])

        def down(k, fg, tb):
            s = fg % 2
            c0, n = TB[tb]
            hb = k % 2
            for dc in range(8):
                ps, rp = self.bank()
                P.op("pe", [MM(ps[:, 0:n], WD[s][:, fc, dc * 128:(dc + 1) * 128], H[hb][:, fc, 0:n], start=(fc == 0), stop=(fc == 3)) for fc in range(4)],
                     reads=[rW[s], rH[hb]], writes=[rp])
                P.op("dve", TT(self.RES[:, dc, c0:c0 + n], self.RES[:, dc, c0:c0 + n], ps[:, 0:n], ALU.add), reads=[rp, self.rRES[tb]], writes=[self.rRES[tb]])

        pairs = [(fg, tb) for fg in range(8) for tb in range(5)]
        wl(0)
        wl(1)
        up(0, *pairs[0])
        for k, (fg, tb) in enumerate(pairs):
            if k + 1 < len(pairs):
                up(k + 1, *pairs[k + 1])
            down(k, fg, tb)
            if tb == 4 and fg + 2 < 8:
                wl(fg + 2)
        A.release(m0)
        P.barrier()

    def final(self):
        P, A, O = self.P, self.A, self.O
        m0 = A.mark()
        yt = [A.f32(1024) for _ in range(2)]; ryt = [Res(), Res()]
        yo = [A.f32(1024) for _ in range(2)]; ryo = [Res(), Res()]
        junk = A.f32(1024); rjunk = Res()
        st_ = [A.f32(2) for _ in range(2)]; rst = [Res(), Res()]
        self._bank = 0
        for ti, (c0, r) in enumerate(TILES):
            b = ti % 2
            tbi = min(c0 // 512, 4)
            for half in range(2):
                ps, rp = self.bank()
                P.op("pe", [TR(ps[0:r, k * 128:(k + 1) * 128], self.RES[:, half * 4 + k, c0:c0 + r], self.identf[:, :]) for k in range(4)],
                     reads=[self.rRES[tbi], self.rC], writes=[rp])
                P.op("act" if half == 0 else "dve", (ACP if half == 0 else CP)(yt[b][0:r, half * 512:(half + 1) * 512], ps[0:r, :]), reads=[rp], writes=[ryt[b]])
            P.op("act", ACT(junk[0:r, :], yt[b][0:r, :], AF.Square, accum_out=st_[b][0:r, 0:1]), reads=[ryt[b]], writes=[rjunk, rst[b]])
            P.op("act", ACT(st_[b][0:r, 1:2], st_[b][0:r, 0:1], AF.Sqrt, bias=1e-6, scale=1.0 / 1024), reads=[rst[b]], writes=[rst[b]])
            P.op("dve", RECIP(st_[b][0:r, 1:2], st_[b][0:r, 1:2]), reads=[rst[b]], writes=[rst[b]])
            P.op("dve", STT(yo[b][0:r, :], yt[b][0:r, :], st_[b][0:r, 1:2], self.nfin[0:r, :], ALU.mult, ALU.mult), reads=[ryt[b], rst[b], self.rC], writes=[ryo[b]])
            dst = O["yp"][c0:c0 + r, :] if ti < 16 else O["ys"]
            self.store(dst, yo[b][0:r, :], ryo[b])
        A.release(m0)


def _host_consts():
    pos = np.zeros((128, 17), np.float32)
    for i in range(16):
        pos[:, i] = i * 128 + np.arange(128)
    pos[:, 16] = 2048 + (np.arange(128) % 4)
    inv = (10000.0 ** (-np.arange(0, 64, 2, dtype=np.float32) / 64)).astype(np.float32)
    ang = pos[:, :, None].astype(np.float32) * inv[None, None, :]
    cosT = np.cos(ang).astype(np.float32)
    sinT = np.sin(ang).astype(np.float32)
    k = np.arange(128)
    tri = (k[:, None] <= k[None, :]).astype(np.float32)
    kk = np.arange(64)
    smask = np.zeros((64, 64), np.float32)
    for s in range(16):
        for q in range(4):
            smask[:, 4 * s + q] = ((kk // 4 == s) & (kk % 4 <= q)).astype(np.float32)
    cand = np.zeros((128, 4, 64), np.float32)
    for a, nq in enumerate([4, 5, 6, 7]):
        c = np.where(np.arange(8) < nq, 0.0, -1e30).astype(np.float32)
        cand[:, a, :] = np.tile(c, 8)[None, :]
    invcnt = np.zeros((128, 4, 16), np.float32)
    for g, w in enumerate([2, 4, 8, 16]):
        invcnt[:, g, :] = (1.0 / np.minimum(np.arange(16) + 1, w)).astype(np.float32)[None, :]
    return dict(cosT=cosT, sinT=sinT, tri=tri, smask=smask, cand=cand, invcnt=invcnt,
                iota=np.arange(128, dtype=np.float32)[:, None], iota2=(np.arange(128) % 64).astype(np.float32)[:, None], ident=np.eye(128, dtype=np.float32),
                ones=np.ones((128, 128), np.float32))


def _core_inputs(c, inp, ck, cv, ptab_c, consts):
    f = np.ascontiguousarray
    d = dict(consts)
    d["xp"] = f(inp["x_prompt"][c])
    d["xs"] = f(inp["x_sample"][16 * c:16 * c + 16].reshape(64, 1024))
    d["ck0"], d["ck1"], d["cv0"], d["cv1"] = ck[0], ck[1], cv[0], cv[1]
    d["spool"] = f(inp["state_pool"][:, 16 * c:16 * c + 16].reshape(2, 240, 512))
    d["ptab"] = f(ptab_c.reshape(1, 256).astype(np.int32))
    d["nmix"] = f(inp["norm_mix"].reshape(4, 8, 128).transpose(2, 0, 1))
    d["nffn"] = f(inp["norm_ffn"].reshape(4, 8, 128).transpose(2, 0, 1))
    d["nfin"] = f(inp["norm_final"].reshape(1, 1024))
    d["pscale"] = f(inp["pool_scale"].reshape(2, 4, 128).transpose(2, 0, 1))
    for a, b in [("w_in_even", "w_in_even"), ("w_o_even", "w_o_even"), ("w_pool", "w_pool"), ("w_in_odd", "w_in_odd"),
                 ("ln_g", "sgu_ln_g"), ("ln_b", "sgu_ln_b"), ("sgu_w", "sgu_w"), ("sgu_b", "sgu_b"), ("w_o_odd", "w_o_odd"),
                 ("w_up", "w_ffn_up"), ("w_dn", "w_ffn_down")]:
        d[a] = inp[b]
    d["sgu_wT"] = f(inp["sgu_w"].transpose(0, 1, 3, 2))
    return d


_NC_CACHE = {}


def _get_nc(npg, **kw):
    key = (npg, tuple(sorted(kw.items())))
    if key not in _NC_CACHE:
        _NC_CACHE[key] = Builder(npg, **kw).build()
    return _NC_CACHE[key]


def _assemble(results, cores):
    n = len(cores)
    yp = np.stack([r["yp"] for r in results])
    ys = np.concatenate([r["ys"].reshape(16, 4, 1024) for r in results])
    pk = np.stack([r["pk"].reshape(2, 2048, 8, 64) for r in results], axis=1)
    pv = np.stack([r["pv"].reshape(2, 2048, 8, 64) for r in results], axis=1)
    sk = np.concatenate([r["sk"].reshape(2, 16, 4, 8, 64) for r in results], axis=1)
    sv = np.concatenate([r["sv"].reshape(2, 16, 4, 8, 64) for r in results], axis=1)
    pps = np.stack([r["pps"] for r in results], axis=1)
    sps = np.concatenate([r["sps"] for r in results], axis=1)
    sgv = np.concatenate([r["sgv"].reshape(2, 16, 4, 1024) for r in results], axis=1)
    return (yp, ys, pk, pv, sk, sv, pps, sps, sgv)


def kernel(**inp):
    inp = {k: np.asarray(v) for k, v in inp.items()}
    consts = _host_consts()
    npg = inp["cache_k"].shape[1]
    ck = [np.ascontiguousarray(inp["cache_k"][i].reshape(npg * 128, 512)) for i in range(2)]
    cv = [np.ascontiguousarray(inp["cache_v"][i].reshape(npg * 128, 512)) for i in range(2)]
    nc = _get_nc(npg)
    in_maps = [_core_inputs(c, inp, ck, cv, inp["page_table"][16 * c:16 * c + 16], consts) for c in range(8)]
    res = run_bass_kernel_spmd(nc, in_maps, core_ids=list(range(8)))
    return _assemble(res.results, list(range(8)))
```

```python
import numpy as np
import concourse.bass as bass
import concourse.mybir as mybir
from concourse.bass_utils import run_bass_kernel_spmd
from contextlib import ExitStack

F32 = mybir.dt.float32
BF16 = mybir.dt.bfloat16
I32 = mybir.dt.int32
ALU = mybir.AluOpType
AF = mybir.ActivationFunctionType
AX = mybir.AxisListType

NT = 2112
TB = [(0, 512), (512, 512), (1024, 512), (1536, 512), (2048, 64)]
TILES = [(i * 128, 128) for i in range(16)] + [(2048, 64)]
NEG = -30000.0
import os
DBG = {k: 1 for k in os.environ.get('KDBG', '').split(',') if k}


class Res:
    __slots__ = ("name", "wr", "acc", "sem", "cnt", "excl")

    def __init__(self, name="", excl=False):
        self.name = name
        self.excl = excl
        self.wr = {}
        self.acc = {}
        self.sem = None
        self.cnt = 0


class Op:
    __slots__ = ("eng", "fns", "deps", "seq", "tkey", "order", "val", "target", "dma", "cclock", "waits")


class Prog:
    ENGS = ["pe", "act", "dve", "pool", "sp"]

    def __init__(self, nc):
        self.nc = nc
        self.ops = []
        self.per_eng = {e: [] for e in self.ENGS}
        self.stores = []
        self.last_by_track = {}
        self.pending = {e: None for e in self.ENGS}

    def barrier(self):
        snap = list(self.last_by_track.values())
        for e in self.ENGS:
            self.pending[e] = snap

    def op(self, eng, fns, reads=(), writes=(), dma=None, store=False):
        o = Op()
        o.eng = eng
        o.fns = list(fns) if isinstance(fns, (list, tuple)) else [fns]
        deps = []
        if self.pending[eng] is not None:
            deps.extend(self.pending[eng])
            self.pending[eng] = None
        reads = list(reads)
        writes = list(writes)
        for r in reads:
            if r.excl and r not in writes:
                writes.append(r)
        for r in reads:
            deps.extend(r.wr.values())
        for w in writes:
            deps.extend(w.acc.values())
        o.deps = deps
        o.dma = dma
        o.target = False
        o.waits = []
        o.seq = len(self.per_eng[eng])
        if dma is not None:
            dma.cnt += 16 * len(o.fns)
            o.tkey = ("dma", id(dma))
            o.order = dma.cnt
            o.val = dma.cnt
        else:
            o.tkey = eng
            o.order = o.seq
            o.val = None
        for r in reads:
            r.acc[o.tkey] = o
        for w in writes:
            w.wr[o.tkey] = o
            w.acc[o.tkey] = o
        self.per_eng[eng].append(o)
        self.ops.append(o)
        self.last_by_track[o.tkey] = o
        if store:
            self.stores.append(o)
        return o

    def finalize(self):
        fin = Op()
        fin.eng = "sp"; fin.fns = []; fin.deps = list(self.stores); fin.dma = None
        fin.target = False; fin.waits = []; fin.seq = len(self.per_eng["sp"])
        fin.tkey = "sp"; fin.order = fin.seq; fin.val = None
        self.per_eng["sp"].append(fin)
        self.ops.append(fin)
        clock = {e: {} for e in self.ENGS}
        for o in self.ops:
            ck = clock[o.eng]
            best = {}
            for d in o.deps:
                if ck.get(d.tkey, -1) >= d.order:
                    continue
                b = best.get(d.tkey)
                if b is None or b.order < d.order:
                    best[d.tkey] = d
            for d in best.values():
                d.target = True
                for k, v in d.cclock.items():
                    if ck.get(k, -1) < v:
                        ck[k] = v
            o.waits = list(best.values())
            cc = dict(ck)
            cc[o.tkey] = o.order
            o.cclock = cc
            o.deps = None
        for e in self.ENGS:
            c = 0
            for o in self.per_eng[e]:
                if o.dma is None:
                    if o.target:
                        c += 1
                    o.val = c
        for o in self.ops:
            o.cclock = None

    def emit(self):
        nc = self.nc
        with ExitStack() as st:
            esem = {e: st.enter_context(nc.semaphore("s_" + e)) for e in ["pe", "act", "dve", "pool", "sp"]}
            n = 0
            for o in self.ops:
                if o.dma is not None and o.dma.sem is None:
                    o.dma.sem = st.enter_context(nc.semaphore("d%d" % n))
                    n += 1
            block = st.enter_context(nc.Block())

            def run(eng_name):
                def body(eng):
                    for o in self.per_eng[eng_name]:
                        for d in o.waits:
                            sem = d.dma.sem if d.dma is not None else esem[d.eng]
                            eng.wait_ge(sem, d.val)
                        last = None
                        for fn in o.fns:
                            last = fn(eng)
                            if o.dma is not None:
                                last.then_inc(o.dma.sem, 16)
                        if o.dma is None and o.target:
                            last.then_inc(esem[eng_name], 1)
                return body

            block.tensor(run("pe"))
            block.scalar(run("act"))
            block.vector(run("dve"))
            block.gpsimd(run("pool"))
            block.sync(run("sp"))


def MM(out, lhsT, rhs, start=True, stop=True):
    return lambda e: e.matmul(out, lhsT=lhsT, rhs=rhs, start=start, stop=stop)


def TR(out, in_, ident):
    return lambda e: e.transpose(out, in_, ident)


def ACT(out, in_, func, **kw):
    return lambda e: e.activation(out=out, in_=in_, func=func, **kw)


def TT(out, in0, in1, op):
    return lambda e: e.tensor_tensor(out=out, in0=in0, in1=in1, op=op)


def TS(out, in0, s1, s2, op0, op1=None):
    if op1 is None:
        return lambda e: e.tensor_scalar(out=out, in0=in0, scalar1=s1, scalar2=None, op0=op0)
    return lambda e: e.tensor_scalar(out=out, in0=in0, scalar1=s1, scalar2=s2, op0=op0, op1=op1)


def STT(out, in0, scalar, in1, op0, op1):
    return lambda e: e.scalar_tensor_tensor(out=out, in0=in0, scalar=scalar, in1=in1, op0=op0, op1=op1)


def CP(out, in_):
    return lambda e: e.tensor_copy(out=out, in_=in_)


def ACP(out, in_):
    return lambda e: e.copy(out=out, in_=in_)


def DMA(out, in_):
    return lambda e: e.dma_start(out=out, in_=in_)


def IDMA(out, in_, idx):
    return lambda e: e.indirect_dma_start(out=out, out_offset=None, in_=in_,
                                          in_offset=bass.IndirectOffsetOnAxis(ap=idx, axis=0))


def RED(out, in_, axis, op):
    return lambda e: e.tensor_reduce(out=out, in_=in_, axis=axis, op=op)


def RECIP(out, in_):
    return lambda e: e.reciprocal(out=out, in_=in_)


def MAX8(out, in_):
    return lambda e: e.max(out=out, in_=in_)


def MEMSET(ap, v):
    return lambda e: e.memset(ap, v)


def _reshape(v, shape):
    if len(shape) == 1:
        return v
    names = ["a", "b", "c", "d"][:len(shape)]
    kw = {names[k]: shape[k] for k in range(1, len(shape))}
    return v.rearrange("p (%s) -> p %s" % (" ".join(names), " ".join(names)), **kw)


class Arena:
    def __init__(self, ap, nwords):
        self.ap = ap
        self.n = nwords
        self.top = 0
        self.peak = 0

    def mark(self):
        return self.top

    def release(self, m):
        self.top = m

    def _take(self, w):
        w = (w + 7) // 8 * 8
        off = self.top
        self.top += w
        self.peak = max(self.peak, self.top)
        assert self.top <= self.n, "arena overflow %d > %d" % (self.top, self.n)
        return off, w

    def f32(self, *shape):
        n = int(np.prod(shape))
        off, w = self._take(n)
        return _reshape(self.ap[:, off:off + n], shape)

    def i32(self, *shape):
        n = int(np.prod(shape))
        off, w = self._take(n)
        return _reshape(self.ap[:, off:off + n].bitcast(I32), shape)

    def bf16(self, *shape):
        n = int(np.prod(shape))
        off, w = self._take((n + 1) // 2)
        return _reshape(self.ap[:, off:off + w].bitcast(BF16)[:, 0:n], shape)


class StopBuild(Exception):
    pass


class Builder:
    def stage(self, name):
        if self.stop_at == name:
            raise StopBuild()

    def __init__(self, npg, n_layers=4, do_sample_attn=True, stop_at=None):
        self.stop_at = stop_at
        self.npg = npg
        self.n_layers = n_layers
        self.do_sample_attn = do_sample_attn
        self.nc = bass.Bass("TRN2", target_bir_lowering=False)

    def dram(self):
        nc = self.nc
        I = {}
        O = {}

        def inp(name, shape, dt=F32):
            I[name] = nc.dram_tensor(name, list(shape), dt, kind="ExternalInput").ap()

        def out(name, shape):
            O[name] = nc.dram_tensor(name, list(shape), F32, kind="ExternalOutput").ap()

        inp("xp", [2048, 1024]); inp("xs", [64, 1024])
        for nm in ["ck0", "ck1", "cv0", "cv1"]:
            inp(nm, [self.npg * 128, 512])
        inp("spool", [2, 240, 512]); inp("ptab", [1, 256], I32)
        inp("nmix", [128, 4, 8]); inp("nffn", [128, 4, 8]); inp("nfin", [1, 1024]); inp("pscale", [128, 2, 4])
        inp("w_in_even", [2, 1024, 2048]); inp("w_o_even", [2, 1024, 1024]); inp("w_pool", [2, 4, 128, 128])
        inp("w_in_odd", [2, 1024, 2048]); inp("ln_g", [2, 1024]); inp("ln_b", [2, 1024])
        inp("sgu_wT", [2, 8, 128, 128]); inp("sgu_w", [2, 8, 128, 128]); inp("sgu_b", [2, 8, 128]); inp("w_o_odd", [2, 1024, 1024])
        inp("w_up", [4, 1024, 4096]); inp("w_dn", [4, 4096, 1024])
        inp("cosT", [128, 17, 32]); inp("sinT", [128, 17, 32]); inp("tri", [128, 128]); inp("smask", [64, 64])
        inp("cand", [128, 4, 64]); inp("invcnt", [128, 4, 16]); inp("iota", [128, 1]); inp("iota2", [128, 1]); inp("ident", [128, 128])
        inp("ones", [128, 128])
        out("yp", [2048, 1024]); out("ys", [64, 1024])
        out("pk", [2, 2048, 512]); out("pv", [2, 2048, 512]); out("sk", [2, 64, 512]); out("sv", [2, 64, 512])
        out("pps", [2, 15, 512]); out("sps", [2, 16, 15, 512]); out("sgv", [2, 64, 1024])
        self.I, self.O = I, O

    def bank(self):
        b = self._bank
        self._bank = (self._bank + 1) % 8
        return self.ps[b], self.rps[b]

    def load(self, out_ap, in_ap, res, eng="sp"):
        self.P.op(eng, DMA(out_ap, in_ap), writes=[res], dma=res)

    def store(self, out_ap, in_ap, res):
        self.P.op("sp", DMA(out_ap, in_ap), reads=[res], dma=res, store=True)

    def build(self):
        self.dram()
        nc = self.nc
        with ExitStack() as st:
            self.st = st
            AW = 32500
            RES = st.enter_context(nc.sbuf_tensor("RES", [128, 8, NT], F32))
            CW = 3776
            cst = st.enter_context(nc.sbuf_tensor("CST", [128, CW], F32))
            arena_t = st.enter_context(nc.sbuf_tensor("ARENA", [128, AW], F32))
            self.ps = []
            for b in range(8):
                t = st.enter_context(nc.psum_tensor("ps%d" % b, [128, 512], F32))
                self.ps.append(t)
            self.P = Prog(nc)
            P = self.P
            self.rps = [Res("ps%d" % b, excl=True) for b in range(8)]
            self._bank = 0
            self.RES = RES
            self.rRES = [Res("RES%d" % i) for i in range(5)]
            self.A = Arena(arena_t[:, :], AW)
            self.C = Arena(cst[:, :], CW)
            self.consts()
            self.XN = self.A.bf16(8, NT)
            self.rXN = [Res("XN%d" % i) for i in range(5)]
            try:
                self.stage("consts")
                self.load_x()
                self.stage("loadx")
                for l in range(self.n_layers):
                    if l % 2 == 0:
                        self.even_layer(l)
                    else:
                        self.odd_layer(l)
                    self.stage("mixer%d" % l)
                    self.ffn(l)
                    self.stage("ffn%d" % l)
                self.final()
            except StopBuild:
                pass
            P.finalize()
            P.emit()
        return nc

    def consts(self):
        P, C, I = self.P, self.C, self.I
        self.rC = Res("consts")
        rC = self.rC

        def ld(name, shape, src=None):
            t = C.f32(*shape)
            self.load(t, I[name] if src is None else src, rC)
            return t

        def ldb(name, shape, src=None, rows=128):
            t = C.bf16(*shape)
            self.load(t[0:rows], I[name] if src is None else src, rC, eng="pool")
            return t

        self.cosT = ld("cosT", [17, 32]); self.sinT = ld("sinT", [17, 32])
        self.trif = ld("tri", [128]); self.trib = ldb("tri", [128])
        self.smaskb = ldb("smask", [64], rows=64)
        self.cand = ld("cand", [4, 64]); self.invcnt = ld("invcnt", [4, 16]); self.iota = ld("iota", [1])
        self.identf = ld("ident", [128]); self.identb = ldb("ident", [128])
        self.onesb = ldb("ones", [128])
        self.nmix = ld("nmix", [4, 8]); self.nffn = ld("nffn", [4, 8]); self.pscale = ld("pscale", [2, 4])
        self.nfin = ld("nfin", [1024], src=I["nfin"].partition_broadcast(128))
        ptb = C.i32(256)
        self.load(ptb, I["ptab"].partition_broadcast(128), rC)
        io2 = ld("iota2", [1])
        pf0 = C.f32(256)
        pf2 = C.f32(128)
        self.pidx2 = C.i32(128)
        P.op("dve", CP(pf0, ptb), reads=[rC], writes=[rC])
        pv_ = pf0.rearrange("p (m j) -> p m j", j=2)
        P.op("dve", TS(pf2[0:64, :], pv_[0:64, :, 0], 64.0, io2[0:64, 0:1], ALU.mult, ALU.add), reads=[rC], writes=[rC])
        P.op("dve", TS(pf2[64:128, :], pv_[64:128, :, 1], 64.0, io2[64:128, 0:1], ALU.mult, ALU.add), reads=[rC], writes=[rC])
        P.op("dve", CP(self.pidx2, pf2), reads=[rC], writes=[rC])

    def load_x(self):
        P, A = self.P, self.A
        m = A.mark()
        xt = [A.f32(1024) for _ in range(2)]
        rxt = [Res(), Res()]
        for ti, (c0, r) in enumerate(TILES):
            b = ti % 2
            src = self.I["xp"][c0:c0 + r, :] if ti < 16 else self.I["xs"]
            self.load(xt[b][0:r], src, rxt[b])
            for half in range(2):
                ps, rp = self.bank()
                P.op("pe", [TR(ps[:, k * 128:k * 128 + r], xt[b][0:r, (half * 4 + k) * 128:(half * 4 + k + 1) * 128], self.identf[0:r, 0:r])
                            for k in range(4)], reads=[rxt[b], self.rC], writes=[rp])
                src_v = ps[:, :].rearrange("p (k t) -> p k t", k=4)[:, :, 0:r]
                P.op("act" if half == 0 else "dve",
                     (ACP if half == 0 else CP)(self.RES[:, half * 4:half * 4 + 4, c0:c0 + r], src_v),
                     reads=[rp], writes=[self.rRES[min(c0 // 512, 4)]])
        A.release(m)
        P.barrier()

    def rmsnorm_all(self, gains):
        P, A = self.P, self.A
        m = A.mark()
        sq = [A.bf16(512) for _ in range(2)]
        rsq = [Res(), Res()]
        rs = [A.f32(512) for _ in range(2)]
        rrs = [Res(), Res()]
        k = 0
        for tb, (c0, n) in enumerate(TB):
            ps, rp = self.bank()
            for c in range(8):
                b = k % 2
                k += 1
                P.op("act", ACT(sq[b][:, 0:n], self.RES[:, c, c0:c0 + n], AF.Square), reads=[self.rRES[tb]], writes=[rsq[b]])
                P.op("pe", MM(ps[:, 0:n], self.onesb[:, :], sq[b][:, 0:n], start=(c == 0), stop=(c == 7)),
                     reads=[rsq[b], self.rC], writes=[rp])
            b2 = tb % 2
            P.op("act", ACT(rs[b2][:, 0:n], ps[:, 0:n], AF.Sqrt, bias=1e-6, scale=1.0 / 1024), reads=[rp], writes=[rrs[b2]])
            P.op("dve", RECIP(rs[b2][:, 0:n], rs[b2][:, 0:n]), reads=[rrs[b2]], writes=[rrs[b2]])
            for c in range(8):
                P.op("dve",
                     STT(self.XN[:, c, c0:c0 + n], self.RES[:, c, c0:c0 + n], gains[:, c:c + 1], rs[b2][:, 0:n], ALU.mult, ALU.mult),
                     reads=[self.rRES[tb], rrs[b2], self.rC], writes=[self.rXN[tb]])
        A.release(m)
        P.barrier()

    def wload(self, dst, src, res):
        self.P.op("pool", DMA(dst, src), writes=[res], dma=res)

    def even_layer(self, l):
        P, A, I, O = self.P, self.A, self.I, self.O
        i = l // 2
        self.rmsnorm_all(self.nmix[:, l, :])
        self.stage("norm")
        Win = I["w_in_even"][i].rearrange("(c p) f -> p c f", p=128)
        m0 = A.mark()
        QsT = A.bf16(4, 64); KsT = A.bf16(4, 64); Vs = A.bf16(512)
        self.attn_setup()
        kmf = A.f32(4, 8); kmT = A.bf16(4, 8); rkm = Res()
        wp = A.bf16(4, 128)
        rwp = Res()
        mP = A.mark()
        WS = [A.bf16(8, 512)]
        WS.append(WS[0])
        rWS = [Res()]
        rWS.append(rWS[0])
        POOLED = A.bf16(4, NT)
        rPOOLED = [Res() for _ in range(4)]
        m1 = A.mark()
        L = 16 + 2048
        PG = A.f32(L); Y = [A.f32(L), A.f32(L)]
        FULL = A.f32(16, 19); FY = [A.f32(16, 19), A.f32(16, 19)]
        hist = [A.f32(128), A.f32(128)]
        t16 = A.f32(16)
        pst = A.f32(512)
        nrow = A.f32(512)
        newc = A.f32(64); rnewc = Res()
        rPG, rY, rFULL, rFY, rhist, rt16, rpst, rnrow = Res(), [Res(), Res()], Res(), [Res(), Res()], [Res(), Res()], Res(), Res(), Res()
        P.op("pool", [MEMSET(PG[:, 0:16], 0.0), MEMSET(Y[0][:, 0:16], 0.0), MEMSET(Y[1][:, 0:16], 0.0)], writes=[rPG, rY[0], rY[1]])
        self.wload(WS[0], Win[:, :, 1536:2048], rWS[0])
        psS, rpsS = self.ps[7], self.rps[7]
        psN, rpsN = self.ps[6], self.rps[6]
        self._bank = 0
        wins = [2, 4, 8, 16]
        for g in range(4):
            w = wins[g]
            for tb, (c0, n) in enumerate(TB):
                ps, rp = self.ps[tb % 4], self.rps[tb % 4]
                P.op("pe", [MM(ps[:, 0:n], WS[0][:, c, g * 128:(g + 1) * 128], self.XN[:, c, c0:c0 + n], start=(c == 0), stop=(c == 7))
                            for c in range(8)], reads=[rWS[0], self.rXN[tb]], writes=[rp])
                if tb < 4:
                    P.op("act", ACP(PG[:, 16 + c0:16 + c0 + n], ps[:, 0:n]), reads=[rp], writes=[rPG])
                else:
                    P.op("act", ACP(FULL[:, :, 15:19], ps[:, 0:64].rearrange("p (s t) -> p s t", t=4)), reads=[rp], writes=[rFULL])
            for half in range(2):
                self.load(hist[half][0:120], I["spool"][i, half * 120:(half + 1) * 120, g * 128:(g + 1) * 128], rhist[half])
                ps, rp = self.ps[4 + half], self.rps[4 + half]
                P.op("pe", TR(ps[:, 0:120], hist[half][0:120, :], self.identf[0:120, 0:120]), reads=[rhist[half], self.rC], writes=[rp])
                P.op("dve", CP(FULL[:, half * 8:(half + 1) * 8, 0:15], ps[:, 0:120].rearrange("p (s t) -> p s t", t=15)), reads=[rp], writes=[rFULL])
            src, rsrc = PG, rPG
            s = 1
            k = 0
            while s < w:
                dst, rdst = Y[k % 2], rY[k % 2]
                P.op("dve", TT(dst[:, 16:L], src[:, 16:L], src[:, 16 - s:L - s], ALU.add), reads=[rsrc], writes=[rdst])
                src, rsrc = dst, rdst
                s *= 2
                k += 1
            P.op("dve", STT(POOLED[:, g, 0:2048], src[:, 16:L], 1.0 / w, PG[:, 16:L], ALU.mult, ALU.subtract), reads=[rsrc, rPG], writes=[rPOOLED[g]])
            P.op("dve", TT(t16, src[:, 16:32], self.invcnt[:, g, :], ALU.mult), reads=[rsrc, self.rC], writes=[rt16])
            P.op("dve", TT(POOLED[:, g, 0:16], t16, PG[:, 16:32], ALU.subtract), reads=[rt16, rPG], writes=[rPOOLED[g]])
            P.op("pe", TR(psS[0:16, g * 128:(g + 1) * 128], PG[:, L - 16:L], self.identf[:, :]), reads=[rPG, self.rC], writes=[rpsS])
            src, rsrc = FULL, rFULL
            s = 1
            k = 0
            lo = 0
            while s < w:
                dst, rdst = FY[k % 2], rFY[k % 2]
                lo2 = lo + s
                P.op("dve", TT(dst[:, :, lo2:19], src[:, :, lo2:19], src[:, :, lo2 - s:19 - s], ALU.add), reads=[rsrc], writes=[rdst])
                src, rsrc = dst, rdst
                lo = lo2
                s *= 2
                k += 1
            P.op("dve", STT(POOLED[:, g, 2048:2112].rearrange("p (s t) -> p s t", t=4), src[:, :, 15:19], 1.0 / w, FULL[:, :, 15:19],
                            ALU.mult, ALU.subtract), reads=[rsrc, rFULL], writes=[rPOOLED[g]])
            P.op("dve", CP(newc[:, :].rearrange("p (s t) -> p s t", t=4), FULL[:, :, 15:19]), reads=[rFULL], writes=[rnewc])
            P.op("pe", TR(psN[0:64, g * 128:(g + 1) * 128], newc[:, :], self.identf[:, :]), reads=[rnewc, self.rC], writes=[rpsN])
        P.op("act", ACP(pst[0:16, :], psS[0:16, :]), reads=[rpsS], writes=[rpst])
        self.store(O["pps"][i], pst[1:16, :], rpst)
        P.op("act", ACP(nrow[0:64, :], psN[0:64, :]), reads=[rpsN], writes=[rnrow])
        for s_ in range(16):
            self.store(O["sps"][i, s_, 11:15, :], nrow[4 * s_:4 * s_ + 4, :], rnrow)
        rd2d = Res()
        P.op("sp", DMA(O["sps"][i, :, 0:11, :], I["spool"][i].rearrange("(s r) f -> s r f", r=15)[:, 4:15, :]), dma=rd2d, store=True)
        A.release(m1)
        P.barrier()
        self.stage("pool")
        QT = A.bf16(4, 2048); KT = A.bf16(4, 2048); V = A.bf16(16, 512)
        rQT = [Res() for _ in range(17)]; rKT = [Res() for _ in range(17)]; rV = [Res() for _ in range(17)]
        raw = [A.f32(512)] * 2; rot = [A.f32(512)] * 2
        ta = [A.f32(256)] * 2; tb_ = [A.f32(256)] * 2
        xb = [A.bf16(512)] * 2
        rraw, rrot, rta, rtb, rxb = ([Res()] * 2 for _ in range(5))
        self._bank = 0
        for pi, pname in enumerate(["q", "k", "v"]):
            sl = (pi + 1) % 2
            self.wload(WS[sl], Win[:, :, pi * 512:(pi + 1) * 512], rWS[sl])
            for ti, (c0, r) in enumerate(TILES):
                tbi = min(c0 // 512, 4)
                b = ti % 2
                ps, rp = self.bank()
                P.op("pe", [MM(ps[0:r, :], self.XN[:, c, c0:c0 + r], WS[sl][:, c, :], start=(c == 0), stop=(c == 7)) for c in range(8)],
                     reads=[rWS[sl], self.rXN[tbi]], writes=[rp])
                P.op("act", ACP(raw[b][0:r, :], ps[0:r, :]), reads=[rp], writes=[rraw[b]])
                if pname == "v":
                    dst = O["pv"][i, c0:c0 + r, :] if ti < 16 else O["sv"][i]
                    self.store(dst, raw[b][0:r, :], rraw[b])
                    P.op("dve", CP(V[0:r, ti, :] if ti < 16 else Vs[0:64, :], raw[b][0:r, :]), reads=[rraw[b]], writes=[rV[ti]])
                    continue
                r3 = raw[b][0:r, :].rearrange("p (h d) -> p h d", h=8)
                o3 = rot[b][0:r, :].rearrange("p (h d) -> p h d", h=8)
                x1, x2, o1, o2 = r3[:, :, 0:32], r3[:, :, 32:64], o3[:, :, 0:32], o3[:, :, 32:64]
                cb = self.cosT[0:r, ti, :].unsqueeze(1).to_broadcast([r, 8, 32])
                sb_ = self.sinT[0:r, ti, :].unsqueeze(1).to_broadcast([r, 8, 32])
                ta3 = ta[b][0:r, :].rearrange("p (h d) -> p h d", h=8)
                tb3 = tb_[b][0:r, :].rearrange("p (h d) -> p h d", h=8)
                P.op("dve", TT(o1, x1, cb, ALU.mult), reads=[rraw[b], self.rC], writes=[rrot[b]])
                P.op("dve", TT(ta3, x2, sb_, ALU.mult), reads=[rraw[b], self.rC], writes=[rta[b]])
                P.op("dve", TT(o1, o1, ta3, ALU.subtract), reads=[rrot[b], rta[b]], writes=[rrot[b]])
                P.op("dve", TT(o2, x2, cb, ALU.mult), reads=[rraw[b], self.rC], writes=[rrot[b]])
                P.op("dve", TT(tb3, x1, sb_, ALU.mult), reads=[rraw[b], self.rC], writes=[rtb[b]])
                P.op("dve", TT(o2, o2, tb3, ALU.add), reads=[rrot[b], rtb[b]], writes=[rrot[b]])
                if pname == "k":
                    dst = O["pk"][i, c0:c0 + r, :] if ti < 16 else O["sk"][i]
                    self.store(dst, rot[b][0:r, :], rrot[b])
                P.op("act", ACP(xb[b][0:r, :], rot[b][0:r, :]), reads=[rrot[b]], writes=[rxb[b]])
                ps2, rp2 = self.bank()
                psb = ps2[:, 0:256].bitcast(BF16)
                P.op("pe", [TR(psb[:, pr * 128:pr * 128 + r], xb[b][0:r, pr * 128:(pr + 1) * 128], self.identb[0:r, 0:r]) for pr in range(4)],
                     reads=[rxb[b], self.rC], writes=[rp2])
                dstT = (QT if pname == "q" else KT)
                rdT = (rQT if pname == "q" else rKT)
                dst_ap = dstT[:, :, c0:c0 + r] if ti < 16 else (QsT if pname == "q" else KsT)[:, :, 0:64]
                P.op("dve", CP(dst_ap, psb.rearrange("p (k t) -> p k t", k=4)[:, :, 0:r]), reads=[rp2], writes=[rdT[ti]])
        P.barrier()
        self.stage("qkv")
        CAT = self.XN
        rCAT = [Res() for _ in range(17)]
        self.wload(wp, I["w_pool"][i].rearrange("g c d -> c g d"), rwp)
        self._bank = 0
        for g in range(4):
            for tb, (c0, n) in enumerate(TB):
                ps, rp = self.bank()
                P.op("pe", MM(ps[:, 0:n], wp[:, g, :], POOLED[:, g, c0:c0 + n]), reads=[rwp, rPOOLED[g]], writes=[rp])
                P.op("act", ACT(CAT[:, 4 + g, c0:c0 + n], ps[:, 0:n], AF.Copy, scale=self.pscale[:, i, g:g + 1]), reads=[rp, self.rC],
                     writes=[rCAT[t] for t in range(17) if min(TILES[t][0] // 512, 4) == tb])
        self.stage("wpool")
        P.op("dve", RED(kmf, KT[:, :, 0:2048].rearrange("p k (n t) -> p k n t", t=256), AX.X, ALU.add), reads=rKT[0:16], writes=[rkm])
        P.op("dve", TS(kmT, kmf, 1.0 / 256, None, ALU.mult), reads=[rkm], writes=[rkm])
        for qt in range(16):
            nq = qt // 2
            c0 = qt * 128
            sel = nq >= 4
            tiles = []
            for kt in range(qt + 1):
                own = kt >= 2 * nq
                tiles.append(dict(KT=(lambda h, kt=kt: KT[(h % 2) * 64:(h % 2) * 64 + 64, h // 2, kt * 128:(kt + 1) * 128]),
                                  V=(lambda h, kt=kt: V[:, kt, h * 64:(h + 1) * 64]), nk=128,
                                  slot=(kt // 2 if (sel and not own) else (7 if sel else 0)),
                                  mask=(self.trib[:, :] if kt == qt else None), res=[rKT[kt], rV[kt]]))
            self.attend(128, lambda h, c0=c0: QT[(h % 2) * 64:(h % 2) * 64 + 64, h // 2, c0:c0 + 128], [rQT[qt]], tiles,
                        sel, nq, kmT, rkm, self.cand[:, nq - 4, :] if sel else None, CAT, c0, rCAT[qt])
            self.stage("attn%d" % qt)
        self.stage("pattn")
        P.barrier()
        A.release(mP)
        if self.do_sample_attn:
            Kgf = [A.bf16(8192) for _ in range(2)]; rKg = [Res(), Res()]
            Kg = [k_.rearrange("p (j f) -> p j f", j=16) for k_ in Kgf]
            Vgf = [A.bf16(8192) for _ in range(2)]; rVg = [Res(), Res()]
            Vg = [v_.rearrange("p (j f) -> p j f", j=16) for v_ in Vgf]
            KgT = A.bf16(4, 2048); rKgT = Res()
            ck = I["ck%d" % i].rearrange("(r two) f -> r (two f)", two=2); cv = I["cv%d" % i].rearrange("(r two) f -> r (two f)", two=2)

            def gatherK(s):
                if DBG.get("nogather"):
                    return
                for nb in range(8):
                    P.op("pool", IDMA(Kgf[s % 2][:, nb * 1024:(nb + 1) * 1024], ck, self.pidx2[:, s * 8 + nb:s * 8 + nb + 1]), reads=[self.rC], writes=[rKg[s % 2]], dma=rKg[s % 2])
                for nb in range(8):
                    P.op("pool", IDMA(Vgf[s % 2][:, nb * 1024:(nb + 1) * 1024], cv, self.pidx2[:, s * 8 + nb:s * 8 + nb + 1]), reads=[self.rC], writes=[rVg[s % 2]], dma=rVg[s % 2])
            gatherK(0)
            for s in range(16):
                if s + 1 < 16:
                    gatherK(s + 1)
                for pr in range(4):
                    for half in range(2):
                        ps, rp = self.ps[6 + (pr * 2 + half) % 2], self.rps[6 + (pr * 2 + half) % 2]
                        psb = ps[:, :].bitcast(BF16)
                        P.op("pe", [TR(psb[:, k * 128:(k + 1) * 128], Kg[s % 2][:, half * 8 + k, pr * 128:(pr + 1) * 128], self.identb[:, :]) for k in range(8)],
                             reads=[rKg[s % 2], self.rC], writes=[rp])
                        P.op("act" if half == 0 else "dve", (ACP if half == 0 else CP)(KgT[:, pr, half * 1024:(half + 1) * 1024], psb), reads=[rp], writes=[rKgT])
                P.op("dve", RED(kmf, KgT[:, :, :].rearrange("p k (n t) -> p k n t", t=256), AX.X, ALU.add), reads=[rKgT], writes=[rkm])
                P.op("dve", TS(kmT, kmf, 1.0 / 256, None, ALU.mult), reads=[rkm], writes=[rkm])
                tiles = []
                for kt in range(16):
                    tiles.append(dict(KT=(lambda h, kt=kt: KgT[(h % 2) * 64:(h % 2) * 64 + 64, h // 2, kt * 128:(kt + 1) * 128]),
                                      V=(lambda h, kt=kt, s=s: Vg[s % 2][:, kt, h * 64:(h + 1) * 64]), nk=128, slot=kt // 2, mask=None, res=[rKgT, rVg[s % 2]]))
                tiles.append(dict(KT=(lambda h: KsT[(h % 2) * 64:(h % 2) * 64 + 64, h // 2, 0:64]),
                                  V=(lambda h: Vs[0:64, h * 64:(h + 1) * 64]), nk=64, slot=8,
                                  mask=self.smaskb[0:64, 4 * s:4 * s + 4], res=[rKT[16], rV[16]]))
                c0 = 2048 + 4 * s
                if DBG.get("noattn"):
                    continue
                self.attend(4, lambda h, s=s: QsT[(h % 2) * 64:(h % 2) * 64 + 64, h // 2, 4 * s:4 * s + 4], [rQT[16]], tiles,
                            True, 8, kmT, rkm, None, CAT, c0, rCAT[16])
        else:
            P.op("pool", MEMSET(CAT[:, 0:4, 2048:2112], 0.0), writes=[rCAT[16]])
        A.release(m0)
        P.barrier()
        WS = [A.bf16(8, 512) for _ in range(2)]
        rWS = [Res(), Res()]
        Wo = I["w_o_even"][i].rearrange("(c p) f -> p c f", p=128)
        self.out_proj(Wo, WS, rWS, CAT, rCAT)
        A.release(m0)
        P.barrier()

    def out_proj(self, Wo, WS, rWS, SRC, rSRC17):
        P = self.P
        self._bank = 0
        self.wload(WS[0], Wo[:, :, 0:512], rWS[0])
        self.wload(WS[1], Wo[:, :, 512:1024], rWS[1])
        for ph in range(2):
            for tb, (c0, n) in enumerate(TB):
                rs = [rSRC17[t] for t in range(17) if min(TILES[t][0] // 512, 4) == tb]
                for dcl in range(4):
                    dc = ph * 4 + dcl
                    ps, rp = self.bank()
                    P.op("pe", [MM(ps[:, 0:n], WS[ph][:, k, dcl * 128:(dcl + 1) * 128], SRC[:, k, c0:c0 + n], start=(k == 0), stop=(k == 7)) for k in range(8)],
                         reads=[rWS[ph]] + rs, writes=[rp])
                    P.op("dve", TT(self.RES[:, dc, c0:c0 + n], self.RES[:, dc, c0:c0 + n], ps[:, 0:n], ALU.add), reads=[rp, self.rRES[tb]], writes=[self.rRES[tb]])

    def attn_setup(self):
        A = self.A
        self.PT = [A.bf16(512) for _ in range(2)]; self.rPT = [Res(), Res()]
        self.acc = A.f32(8, 65); self.racc = Res()
        self.gs = A.f32(64); self.mx = A.f32(8, 8); self.sel = A.f32(8, 9); self.rsel = Res()
        self.rec = A.f32(8); self.obf = A.bf16(512); self.robf = Res()
        self.ctmp = [A.f32(8, 65) for _ in range(2)]; self.cred = [A.f32(65) for _ in range(2)]; self.rct = [Res(), Res()]
        self._pt = 0
        self._ob = 0

    def attend(self, M, QTf, rQ, tiles, sel, nq, kmT, rkm, cand, CAT, c0, rcat):
        P = self.P
        S = [(self.ps[0], self.rps[0]), (self.ps[1], self.rps[1])]
        Ob = [(self.ps[2], self.rps[2]), (self.ps[3], self.rps[3])]
        Gb, rGb = self.ps[6], self.rps[6]
        acc, racc = self.acc, self.racc
        if sel and not DBG.get("nogate"):
            Gb2, rGb2 = self.ps[7], self.rps[7]
            for par, (gb_, rgb_) in enumerate([(Gb, rGb), (Gb2, rGb2)]):
                P.op("pe", [MM(gb_[0:M, (h // 2) * 8:(h // 2) * 8 + 8], QTf(h), kmT[(h % 2) * 64:(h % 2) * 64 + 64, h // 2, :])
                            for h in range(par, 8, 2)], reads=rQ + [rkm], writes=[rgb_])
            gs4 = self.gs[0:M, :].rearrange("p (k q n) -> p k q n", k=4, q=2)
            for par, (gb_, rgb_) in enumerate([(Gb, rGb), (Gb2, rGb2)]):
                src = gb_[0:M, 0:32].rearrange("p (k n) -> p k n", k=4)
                if cand is not None:
                    P.op("dve", TT(gs4[:, :, par, :], src, cand[0:M, :].rearrange("p (k q n) -> p k q n", k=4, q=2)[:, :, par, :], ALU.add),
                         reads=[rgb_, self.rC], writes=[self.rsel])
                else:
                    P.op("dve", CP(gs4[:, :, par, :], src), reads=[rgb_], writes=[self.rsel])
            for h in range(8):
                P.op("dve", MAX8(self.mx[0:M, h, :], self.gs[0:M, h * 8:(h + 1) * 8]), reads=[self.rsel], writes=[self.rsel])
            P.op("dve", TT(self.sel[0:M, :, 0:8], self.gs[0:M, :].rearrange("p (h n) -> p h n", h=8),
                           self.mx[0:M, :, 2:3].to_broadcast([M, 8, 8]), ALU.is_ge), reads=[self.rsel], writes=[self.rsel])
        G = 512 // M if M >= 128 else 16
        groups = [tiles[k:k + G] for k in range(0, len(tiles), G)]
        g2 = []
        for grp in groups:
            a = [t for t in grp if t["nk"] == 128]
            b = [t for t in grp if t["nk"] != 128]
            if a:
                g2.append(a)
            if b:
                g2.append(b)
        groups = g2
        first_in_slot = {}
        last_in_slot = {}
        for ti, t in enumerate(tiles):
            first_in_slot.setdefault(t["slot"], ti)
            last_in_slot[t["slot"]] = ti
        for h in range(8):
            ob, rob = Ob[self._ob % 2]
            Db, rDb = self.ps[4 + self._ob % 2], self.rps[4 + self._ob % 2]
            dofs = 0
            self._ob += 1
            ti = 0
            for grp in groups:
                sb_, rsb = S[self._pt % 2]
                pt, rpt = self.PT[self._pt % 2], self.rPT[self._pt % 2]
                self._pt += 1
                nk = grp[0]["nk"]
                rr = []
                for t in grp:
                    rr += t["res"]
                P.op("pe", [MM(sb_[0:nk, k * M:(k + 1) * M], t["KT"](h), QTf(h)) for k, t in enumerate(grp)], reads=rQ + rr, writes=[rsb])
                W = len(grp) * M
                P.op("act", ACT(pt[0:nk, 0:W], sb_[0:nk, 0:W], AF.Exp, scale=0.125), reads=[rsb], writes=[rpt])
                for k, t in enumerate(grp):
                    if t["mask"] is not None:
                        P.op("dve", TT(pt[0:nk, k * M:(k + 1) * M], pt[0:nk, k * M:(k + 1) * M], t["mask"], ALU.mult), reads=[rpt, self.rC], writes=[rpt])
                mms = []
                for k, t in enumerate(grp):
                    sl = t["slot"]
                    st_, sp_ = (first_in_slot[sl] == ti), (last_in_slot[sl] == ti)
                    mms.append(MM(ob[0:M, (sl % 8) * 64:(sl % 8) * 64 + 64] if sl < 8 else Db[0:M, 64 + dofs * 4:64 + dofs * 4 + 64],
                                  pt[0:nk, k * M:(k + 1) * M], t["V"](h), start=st_, stop=sp_))
                    mms.append(MM(Db[0:M, dofs + sl:dofs + sl + 1], pt[0:nk, k * M:(k + 1) * M], self.onesb[0:nk, 0:1], start=st_, stop=sp_))
                    ti += 1
                P.op("pe", mms, reads=[rpt, self.rC] + rr, writes=[rob, rDb])
            own = 8 if (sel and nq == 8) else (7 if sel else 0)
            own_ap = ob[0:M, own * 64:own * 64 + 64] if own < 8 else Db[0:M, 64 + dofs * 4:64 + dofs * 4 + 64]
            P.op("act", ACP(acc[0:M, h, 0:64], own_ap), reads=[rob, rDb], writes=[racc])
            P.op("act", ACP(acc[0:M, h, 64:65], Db[0:M, dofs + own:dofs + own + 1]), reads=[rDb], writes=[racc])
            if sel and not DBG.get("nocombo"):
                ct, cr, rct = self.ctmp[h % 2], self.cred[h % 2], self.rct[h % 2]
                P.op("dve", TT(ct[0:M, 0:nq, 0:64], ob[0:M, 0:nq * 64].rearrange("p (n d) -> p n d", d=64),
                               self.sel[0:M, h, 0:nq].unsqueeze(2).to_broadcast([M, nq, 64]), ALU.mult), reads=[rob, self.rsel], writes=[rct])
                P.op("dve", TT(ct[0:M, 0:nq, 64], Db[0:M, 0:nq], self.sel[0:M, h, 0:nq], ALU.mult), reads=[rDb, self.rsel], writes=[rct])
                P.op("dve", RED(cr[0:M, :], ct[0:M, 0:nq, :].rearrange("p n d -> p d n"), AX.X, ALU.add), reads=[rct], writes=[rct])
                P.op("dve", TT(acc[0:M, h, :], acc[0:M, h, :], cr[0:M, :], ALU.add), reads=[rct, racc], writes=[racc])
        P.op("dve", RECIP(self.rec[0:M, :], acc[0:M, :, 64]), reads=[racc], writes=[self.robf])
        P.op("dve", TT(self.obf[0:M, :].rearrange("p (h d) -> p h d", h=8), acc[0:M, :, 0:64],
                       self.rec[0:M, :].unsqueeze(2).to_broadcast([M, 8, 64]), ALU.mult), reads=[racc, self.robf], writes=[self.robf])
        ps, rp = self.ps[7], self.rps[7]
        psb = ps[:, 0:256].bitcast(BF16)
        P.op("pe", [TR(psb[:, pr * 128:pr * 128 + M], self.obf[0:M, pr * 128:(pr + 1) * 128], self.identb[0:M, 0:M]) for pr in range(4)],
             reads=[self.robf, self.rC], writes=[rp])
        P.op("act", ACP(CAT[:, 0:4, c0:c0 + M], psb.rearrange("p (k t) -> p k t", k=4)[:, :, 0:M]), reads=[rp], writes=[rcat])

    def odd_layer(self, l):
        P, A, I, O = self.P, self.A, self.I, self.O
        j = l // 2
        self.rmsnorm_all(self.nmix[:, l, :])
        Win = I["w_in_odd"][j].rearrange("(c p) f -> p c f", p=128)
        m0 = A.mark()
        WS = [A.bf16(8, 512) for _ in range(2)]; rWS = [Res(), Res()]
        U = A.bf16(8, NT); rU = [Res() for _ in range(17)]
        lng = A.f32(1024); lnb = A.f32(1024); rln = Res()
        self.load(lng, I["ln_g"][j:j + 1, :].partition_broadcast(128), rln)
        self.load(lnb, I["ln_b"][j:j + 1, :].partition_broadcast(128), rln)
        wtf = A.f32(8, 128); WTm = A.bf16(8, 128); rwt = Res()
        self.load(wtf, I["sgu_wT"][j].rearrange("g s t -> s g t"), rwt)
        P.op("dve", TT(WTm, wtf, self.trif.unsqueeze(1).to_broadcast([128, 8, 128]), ALU.mult), reads=[rwt, self.rC], writes=[rwt])
        sbb = A.bf16(8, 128); rsbb = Res()
        self.wload(sbb[0:1], I["sgu_b"][j:j + 1].rearrange("o g t -> o g t"), rsbb)
        wsm = A.f32(8, 4, 4); bsm = A.f32(8, 4); rsm = Res()
        for g_ in range(8):
            self.load(wsm[:, g_, :, :], I["sgu_w"][j][g_, 0:4, 0:4].partition_broadcast(128), rsm)
        self.load(bsm, I["sgu_b"][j][:, 0:4].partition_broadcast(128), rsm)
        self._bank = 0
        for ph in range(2):
            self.wload(WS[ph], Win[:, :, ph * 512:(ph + 1) * 512], rWS[ph])
        for ph in range(2):
            for tb, (c0, n) in enumerate(TB):
                for fl in range(4):
                    fc = ph * 4 + fl
                    ps, rp = self.bank()
                    P.op("pe", [MM(ps[:, 0:n], WS[ph][:, c, fl * 128:(fl + 1) * 128], self.XN[:, c, c0:c0 + n], start=(c == 0), stop=(c == 7)) for c in range(8)],
                         reads=[rWS[ph], self.rXN[tb]], writes=[rp])
                    P.op("act", ACT(U[:, fc, c0:c0 + n], ps[:, 0:n], AF.Gelu_apprx_tanh), reads=[rp],
                         writes=[rU[t] for t in range(17) if min(TILES[t][0] // 512, 4) == tb])
        for ph in range(2):
            self.wload(WS[ph], Win[:, :, 1024 + ph * 512:1024 + (ph + 1) * 512], rWS[ph])
        vg = [A.f32(1024) for _ in range(2)]; rvg = [Res(), Res()]
        vln = [A.f32(1024) for _ in range(2)]; rvln = [Res(), Res()]
        vlb = [A.bf16(1024) for _ in range(2)]; rvlb = [Res(), Res()]
        st1 = [A.f32(4) for _ in range(2)]; rst = [Res(), Res()]
        vT = A.f32(8, 64); mixT = A.f32(8, 64); rvT = Res(); rmix = Res()
        for ti, (c0, r) in enumerate(TILES):
            b = ti % 2
            tbi = min(c0 // 512, 4)
            for ph in range(2):
                ps, rp = self.bank()
                P.op("pe", [MM(ps[0:r, :], self.XN[:, c, c0:c0 + r], WS[ph][:, c, :], start=(c == 0), stop=(c == 7)) for c in range(8)],
                     reads=[rWS[ph], self.rXN[tbi]], writes=[rp])
                P.op("act", ACT(vg[b][0:r, ph * 512:(ph + 1) * 512], ps[0:r, :], AF.Gelu_apprx_tanh), reads=[rp], writes=[rvg[b]])
            s1 = st1[b]
            P.op("dve", RED(s1[0:r, 0:1], vg[b][0:r, :], AX.X, ALU.add), reads=[rvg[b]], writes=[rst[b]])
            P.op("dve", TS(s1[0:r, 1:2], s1[0:r, 0:1], 1.0 / 1024, None, ALU.mult), reads=[rst[b]], writes=[rst[b]])
            P.op("dve", TS(vg[b][0:r, :], vg[b][0:r, :], s1[0:r, 1:2], None, ALU.subtract), reads=[rst[b], rvg[b]], writes=[rvg[b]])
            P.op("act", ACT(vln[b][0:r, :], vg[b][0:r, :], AF.Square, accum_out=s1[0:r, 2:3]), reads=[rvg[b], rst[b]], writes=[rvln[b], rst[b]])
            P.op("act", ACT(s1[0:r, 3:4], s1[0:r, 2:3], AF.Sqrt, bias=1e-5, scale=1.0 / 1024), reads=[rst[b]], writes=[rst[b]])
            P.op("dve", RECIP(s1[0:r, 3:4], s1[0:r, 3:4]), reads=[rst[b]], writes=[rst[b]])
            P.op("dve", STT(vln[b][0:r, :], vg[b][0:r, :], s1[0:r, 3:4], lng[0:r, :], ALU.mult, ALU.mult), reads=[rvg[b], rst[b], rln], writes=[rvln[b]])
            P.op("pool", TT(vln[b][0:r, :], vln[b][0:r, :], lnb[0:r, :], ALU.add), reads=[rvln[b], rln], writes=[rvln[b]])
            if ti == 16:
                self.store(O["sgv"][j], vln[b][0:64, :], rvln[b])
                for half in range(2):
                    ps, rp = self.bank()
                    P.op("pe", [TR(ps[:, k * 64:(k + 1) * 64], vln[b][0:64, (half * 4 + k) * 128:(half * 4 + k + 1) * 128], self.identf[0:64, 0:64]) for k in range(4)],
                         reads=[rvln[b], self.rC], writes=[rp])
                    P.op("act", ACP(vT[:, half * 4:half * 4 + 4, :], ps[:, 0:256].rearrange("p (k t) -> p k t", k=4)), reads=[rp], writes=[rvT])
                v4 = vT[:, :, :].rearrange("p g (s t) -> p g s t", t=4)
                m4 = mixT[:, :, :].rearrange("p g (s t) -> p g s t", t=4)
                for g in range(8):
                    for t in range(4):
                        P.op("dve", TS(m4[:, g, :, t], v4[:, g, :, 0], wsm[:, g, t, 0:1], bsm[:, g, t:t + 1], ALU.mult, ALU.add), reads=[rvT, rsm], writes=[rmix])
                        for s_ in range(1, t + 1):
                            P.op("dve", STT(m4[:, g, :, t], v4[:, g, :, s_], wsm[:, g, t, s_:s_ + 1], m4[:, g, :, t], ALU.mult, ALU.add), reads=[rvT, rsm, rmix], writes=[rmix])
                P.op("dve", TT(U[:, :, 2048:2112], U[:, :, 2048:2112], mixT[:, :, :], ALU.mult), reads=[rmix, rU[16]], writes=[rU[16]])
                continue
            P.op("act", ACP(vlb[b][:, :], vln[b][:, :]), reads=[rvln[b]], writes=[rvlb[b]])
            for half in range(2):
                ps, rp = self.bank()
                mms = []
                for k in range(4):
                    g = half * 4 + k
                    mms.append(MM(ps[:, k * 128:(k + 1) * 128], vlb[b][:, g * 128:(g + 1) * 128], WTm[:, g, :], start=True, stop=False))
                    mms.append(MM(ps[:, k * 128:(k + 1) * 128], self.onesb[0:1, :], sbb[0:1, g, :], start=False, stop=True))
                P.op("pe", mms, reads=[rvlb[b], rwt, rsbb, self.rC], writes=[rp])
                P.op("dve", TT(U[:, half * 4:half * 4 + 4, c0:c0 + 128], U[:, half * 4:half * 4 + 4, c0:c0 + 128],
                               ps[:, :].rearrange("p (k t) -> p k t", k=4), ALU.mult), reads=[rp, rU[ti]], writes=[rU[ti]])
        Wo = I["w_o_odd"][j].rearrange("(c p) f -> p c f", p=128)
        self.out_proj(Wo, WS, rWS, U, rU)
        A.release(m0)
        P.barrier()

    def ffn(self, l):
        P, A, I = self.P, self.A, self.I
        self.rmsnorm_all(self.nffn[:, l, :])
        m0 = A.mark()
        WU = [A.bf16(8, 512) for _ in range(2)]; WD = [A.bf16(4, 1024) for _ in range(2)]
        rW = [Res(), Res()]
        H = [A.bf16(4, 512) for _ in range(2)]; rH = [Res(), Res()]
        tmp = [A.f32(512) for _ in range(2)]; rtmp = [Res(), Res()]
        Wup = I["w_up"][l].rearrange("(c p) f -> p c f", p=128)
        Wdn = I["w_dn"][l].rearrange("(c p) f -> p c f", p=128)

        def wl(fg):
            s = fg % 2
            P.op("pool", [DMA(WU[s], Wup[:, :, fg * 512:(fg + 1) * 512]), DMA(WD[s], Wdn[:, fg * 4:(fg + 1) * 4, :])], writes=[rW[s]], dma=rW[s])
        self._bank = 0
        st = dict(kk=0)

        def up(k, fg, tb):
            s = fg % 2
            c0, n = TB[tb]
            hb = k % 2
            for fc in range(4):
                ps, rp = self.bank()
                P.op("pe", [MM(ps[:, 0:n], WU[s][:, c, fc * 128:(fc + 1) * 128], self.XN[:, c, c0:c0 + n], start=(c == 0), stop=(c == 7)) for c in range(8)],
                     reads=[rW[s], self.rXN[tb]], writes=[rp])
                tb_ = st["kk"] % 2
                st["kk"] += 1
                P.op("act", ACT(tmp[tb_][:, 0:n], ps[:, 0:n], AF.Relu), reads=[rp], writes=[rtmp[tb_]])
                P.op("pool", TT(H[hb][:, fc, 0:n], tmp[tb_][:, 0:n], tmp[tb_][:, 0:n], ALU.mult), reads=[rtmp[tb_]], writes=[rH[hb]])

        def down(k, fg, tb):
            s = fg % 2
            c0, n = TB[tb]
            hb = k % 2
            for dc in range(8):
                ps, rp = self.bank()
                P.op("pe", [MM(ps[:, 0:n], WD[s][:, fc, dc * 128:(dc + 1) * 128], H[hb][:, fc, 0:n], start=(fc == 0), stop=(fc == 3)) for fc in range(4)],
                     reads=[rW[s], rH[hb]], writes=[rp])
                P.op("dve", TT(self.RES[:, dc, c0:c0 + n], self.RES[:, dc, c0:c0 + n], ps[:, 0:n], ALU.add), reads=[rp, self.rRES[tb]], writes=[self.rRES[tb]])

        pairs = [(fg, tb) for fg in range(8) for tb in range(5)]
        wl(0)
        wl(1)
        up(0, *pairs[0])
        for k, (fg, tb) in enumerate(pairs):
            if k + 1 < len(pairs):
                up(k + 1, *pairs[k + 1])
            down(k, fg, tb)
            if tb == 4 and fg + 2 < 8:
                wl(fg + 2)
        A.release(m0)
        P.barrier()

    def final(self):
        P, A, O = self.P, self.A, self.O
        m0 = A.mark()
        yt = [A.f32(1024) for _ in range(2)]; ryt = [Res(), Res()]
        yo = [A.f32(1024) for _ in range(2)]; ryo = [Res(), Res()]
        junk = A.f32(1024); rjunk = Res()
        st_ = [A.f32(2) for _ in range(2)]; rst = [Res(), Res()]
        self._bank = 0
        for ti, (c0, r) in enumerate(TILES):
            b = ti % 2
            tbi = min(c0 // 512, 4)
            for half in range(2):
                ps, rp = self.bank()
                P.op("pe", [TR(ps[0:r, k * 128:(k + 1) * 128], self.RES[:, half * 4 + k, c0:c0 + r], self.identf[:, :]) for k in range(4)],
                     reads=[self.rRES[tbi], self.rC], writes=[rp])
                P.op("act" if half == 0 else "dve", (ACP if half == 0 else CP)(yt[b][0:r, half * 512:(half + 1) * 512], ps[0:r, :]), reads=[rp], writes=[ryt[b]])
            P.op("act", ACT(junk[0:r, :], yt[b][0:r, :], AF.Square, accum_out=st_[b][0:r, 0:1]), reads=[ryt[b]], writes=[rjunk, rst[b]])
            P.op("act", ACT(st_[b][0:r, 1:2], st_[b][0:r, 0:1], AF.Sqrt, bias=1e-6, scale=1.0 / 1024), reads=[rst[b]], writes=[rst[b]])
            P.op("dve", RECIP(st_[b][0:r, 1:2], st_[b][0:r, 1:2]), reads=[rst[b]], writes=[rst[b]])
            P.op("dve", STT(yo[b][0:r, :], yt[b][0:r, :], st_[b][0:r, 1:2], self.nfin[0:r, :], ALU.mult, ALU.mult), reads=[ryt[b], rst[b], self.rC], writes=[ryo[b]])
            dst = O["yp"][c0:c0 + r, :] if ti < 16 else O["ys"]
            self.store(dst, yo[b][0:r, :], ryo[b])
        A.release(m0)


def _host_consts():
    pos = np.zeros((128, 17), np.float32)
    for i in range(16):
        pos[:, i] = i * 128 + np.arange(128)
    pos[:, 16] = 2048 + (np.arange(128) % 4)
    inv = (10000.0 ** (-np.arange(0, 64, 2, dtype=np.float32) / 64)).astype(np.float32)
    ang = pos[:, :, None].astype(np.float32) * inv[None, None, :]
    cosT = np.cos(ang).astype(np.float32)
    sinT = np.sin(ang).astype(np.float32)
    k = np.arange(128)
    tri = (k[:, None] <= k[None, :]).astype(np.float32)
    kk = np.arange(64)
    smask = np.zeros((64, 64), np.float32)
    for s in range(16):
        for q in range(4):
            smask[:, 4 * s + q] = ((kk // 4 == s) & (kk % 4 <= q)).astype(np.float32)
    cand = np.zeros((128, 4, 64), np.float32)
    for a, nq in enumerate([4, 5, 6, 7]):
        c = np.where(np.arange(8) < nq, 0.0, -1e30).astype(np.float32)
        cand[:, a, :] = np.tile(c, 8)[None, :]
    invcnt = np.zeros((128, 4, 16), np.float32)
    for g, w in enumerate([2, 4, 8, 16]):
        invcnt[:, g, :] = (1.0 / np.minimum(np.arange(16) + 1, w)).astype(np.float32)[None, :]
    return dict(cosT=cosT, sinT=sinT, tri=tri, smask=smask, cand=cand, invcnt=invcnt,
                iota=np.arange(128, dtype=np.float32)[:, None], iota2=(np.arange(128) % 64).astype(np.float32)[:, None], ident=np.eye(128, dtype=np.float32),
                ones=np.ones((128, 128), np.float32))


def _core_inputs(c, inp, ck, cv, ptab_c, consts):
    f = np.ascontiguousarray
    d = dict(consts)
    d["xp"] = f(inp["x_prompt"][c])
    d["xs"] = f(inp["x_sample"][16 * c:16 * c + 16].reshape(64, 1024))
    d["ck0"], d["ck1"], d["cv0"], d["cv1"] = ck[0], ck[1], cv[0], cv[1]
    d["spool"] = f(inp["state_pool"][:, 16 * c:16 * c + 16].reshape(2, 240, 512))
    d["ptab"] = f(ptab_c.reshape(1, 256).astype(np.int32))
    d["nmix"] = f(inp["norm_mix"].reshape(4, 8, 128).transpose(2, 0, 1))
    d["nffn"] = f(inp["norm_ffn"].reshape(4, 8, 128).transpose(2, 0, 1))
    d["nfin"] = f(inp["norm_final"].reshape(1, 1024))
    d["pscale"] = f(inp["pool_scale"].reshape(2, 4, 128).transpose(2, 0, 1))
    for a, b in [("w_in_even", "w_in_even"), ("w_o_even", "w_o_even"), ("w_pool", "w_pool"), ("w_in_odd", "w_in_odd"),
                 ("ln_g", "sgu_ln_g"), ("ln_b", "sgu_ln_b"), ("sgu_w", "sgu_w"), ("sgu_b", "sgu_b"), ("w_o_odd", "w_o_odd"),
                 ("w_up", "w_ffn_up"), ("w_dn", "w_ffn_down")]:
        d[a] = inp[b]
    d["sgu_wT"] = f(inp["sgu_w"].transpose(0, 1, 3, 2))
    return d


_NC_CACHE = {}


def _get_nc(npg, **kw):
    key = (npg, tuple(sorted(kw.items())))
    if key not in _NC_CACHE:
        _NC_CACHE[key] = Builder(npg, **kw).build()
    return _NC_CACHE[key]


def _assemble(results, cores):
    n = len(cores)
    yp = np.stack([r["yp"] for r in results])
    ys = np.concatenate([r["ys"].reshape(16, 4, 1024) for r in results])
    pk = np.stack([r["pk"].reshape(2, 2048, 8, 64) for r in results], axis=1)
    pv = np.stack([r["pv"].reshape(2, 2048, 8, 64) for r in results], axis=1)
    sk = np.concatenate([r["sk"].reshape(2, 16, 4, 8, 64) for r in results], axis=1)
    sv = np.concatenate([r["sv"].reshape(2, 16, 4, 8, 64) for r in results], axis=1)
    pps = np.stack([r["pps"] for r in results], axis=1)
    sps = np.concatenate([r["sps"] for r in results], axis=1)
    sgv = np.concatenate([r["sgv"].reshape(2, 16, 4, 1024) for r in results], axis=1)
    return (yp, ys, pk, pv, sk, sv, pps, sps, sgv)


def kernel(**inp):
    inp = {k: np.asarray(v) for k, v in inp.items()}
    consts = _host_consts()
    npg = inp["cache_k"].shape[1]
    ck = [np.ascontiguousarray(inp["cache_k"][i].reshape(npg * 128, 512)) for i in range(2)]
    cv = [np.ascontiguousarray(inp["cache_v"][i].reshape(npg * 128, 512)) for i in range(2)]
    nc = _get_nc(npg)
    in_maps = [_core_inputs(c, inp, ck, cv, inp["page_table"][16 * c:16 * c + 16], consts) for c in range(8)]
    res = run_bass_kernel_spmd(nc, in_maps, core_ids=list(range(8)))
    return _assemble(res.results, list(range(8)))
```
